# Optimizing a Trainium2 kernel written in Bass

```python
import math
import jax
import jax.numpy as jnp
from jax import lax
import numpy as np

D_MODEL = 1024
BATCH = 8
SEQ = 2048
DEPTH = 1
DEC_BATCH = 128
DEC_SEQ = 8
PAST_LEN = 16384
PAGE_SIZE = 128

N_HEADS_A = 4
HEAD_K = 128
HEAD_V = 128
QK_W = N_HEADS_A * HEAD_K
WIDTH_A = N_HEADS_A * HEAD_V
QKV_W = 2 * QK_W + WIDTH_A
CONV_W = 4
CHUNK = 64
POOL_WINDOWS = (2, 4, 8, 16)
N_POOL_GROUPS = 4
POOL_GROUP = 128
WIDTH_B = N_POOL_GROUPS * POOL_GROUP
POOL_HIST = 15
D_FF = 4 * D_MODEL
EPS = 1e-6
IN_SPLITS = (QK_W, QK_W, WIDTH_A, WIDTH_A, N_HEADS_A, N_HEADS_A, WIDTH_B, D_MODEL, D_MODEL)
IN_W = 2 * QK_W + 2 * WIDTH_A + 2 * N_HEADS_A + WIDTH_B + 2 * D_MODEL

kernel_name = 'hybrid_gdn_pool_decoder_step'


def rmsnorm(x, g):
    xf = x.astype(jnp.float32)
    return xf * lax.rsqrt(jnp.mean(xf * xf, axis=-1, keepdims=True) + EPS) * g.astype(jnp.float32)


def l2norm(x):
    return x * lax.rsqrt(jnp.sum(x * x, axis=-1, keepdims=True) + EPS)


def gated_delta_rule(q, k, v, g, beta, s0):
    b, h, l, dk = q.shape
    dv = v.shape[-1]
    c = CHUNK if l % CHUNK == 0 else l
    n = l // c
    rs = lambda t: t.reshape((b, h, n, c) + t.shape[3:])
    q, k, v, g, beta = rs(q), rs(k), rs(v), rs(g), rs(beta)
    g = jnp.cumsum(g, axis=-1)
    idx = jnp.arange(c)
    causal = idx[:, None] >= idx[None, :]
    strict = idx[:, None] > idx[None, :]
    decay = jnp.exp(jnp.where(causal, g[..., :, None] - g[..., None, :], -jnp.inf))
    kk = jnp.einsum('bhnid,bhnjd->bhnij', k, k)
    m = jnp.where(strict, kk * decay * beta[..., :, None], 0.0)
    a = m + jnp.eye(c, dtype=jnp.float32)
    eg = jnp.exp(g)
    rhs = jnp.concatenate([v * beta[..., None], k * (beta * eg)[..., None]], axis=-1)
    sol = lax.linalg.triangular_solve(a, rhs, left_side=True, lower=True, unit_diagonal=True)
    u, w = sol[..., :dv], sol[..., dv:]
    qk = jnp.where(causal, jnp.einsum('bhnid,bhnjd->bhnij', q, k) * decay, 0.0)
    q_dec = q * eg[..., None]
    k_dec = k * jnp.exp(g[..., -1:] - g)[..., None]
    last = jnp.exp(g[..., -1])

    def step(s, inp):
        u_c, w_c, qk_c, qd_c, kd_c, last_c = inp
        v_new = u_c - jnp.einsum('bhcd,bhde->bhce', w_c, s)
        o = jnp.einsum('bhcd,bhde->bhce', qd_c, s) + jnp.einsum('bhij,bhje->bhie', qk_c, v_new)
        s = s * last_c[..., None, None] + jnp.einsum('bhcd,bhce->bhde', kd_c, v_new)
        return s, o

    xs = (jnp.moveaxis(u, 2, 0), jnp.moveaxis(w, 2, 0), jnp.moveaxis(qk, 2, 0),
          jnp.moveaxis(q_dec, 2, 0), jnp.moveaxis(k_dec, 2, 0), jnp.moveaxis(last, 2, 0))
    s, o = lax.scan(step, s0, xs)
    o = jnp.moveaxis(o, 0, 2).reshape(b, h, l, dv)
    return o, s


def gated_delta_branch(q, k, v, z, b_raw, a_raw, conv_prev, s0, w_conv, a_log, dt_bias, w_onorm):
    bsz, l, _ = q.shape
    qkv = jnp.concatenate([q, k, v], axis=-1)
    full = jnp.concatenate([conv_prev.astype(jnp.float32), qkv], axis=1)
    conv = lax.conv_general_dilated(full, w_conv.astype(jnp.float32)[:, None, :], (1,), 'VALID',
                                    dimension_numbers=('NWC', 'WIO', 'NWC'),
                                    feature_group_count=QKV_W)
    conv = jax.nn.silu(conv)
    qc, kc, vc = conv[..., :QK_W], conv[..., QK_W:2 * QK_W], conv[..., 2 * QK_W:]
    qh = jnp.swapaxes(l2norm(qc.reshape(bsz, l, N_HEADS_A, HEAD_K)) * (HEAD_K ** -0.5), 1, 2)
    kh = jnp.swapaxes(l2norm(kc.reshape(bsz, l, N_HEADS_A, HEAD_K)), 1, 2)
    vh = jnp.swapaxes(vc.reshape(bsz, l, N_HEADS_A, HEAD_V), 1, 2)
    beta = jnp.swapaxes(jax.nn.sigmoid(b_raw), 1, 2)
    g = jnp.swapaxes(-jnp.exp(a_log.astype(jnp.float32)) * jax.nn.softplus(a_raw + dt_bias), 1, 2)
    o, s = gated_delta_rule(qh, kh, vh, g, beta, s0.astype(jnp.float32))
    o = jnp.swapaxes(o, 1, 2)
    o = rmsnorm(o, w_onorm) * jax.nn.silu(z.reshape(bsz, l, N_HEADS_A, HEAD_V))
    return o.reshape(bsz, l, WIDTH_A), full[:, -(CONV_W - 1):], s


def pool_branch(p, pool_prev, pos0, w_mix, scale):
    bsz, l, _ = p.shape
    full = jnp.concatenate([pool_prev.astype(jnp.float32), p], axis=1)
    c0 = jnp.concatenate([jnp.zeros((bsz, 1, WIDTH_B), jnp.float32), jnp.cumsum(full, axis=1)], axis=1)
    end = c0[:, POOL_HIST + 1:]
    pos = pos0 + jnp.arange(l)
    outs = []
    for gi, win in enumerate(POOL_WINDOWS):
        lo, hi = gi * POOL_GROUP, (gi + 1) * POOL_GROUP
        start = c0[:, POOL_HIST + 1 - win:POOL_HIST + 1 - win + l, lo:hi]
        cnt = jnp.minimum(pos + 1, win).astype(jnp.float32)
        outs.append((end[..., lo:hi] - start) / cnt[None, :, None])
    pooled = jnp.concatenate(outs, axis=-1) - p
    mixed = jnp.einsum('blgc,gcd->blgd', pooled.reshape(bsz, l, N_POOL_GROUPS, POOL_GROUP), w_mix)
    return mixed.reshape(bsz, l, WIDTH_B) * scale, full[:, -POOL_HIST:]


def trunk(x, conv_prev, pool_prev, ssm_prev, pos0, w_in, w_conv, a_log, dt_bias, w_onorm,
          w_pool_mix, pool_scale, w_a_out, w_b_out, w_o, g_attn, g_mlp, w_up, w_down, g_final):
    out_dtype = x.dtype
    x = x.astype(jnp.float32)
    offs = np.cumsum(IN_SPLITS)[:-1].tolist()
    convs, pools, ssms = [], [], []
    for i in range(DEPTH):
        h = rmsnorm(x, g_attn[i])
        q, k, v, z, b_raw, a_raw, p, ga, gb = jnp.split(h @ w_in[i], offs, axis=-1)
        o_a, conv_new, s_new = gated_delta_branch(q, k, v, z, b_raw, a_raw, conv_prev[i], ssm_prev[i],
                                                  w_conv[i], a_log[i], dt_bias[i], w_onorm[i])
        o_b, pool_new = pool_branch(p, pool_prev[i], pos0, w_pool_mix[i], pool_scale[i])
        merged = jax.nn.sigmoid(ga) * (o_a @ w_a_out[i]) + jax.nn.sigmoid(gb) * (o_b @ w_b_out[i])
        x = x + merged @ w_o[i]
        h2 = rmsnorm(x, g_mlp[i])
        x = x + jnp.square(jax.nn.relu(h2 @ w_up[i])) @ w_down[i]
        convs.append(conv_new)
        pools.append(pool_new)
        ssms.append(s_new)
    y = rmsnorm(x, g_final).astype(out_dtype)
    return y, jnp.stack(convs), jnp.stack(pools), jnp.stack(ssms)


def setup_inputs(seed: int = 0) -> dict:
    key = jax.random.key(seed)
    ks = jax.random.split(key, 24)
    f32 = jnp.float32

    def nrm(k, shape, scale):
        return jax.random.normal(k, shape, f32) * scale

    dt = jnp.exp(jax.random.uniform(ks[8], (DEPTH, N_HEADS_A), f32, math.log(1e-3), math.log(1e-1)))
    return {
        'x_prompt': nrm(ks[0], (BATCH, SEQ, D_MODEL), 1.0),
        'x_sample': nrm(ks[1], (DEC_BATCH, DEC_SEQ, D_MODEL), 1.0),
        'state_conv': nrm(ks[2], (DEPTH, DEC_BATCH, CONV_W - 1, QKV_W), 1.0),
        'state_pool': nrm(ks[3], (DEPTH, DEC_BATCH, POOL_HIST, WIDTH_B), 1.0),
        'state_ssm': nrm(ks[4], (DEPTH, DEC_BATCH, N_HEADS_A, HEAD_K, HEAD_V), 0.5),
        'w_in': nrm(ks[5], (DEPTH, D_MODEL, IN_W), D_MODEL ** -0.5),
        'w_conv': nrm(ks[6], (DEPTH, CONV_W, QKV_W), CONV_W ** -0.5),
        'a_log': jnp.log(jax.random.uniform(ks[7], (DEPTH, N_HEADS_A), f32, 1.0, 16.0)),
        'dt_bias': dt + jnp.log(-jnp.expm1(-dt)),
        'w_onorm': 1.0 + nrm(ks[9], (DEPTH, HEAD_V), 0.02),
        'w_pool_mix': nrm(ks[10], (DEPTH, N_POOL_GROUPS, POOL_GROUP, POOL_GROUP), POOL_GROUP ** -0.5),
        'pool_scale': 1.0 + nrm(ks[11], (DEPTH, WIDTH_B), 0.02),
        'w_a_out': nrm(ks[12], (DEPTH, WIDTH_A, D_MODEL), WIDTH_A ** -0.5),
        'w_b_out': nrm(ks[13], (DEPTH, WIDTH_B, D_MODEL), WIDTH_B ** -0.5),
        'w_o': nrm(ks[14], (DEPTH, D_MODEL, D_MODEL), D_MODEL ** -0.5),
        'g_attn': 1.0 + nrm(ks[15], (DEPTH, D_MODEL), 0.02),
        'g_mlp': 1.0 + nrm(ks[16], (DEPTH, D_MODEL), 0.02),
        'w_up': nrm(ks[17], (DEPTH, D_MODEL, D_FF), D_MODEL ** -0.5),
        'w_down': nrm(ks[18], (DEPTH, D_FF, D_MODEL), D_FF ** -0.5),
        'g_final': 1.0 + nrm(ks[19], (D_MODEL,), 0.02),
    }


def reference(x_prompt, x_sample, state_conv, state_pool, state_ssm, w_in, w_conv, a_log, dt_bias,
              w_onorm, w_pool_mix, pool_scale, w_a_out, w_b_out, w_o, g_attn, g_mlp, w_up, w_down,
              g_final):
    bp = x_prompt.shape[0]
    zero_conv = jnp.zeros((DEPTH, bp, CONV_W - 1, QKV_W), jnp.float32)
    zero_pool = jnp.zeros((DEPTH, bp, POOL_HIST, WIDTH_B), jnp.float32)
    zero_ssm = jnp.zeros((DEPTH, bp, N_HEADS_A, HEAD_K, HEAD_V), jnp.float32)
    y_prompt, conv_p, pool_p, ssm_p = trunk(
        x_prompt, zero_conv, zero_pool, zero_ssm, 0, w_in, w_conv, a_log, dt_bias, w_onorm,
        w_pool_mix, pool_scale, w_a_out, w_b_out, w_o, g_attn, g_mlp, w_up, w_down, g_final)
    y_sample, conv_s, pool_s, ssm_s = trunk(
        x_sample, state_conv, state_pool, state_ssm, PAST_LEN, w_in, w_conv, a_log, dt_bias, w_onorm,
        w_pool_mix, pool_scale, w_a_out, w_b_out, w_o, g_attn, g_mlp, w_up, w_down, g_final)
    return (y_prompt, y_sample,
            conv_p.astype(state_conv.dtype), pool_p.astype(state_pool.dtype), ssm_p.astype(state_ssm.dtype),
            conv_s.astype(state_conv.dtype), pool_s.astype(state_pool.dtype), ssm_s.astype(state_ssm.dtype))
```

```python
import math
from contextlib import ExitStack

import numpy as np
import concourse.bass as bass
import concourse.mybir as mybir
from concourse.bass_utils import run_bass_kernel_spmd

F32 = mybir.dt.float32
F32R = mybir.dt.float32r
AF = mybir.ActivationFunctionType
ALU = mybir.AluOpType

D = 1024
NH = 4
QKVW = 1536
INW = 4616
DFF = 4096
EPS = 1e-6
SEQ = 2048
NS = 16
DS = 8
TB = 512
CP = 128
NEG = -1.0e30
FORCE_DVE_EVAC = True


class Trk:
    __slots__ = ("w", "r")

    def __init__(self):
        self.w = None
        self.r = []


class Tl:
    def __init__(self, name, t):
        self.name = name
        self.t = t
        self.tr = {None: Trk()}

    def trackers(self, key):
        if key is None:
            return list(self.tr.values())
        if key not in self.tr:
            self.tr[key] = Trk()
        return [self.tr[key], self.tr[None]]

    def tracker(self, key):
        if key not in self.tr:
            self.tr[key] = Trk()
        return self.tr[key]


def _norm(acc):
    out = []
    for a in acc:
        if isinstance(a, Tl):
            out.append((a, None))
        else:
            out.append(a)
    return out


class KB:
    ENG = ("pe", "act", "dve", "pool", "sp")

    def __init__(self, nc, stack):
        self.nc = nc
        self.stack = stack
        self.ops = {e: [] for e in self.ENG}
        self.sem = {e: stack.enter_context(nc.semaphore("s_" + e)) for e in self.ENG}
        self.cnt = {e: 0 for e in self.ENG}
        self.waited = {e: {} for e in self.ENG}
        self.dsem = {}
        self.semobj = {}
        for e in self.ENG:
            self.semobj[self.sem[e].name] = self.sem[e]
        self.ps_pools = {}
        self.ps_idx = {}
        self.n_sb = 0
        self.needed = set()

    def sb(self, name, shape, dtype=F32):
        t = self.stack.enter_context(self.nc.sbuf_tensor(name, list(shape), dtype))
        return Tl(name, t)

    def psum(self, name, shape):
        t = self.stack.enter_context(self.nc.psum_tensor(name, list(shape), F32))
        return Tl(name, t)

    def mkpool(self, pname, n):
        self.ps_pools[pname] = [self.psum("%s%d" % (pname, i), [128, 512]) for i in range(n)]
        self.ps_idx[pname] = 0

    def ps(self, pname):
        i = self.ps_idx[pname]
        self.ps_idx[pname] = (i + 1) % len(self.ps_pools[pname])
        return self.ps_pools[pname][i]

    def _deps(self, reads, writes):
        evs = []
        for tl, key in reads:
            for trk in tl.trackers(key):
                if trk.w is not None:
                    evs.append(trk.w)
        for tl, key in writes:
            for trk in tl.trackers(key):
                if trk.w is not None:
                    evs.append(trk.w)
                evs.extend(trk.r)
        return evs

    def _commit(self, ev, reads, writes):
        for tl, key in reads:
            tl.tracker(key).r.append(ev)
        for tl, key in writes:
            if key is None:
                for trk in tl.tr.values():
                    trk.w = ev
                    trk.r = []
            else:
                trk = tl.tracker(key)
                trk.w = ev
                trk.r = []

    def _waits(self, eng, evs):
        need = {}
        for semname, val in evs:
            if eng == "pe" and semname == self.sem["pe"].name:
                continue
            if self.waited[eng].get(semname, 0) >= val:
                continue
            if need.get(semname, 0) < val:
                need[semname] = val
        for k, v in need.items():
            self.waited[eng][k] = v
            self.needed.add((k, v))
        return [(self.semobj[k], v) for k, v in need.items()]

    def op(self, eng, fn, reads=(), writes=(), inc=True):
        reads = _norm(reads)
        writes = _norm(writes)
        waits = self._waits(eng, self._deps(reads, writes))
        ev = (self.sem[eng].name, self.cnt[eng] + 1)
        if inc:
            self.cnt[eng] += 1
        self.ops[eng].append((waits, fn, self.sem[eng] if inc else None, 1))
        self._commit(ev, reads, writes)

    def dma(self, q, out_ap, in_ap, reads, writes, semkey, slow=False):
        reads = _norm(reads)
        writes = _norm(writes)
        waits = self._waits(q, self._deps(reads, writes))
        if semkey not in self.dsem:
            s = self.stack.enter_context(self.nc.semaphore("d%d" % len(self.dsem)))
            self.dsem[semkey] = [s, 0]
            self.semobj[s.name] = s
        ent = self.dsem[semkey]
        ent[1] += 16
        ev = (ent[0].name, ent[1])
        if slow:
            self.ops[q].append((waits, lambda e: e.dma_start(out=out_ap, in_=in_ap, allow_slow_non_contiguous=True), ent[0], 16))
        else:
            self.ops[q].append((waits, lambda e: e.dma_start(out=out_ap, in_=in_ap), ent[0], 16))
        self._commit(ev, reads, writes)

    def finish(self):
        waits = []
        for semkey, (s, v) in self.dsem.items():
            if self.waited["sp"].get(s.name, 0) < v:
                waits.append((s, v))
        self.ops["sp"].append((waits, None, None, 0))

    def emit(self):
        nc = self.nc
        kb = self

        eng_sems = {kb.sem[e_].name: e_ for e_ in kb.ENG}
        remap = {}
        for semname, e_ in eng_sems.items():
            vals = sorted(v for (k, v) in kb.needed if k == semname)
            remap[semname] = {v: i + 1 for i, v in enumerate(vals)}

        def run(e, name):
            cnt = 0
            for waits, fn, sem, incv in kb.ops[name]:
                for s, v in waits:
                    if s.name in remap:
                        e.wait_ge(s, remap[s.name][v])
                    else:
                        e.wait_ge(s, v)
                if fn is None:
                    continue
                ins = fn(e)
                if sem is not None:
                    if sem.name in remap:
                        cnt += 1
                        if cnt in remap[sem.name]:
                            ins.then_inc(sem, 1)
                    else:
                        ins.then_inc(sem, incv)

        with nc.Block() as block:
            @block.sync
            def _(e):
                run(e, "sp")

            @block.tensor
            def _(e):
                run(e, "pe")

            @block.scalar
            def _(e):
                run(e, "act")

            @block.vector
            def _(e):
                run(e, "dve")

            @block.gpsimd
            def _(e):
                run(e, "pool")

    def mm(self, out, lhsT, rhs, start, stop, reads, writes):
        self.op("pe", lambda e: e.matmul(out, lhsT, rhs, start=start, stop=stop), reads, writes, inc=stop)

    def tr(self, out, in_, ident, reads, writes, inc=True):
        self.op("pe", lambda e: e.transpose(out, in_, ident), reads, writes, inc=inc)

    def act(self, out, in_, func, reads, writes, bias=None, scale=None, accum_out=None):
        kw = {}
        if bias is not None:
            kw["bias"] = bias
        if scale is not None:
            kw["scale"] = scale
        if accum_out is not None:
            kw["accum_out"] = accum_out
        self.op("act", lambda e: e.activation(out=out, in_=in_, func=func, **kw), reads, writes)

    def ts(self, eng, out, in0, s1, s2, op0, op1, reads, writes):
        if op1 is None:
            self.op(eng, lambda e: e.tensor_scalar(out, in0, s1, None, op0), reads, writes)
        else:
            self.op(eng, lambda e: e.tensor_scalar(out, in0, s1, s2, op0, op1), reads, writes)

    def tt(self, eng, out, in0, in1, op, reads, writes):
        self.op(eng, lambda e: e.tensor_tensor(out, in0, in1, op), reads, writes)

    def stt(self, out, in0, scalar, in1, op0, op1, reads, writes):
        self.op("dve", lambda e: e.scalar_tensor_tensor(out, in0, scalar, in1, op0, op1), reads, writes)

    def copy(self, eng, out, in_, reads, writes):
        if eng == "act":
            self.op("act", lambda e: e.copy(out, in_), reads, writes)
        else:
            self.op(eng, lambda e: e.tensor_copy(out, in_), reads, writes)

    def memset(self, eng, ap, val, writes):
        self.op(eng, lambda e: e.memset(ap, val), (), writes)


class StopBuild(Exception):
    pass


def build_program(stop=None, stop_block=0, dumps=()):
    nc = bass.Bass("TRN2", target_bir_lowering=False)

    def din(name, shape):
        return nc.dram_tensor(name, list(shape), F32, kind="ExternalInput").ap()

    def dout(name, shape):
        return nc.dram_tensor(name, list(shape), F32, kind="ExternalOutput").ap()

    xp = din("xp", [SEQ, D])
    xs = din("xs", [NS * DS, D])
    sconv = din("sconv", [NS * 3, QKVW])
    spool = din("spool", [NS * 15, 512])
    sssm = din("sssm", [NS, NH, 128, 128])
    w_in = din("w_in", [D, INW])
    w_conv = din("w_conv", [4, QKVW])
    a_log = din("a_log", [NH, 1])
    dt_bias = din("dt_bias", [NH, 1])
    w_onorm = din("w_onorm", [128, 1])
    w_mix = din("w_mix", [4, 128, 128])
    pool_scale = din("pool_scale", [512, 1])
    w_a = din("w_a", [512, D])
    w_b = din("w_b", [512, D])
    w_o = din("w_o", [D, D])
    g_attn = din("g_attn", [D, 1])
    g_mlp = din("g_mlp", [D, 1])
    w_up = din("w_up", [D, DFF])
    w_down = din("w_down", [DFF, D])
    g_final = din("g_final", [1, D])

    yp = dout("yp", [SEQ, D])
    ys = dout("ys", [NS * DS, D])
    convp = dout("convp", [3, QKVW])
    poolp = dout("poolp", [15, 512])
    ssmp = dout("ssmp", [NH, 128, 128])
    convs = dout("convs", [NS * 3, QKVW])
    pools = dout("pools", [NS * 15, 512])
    ssms = dout("ssms", [NS, NH, 128, 128])

    with ExitStack() as stack:
        kb = KB(nc, stack)
        TILES = {}
        def sb(name, shape, dtype=F32):
            tl = kb.sb(name, shape, dtype)
            TILES[name] = tl
            return tl
        OUT = Tl("OUT", None)

        X = sb("X", [128, 4, D])
        hT = sb("hT", [128, 8, TB])
        STG = [sb("stg%d" % i, [128, 4096]) for i in range(2)]
        PRE = [sb("pre%d" % i, [128, TB + 3]) for i in range(2)]
        HIST = sb("hist", [128, 12, 3])
        CV = sb("cv", [128, 12, TB])
        PT = sb("pT", [128, 4, TB + 15])
        PL = sb("pooled", [128, 4, TB])
        OB = sb("obT", [128, 4, TB])
        OA = sb("oaT", [128, 4, TB])
        SQ = sb("sq", [128, TB])
        SQR = sb("sqr", [128, TB])
        RS = sb("rsb", [128, TB])
        GBC = sb("gbc", [128, 4, TB])
        LBC = sb("lbc", [128, 4, TB])
        S = sb("S", [128, 4, 128])
        SS = [sb("Ss%d" % i, [128, 4, 128]) for i in range(2)]
        ssq = sb("ssq", [128, 4])
        rstd = sb("rstd", [128, 4])
        ident = sb("ident", [128, 128])
        ones = sb("ones", [128, 128])
        c128 = sb("c128", [128, 128])
        negup = sb("negup", [CP, CP])
        negcz = sb("negcz", [CP, CP])
        poslo = sb("poslo", [CP, CP])
        selg = sb("selg", [8, 4, 128])
        sell = sb("sell", [8, 4, 128])
        mask_p = sb("mask_p", [8, TB])
        mask_s = sb("mask_s", [8, NS * DS])
        gattn_c = sb("gattn_c", [128, 8])
        gmlp_c = sb("gmlp_c", [128, 8])
        gfin = sb("gfin", [128, D])
        wconv_c = sb("wconv_c", [128, 4, 12])
        pscale_c = sb("pscale_c", [128, 4])
        onorm_c = sb("onorm_c", [128, 1])
        wmix = sb("wmix", [128, 4, 128])
        biascol = sb("biascol", [8, 1])
        scalecol = sb("scalecol", [8, 1])
        mulcol = sb("mulcol", [8, 1])
        alog_t = sb("alog_t", [8, 1])
        WBA = sb("wba", [128, 8, 8])
        invcnt = sb("invcnt", [128, 4, 16])
        epscol = sb("epscol", [128, 1])
        eps128col = sb("eps128col", [128, 1])
        onecol = sb("onecol", [128, 1])
        lntmp = sb("lntmp", [128, 4])
        RC = sb("rc", [8, TB])
        COL = sb("col", [CP, 24])
        KBE = sb("kbe", [CP, 4, 128])
        KD = sb("kd", [CP, 4, 128])
        VB = sb("vb", [CP, 4, 128])
        E1 = sb("e1", [CP, 4, CP])
        E2 = sb("e2", [CP, 4, CP])
        E3 = sb("e3", [CP, 4, CP])
        QKD = sb("qkd", [CP, 4, CP])
        PRt = sb("prt", [CP, 4, 2 * CP])
        PTA = [sb("pta%d" % i, [CP, 4, CP]) for i in range(2)]
        WTN = sb("wtn", [128, 4, CP])
        VNEW = sb("vnew", [CP, 4, 128])
        EG = sb("eg", [128, 4, CP])
        QD = sb("qd", [128, 4, CP])
        STP = [sb("stp%d" % i, [128, 512]) for i in range(2)]
        UTT = [sb("utt%d" % i, [NS, 128]) for i in range(2)]
        UT = [sb("ut%d" % i, [128, NS]) for i in range(2)]
        LSEL = sb("lsel", [128, 128])
        MADD = sb("madd", [128, 128])

        class _V:
            pass
        STC = _V()
        STC.t = PL.t[0:48, :, :].rearrange("p g t -> p (g t)")[:, 0:QKVW]
        XNv = CV.t[:, 10:12, :].rearrange("p a b -> p (a b)")
        XNk = [(CV, 10), (CV, 11)]
        ROW1 = _V(); ROW1.t = SQ.t[0:8, :]
        ROW2 = _V(); ROW2.t = RS.t[0:8, :]
        dbg_aps = {}

        def checkpoint(label, bi_):
            if stop is None or label != stop or bi_ != stop_block:
                return
            for nm in dumps:
                tl = TILES[nm]
                shp = list(tl.t.shape)
                d = nc.dram_tensor("dbg_" + nm, shp, F32, kind="ExternalOutput").ap()
                kb.dma("sp", d, tl.t[:], [tl], [], "dbg_" + nm)
            raise StopBuild()

        kb.mkpool("big", 4)
        kb.mkpool("sm", 4)

        R32 = lambda ap: ap.bitcast(F32R)

        kb.memset("pool", CV.t[:, 0, :], 1.0, [CV])
        kb.copy("dve", R32(ones.t[:]), CV.t[:, 0, 0:128], [CV], [ones])
        kb.ts("dve", R32(c128.t[:]), CV.t[:, 0, 0:128], 128.0, None, ALU.mult, None, [CV], [c128])
        kb.op("pool", lambda e: e.affine_select(ident.t[:], CV.t[:, 0, 0:128], [[1, 128]], ALU.is_equal, 0.0, base=0, channel_multiplier=-1), [CV], [ident])
        kb.memset("pool", CV.t[:, 1, 0:CP], 0.0, [CV])
        zsrc = CV.t[0:CP, 1, 0:CP]
        kb.op("pool", lambda e: e.affine_select(negup.t[:], zsrc, [[1, CP]], ALU.is_gt, NEG, base=0, channel_multiplier=-1), [CV], [negup])
        kb.op("pool", lambda e: e.affine_select(negcz.t[:], zsrc, [[1, CP]], ALU.is_ge, NEG, base=0, channel_multiplier=-1), [CV], [negcz])
        kb.op("pool", lambda e: e.affine_select(poslo.t[:], zsrc, [[-1, CP]], ALU.is_gt, -NEG, base=0, channel_multiplier=1), [CV], [poslo])
        o8v = CV.t[0:8, 0, :].rearrange("p (h m) -> p h m", h=4)
        kb.op("pool", lambda e: e.affine_select(selg.t[:], o8v, [[-1, 4], [0, 128]], ALU.is_equal, 0.0, base=0, channel_multiplier=1), [CV], [selg])
        kb.op("pool", lambda e: e.affine_select(sell.t[:], selg.t[:], [[-1, 4], [0, 128]], ALU.not_equal, 1.0, base=-4, channel_multiplier=1), [selg], [sell])
        for mk, T, c, m1, m1t in ((mask_p, TB, CP, ROW1, SQ), (mask_s, NS * DS, DS, ROW2, RS)):
            kb.op("pool", lambda e, mk=mk, T=T, c=c, m1=m1: e.affine_select(m1.t[:, 0:T], CV.t[0:8, 0, 0:T], [[0, T // c], [1, c]], ALU.not_equal, 0.0, base=0, channel_multiplier=0), [CV], [m1t])
            kb.op("pool", lambda e, mk=mk, T=T, m1=m1: e.affine_select(mk.t[:], m1.t[:, 0:T], [[0, T]], ALU.is_ge, 0.0, base=3, channel_multiplier=-1), [m1t], [mk])
        for gi, win in enumerate((2, 4, 8, 16)):
            kb.memset("pool", invcnt.t[:, gi, :], 1.0 / win, [(invcnt, gi)])
            for t in range(win - 1):
                kb.memset("pool", invcnt.t[:, gi, t:t + 1], 1.0 / (t + 1), [(invcnt, gi)])
        def early(label):
            try:
                checkpoint(label, 0)
            except StopBuild:
                return True
            return False

        stopped = early("setup0")

        def small_load(tl, out_ap, in_ap, semkey):
            kb.dma("sp", out_ap, in_ap, [], [tl], semkey, slow=True)

        small_load(gattn_c, gattn_c.t[:], g_attn.rearrange("(c p) o -> p (c o)", p=128), "gattn")
        small_load(gmlp_c, gmlp_c.t[:], g_mlp.rearrange("(c p) o -> p (c o)", p=128), "gmlp")
        small_load(pscale_c, pscale_c.t[:], pool_scale.rearrange("(c p) o -> p (c o)", p=128), "pscale")
        small_load(onorm_c, onorm_c.t[:], w_onorm[:, :], "onorm")
        for j in range(4):
            small_load(wconv_c, wconv_c.t[:, j, :], w_conv[j:j + 1, :].rearrange("o (c p) -> p (o c)", p=128), "wconv")
        kb.dma("sp", gfin.t[:], g_final[0:1, :].to_broadcast([128, D]), [], [gfin], "gfin")
        kb.dma("sp", wmix.t[:], w_mix.rearrange("g c d -> c g d"), [], [wmix], "wmix")
        kb.memset("dve", biascol.t[:], 0.0, [biascol])
        small_load(biascol, biascol.t[0:4, :], dt_bias[:, :], "biascol")
        kb.memset("dve", scalecol.t[:], -1.0, [scalecol])
        kb.memset("dve", scalecol.t[0:4, :], 1.0, [scalecol])
        kb.memset("dve", alog_t.t[:], 0.0, [alog_t])
        small_load(alog_t, alog_t.t[0:4, :], a_log[:, :], "alog")
        kb.memset("dve", mulcol.t[:], -1.0, [mulcol])
        kb.act(mulcol.t[0:4, :], alog_t.t[0:4, :], AF.Exp, [alog_t], [mulcol])
        kb.ts("dve", mulcol.t[0:4, :], mulcol.t[0:4, :], -1.0, None, ALU.mult, None, [mulcol], [mulcol])
        kb.memset("dve", HIST.t[:], 0.0, [HIST])
        kb.memset("dve", epscol.t[:], EPS, [epscol])
        kb.memset("dve", eps128col.t[:], 128.0 * EPS, [eps128col])
        kb.memset("dve", onecol.t[:], 1.0, [onecol])
        kb.ts("dve", R32(S.t[:].rearrange("p h d -> p (h d)")), gfin.t[:, 0:512], 0.0, None, ALU.mult, None, [gfin], [S])

        for (c0, d0) in ((2052, 0), (2048, 4)):
            kb.dma("pool", R32(WBA.t[:, :, d0:d0 + 4]), w_in[:, c0:c0 + 4].rearrange("(k p) n -> p k n", p=128), [], [WBA], "wba", slow=True)
        stopped = stopped or early("setup1")
        stg_i = [0]

        def stage_k8(w_ap, col0, ncol, extra=None):
            st = STG[stg_i[0] % 2]
            stg_i[0] += 1
            view = st.t[:, 0:8 * ncol].rearrange("p (k n) -> p k n", k=8)
            kb.dma("pool", R32(view), w_ap[:, col0:col0 + ncol].rearrange("(k p) n -> p k n", p=128), [], [st], st.name)
            return st, view

        def stage_k4(w_ap, row0):
            st = STG[stg_i[0] % 2]
            stg_i[0] += 1
            view = st.t[:, :].rearrange("p (k n) -> p k n", k=4)
            kb.dma("pool", R32(view), w_ap[row0:row0 + 512, :].rearrange("(k p) n -> p k n", p=128), [], [st], st.name)
            return st, view

        def chunk_epilogue(po, po3, cs, cw):
            n4 = 4 * cw
            sq3 = SQR.t[:, 0:n4].rearrange("p (h n) -> p h n", h=4)
            rs3 = RS.t[:, 0:n4].rearrange("p (h n) -> p h n", h=4)
            kb.act(R32(sq3), po3, AF.Square, [po], [SQR])
            p4 = kb.ps("big")
            kb.mm(p4.t[:, 0:n4], R32(ones.t[:, :]), R32(SQR.t[:, 0:n4]), True, True, [SQR, ones], [p4])
            kb.act(RS.t[:, 0:n4], p4.t[:, 0:n4], AF.Ln, [p4, epscol], [RS], bias=epscol.t[:, :], scale=1.0 / 128.0)
            kb.act(RS.t[:, 0:n4], RS.t[:, 0:n4], AF.Exp, [RS], [RS], scale=-0.5)
            kb.stt(rs3, rs3, onorm_c.t[:, 0:1], PL.t[:, :, cs], ALU.mult, ALU.mult, [RS, onorm_c, PL], [RS])
            kb.tt("dve", R32(OA.t[:, :, cs]), po3, rs3, ALU.mult, [po, RS], [(OA, 0), (OA, 1), (OA, 2), (OA, 3)])

        evac_flip = [0]

        def evac_eng():
            evac_flip[0] ^= 1
            return "act" if evac_flip[0] else "dve"

        blocks = [("p", i * TB) for i in range(SEQ // TB)] + [("s", 0)]
        def run_all():
          for bi, (kind, tok0) in enumerate(blocks):
            T = TB if kind == "p" else NS * DS
            nt = T // 128
            c = CP
            nch = T // c
            xsrc = xp[tok0:tok0 + T, :] if kind == "p" else xs[:, :]
            ydst = yp[tok0:tok0 + T, :] if kind == "p" else ys[:, :]
            first_prompt = kind == "p" and tok0 == 0
            last_prompt = kind == "p" and tok0 == SEQ - TB

            for i in range(nt):
                kb.dma("pool", X.t[:, i, :], xsrc[i * 128:(i + 1) * 128, :], [], [(X, i)], "X%d" % i)

            def norm_to_hT(gcol):
                for i in range(nt):
                    kb.act(XNv, X.t[:, i, :], AF.Square, [(X, i)], XNk + [(ssq, i)], accum_out=ssq.t[:, i:i + 1])
                    kb.act(lntmp.t[:, i:i + 1], ssq.t[:, i:i + 1], AF.Ln, [(ssq, i), epscol], [(lntmp, i)], bias=epscol.t[:, :], scale=1.0 / D)
                    kb.act(rstd.t[:, i:i + 1], lntmp.t[:, i:i + 1], AF.Exp, [(lntmp, i)], [(rstd, i)], scale=-0.5)
                    checkpoint("S0a", bi)
                    kb.ts("dve", XNv, X.t[:, i, :], rstd.t[:, i:i + 1], None, ALU.mult, None, [(X, i), (rstd, i)], XNk)
                    checkpoint("S0b", bi)
                    for half in range(2):
                        p = kb.ps("big")
                        for q in range(4):
                            cch = half * 4 + q
                            kb.tr(p.t[:, q * 128:(q + 1) * 128], XNv[:, cch * 128:(cch + 1) * 128], ident.t[:], XNk + [ident], [p], inc=(q == 3))
                        checkpoint("S0c", bi)
                        for q in range(4):
                            cch = half * 4 + q
                            eng = "dve" if FORCE_DVE_EVAC else evac_eng()
                            dst = R32(hT.t[:, cch, i * 128:(i + 1) * 128])
                            src = p.t[:, q * 128:(q + 1) * 128]
                            if eng == "act":
                                kb.act(dst, src, AF.Copy, [p, gcol], [(hT, cch)], scale=gcol.t[:, cch:cch + 1])
                            else:
                                kb.ts("dve", dst, src, gcol.t[:, cch:cch + 1], None, ALU.mult, None, [p, gcol], [(hT, cch)])

            norm_to_hT(gattn_c)
            checkpoint("S0", bi)

            def proj_chunk(stt_, view, m, ncols_m=128):
                p = kb.ps("big")
                for k in range(8):
                    kb.mm(p.t[:, 0:T], R32(view[:, k, m * 128:m * 128 + ncols_m]), R32(hT.t[:, k, 0:T]), k == 0, k == 7,
                          [(hT, k), stt_], [p])
                return p


            pba = kb.ps("big")
            for k in range(8):
                kb.mm(pba.t[0:8, 0:T], R32(WBA.t[:, k, :]), R32(hT.t[:, k, 0:T]), k == 0, k == 7, [(hT, k), WBA], [pba])
            kb.act(ROW1.t[:, 0:T], pba.t[0:8, 0:T], AF.Exp, [pba, biascol, scalecol], [SQ], bias=biascol.t[:, :], scale=scalecol.t[:, :])
            kb.act(ROW2.t[:, 0:T], ROW1.t[:, 0:T], AF.Ln, [SQ, onecol], [RS], bias=onecol.t[0:8, :])
            kb.ts("dve", ROW1.t[:, 0:T], ROW2.t[:, 0:T], mulcol.t[:, :], None, ALU.mult, None, [RS, mulcol], [SQ])
            mk = mask_p if kind == "p" else mask_s
            kb.op("dve", lambda e, mk=mk, T=T: e.tensor_tensor_scan(RC.t[:, 0:T], mk.t[:, 0:T], ROW1.t[:, 0:T], 0.0, ALU.mult, ALU.add), [mk, SQ], [RC])
            checkpoint("S1a", bi)
            if kind == "s":
                kb.dma("sp", STC.t[:, :], sconv[:, :], [], [PL], "STC")
            deferred = []

            if kind == "p":
                stages_ = {}

                def conv_A(j):
                    grp = j // 4
                    if grp not in stages_:
                        stages_[grp] = stage_k8(w_in, grp * 512, 512)
                    st, vw = stages_[grp]
                    p = proj_chunk(st, vw, j % 4)
                    pre = PRE[j % 2]
                    kb.copy("dve", pre.t[:, 0:3], HIST.t[:, j, :], [(HIST, j)], [pre])
                    kb.copy("act", pre.t[:, 3:3 + T], p.t[:, 0:T], [p], [pre])
                    kb.act(CV.t[:, j, 0:T], p.t[:, 0:T], AF.Identity, [p, wconv_c], [(CV, j)], scale=wconv_c.t[:, 3, j:j + 1])

                def conv_B(j):
                    pre = PRE[j % 2]
                    cvv = CV.t[:, j, 0:T]
                    for tap in (2, 1, 0):
                        kb.stt(cvv, pre.t[:, tap:tap + T], wconv_c.t[:, tap, j:j + 1], cvv, ALU.mult, ALU.add, [pre, wconv_c, (CV, j)], [(CV, j)])
                    kb.act(cvv, cvv, AF.Silu, [(CV, j)], [(CV, j)])
                    kb.copy("dve", HIST.t[:, j, :], pre.t[:, T:T + 3], [pre], [(HIST, j)])
                    if last_prompt:
                        p3 = kb.ps("sm")
                        kb.tr(p3.t[0:3, 0:128], pre.t[:, T:T + 3], ident.t[:], [pre, ident], [p3])
                        kb.copy("act", STC.t[0:3, j * 128:(j + 1) * 128], p3.t[0:3, 0:128], [p3], [PL])
                    if j < 8:
                        deferred.append((j, j // 4))

                conv_A(0)
                for j in range(12):
                    if j + 1 < 12:
                        conv_A(j + 1)
                    conv_B(j)
            else:
                stages_ = {}

                def sviews(j):
                    pre = PRE[j % 2]
                    pre3 = pre.t[:, 0:NS * 11].rearrange("p (s w) -> p s w", w=11)
                    cvv = CV.t[:, j, 0:T].rearrange("p (s w) -> p s w", w=8)
                    return pre, pre3, cvv

                def sconv_A(j):
                    grp = j // 4
                    if grp not in stages_:
                        stages_[grp] = stage_k8(w_in, grp * 512, 512)
                    st, vw = stages_[grp]
                    p = proj_chunk(st, vw, j % 4)
                    pre, pre3, cvv = sviews(j)
                    p2 = kb.ps("sm")
                    kb.tr(p2.t[:, 0:48], STC.t[:, j * 128:(j + 1) * 128], ident.t[0:48, 0:48], [(PL, ("s", j)), ident], [p2])
                    kb.copy("dve", pre3[:, :, 0:3], p2.t[:, 0:48].rearrange("p (s w) -> p s w", w=3), [p2], [pre])
                    kb.copy("act", pre3[:, :, 3:11], p.t[:, 0:T].rearrange("p (s w) -> p s w", w=8), [p], [pre])
                    kb.act(cvv, p.t[:, 0:T].rearrange("p (s w) -> p s w", w=8), AF.Identity, [p, wconv_c], [(CV, j)], scale=wconv_c.t[:, 3, j:j + 1])

                def sconv_B(j):
                    pre, pre3, cvv = sviews(j)
                    for tap in (2, 1, 0):
                        kb.stt(cvv, pre3[:, :, tap:tap + 8], wconv_c.t[:, tap, j:j + 1], cvv, ALU.mult, ALU.add, [pre, wconv_c, (CV, j)], [(CV, j)])
                    kb.act(cvv, cvv, AF.Silu, [(CV, j)], [(CV, j)])
                    p3 = kb.ps("sm")
                    tmpc = SQ.t[:, 0:48].rearrange("p (s w) -> p s w", w=3)
                    kb.copy("dve", tmpc, pre3[:, :, 8:11], [pre], [SQ])
                    kb.tr(p3.t[0:48, 0:128], SQ.t[:, 0:48], ident.t[:], [SQ, ident], [p3])
                    kb.copy("act", STC.t[:, j * 128:(j + 1) * 128], p3.t[0:48, 0:128], [p3], [(PL, ("s", j))])
                    if j < 8:
                        deferred.append((j, j // 4))

                sconv_A(0)
                for j in range(12):
                    if j + 1 < 12:
                        sconv_A(j + 1)
                    sconv_B(j)
            if last_prompt:
                kb.dma("sp", convp[:, :], STC.t[0:3, :], [PL], [], "STC")
            if kind == "s":
                kb.dma("sp", convs[:, :], STC.t[:, :], [PL], [], "STC")

            checkpoint("conv", bi)
            st, vw = stage_k8(w_in, 2056, 512)
            stz, vwz = stage_k8(w_in, 1536, 512)

            def z_chunk(m):
                pz = proj_chunk(stz, vwz, m)
                kb.act(PL.t[:, m, 0:T], pz.t[:, 0:T], AF.Silu, [pz], [(PL, m)])

            if kind == "s":
                for r in range(2):
                    nr = 128 if r == 0 else NS * 15 - 128
                    kb.dma("sp", STP[r].t[0:nr, :], spool[r * 128:r * 128 + nr, :], [], [STP[r]], STP[r].name)
            for gi, win in enumerate((2, 4, 8, 16)):
                p = proj_chunk(st, vw, gi)
                if gi >= 1:
                    z_chunk(gi - 1)
                if kind == "p":
                    if first_prompt:
                        kb.memset("dve", PT.t[:, gi, 0:15], 0.0, [(PT, gi)])
                    else:
                        kb.copy("dve", PT.t[:, gi, 0:15], PT.t[:, gi, TB:TB + 15], [(PT, gi)], [(PT, gi)])
                    kb.copy(evac_eng(), PT.t[:, gi, 15:15 + T], p.t[:, 0:T], [p], [(PT, gi)])
                    W = T + 15
                    full = lambda a, b: PT.t[:, gi, a:b]
                    newp = PT.t[:, gi, 15:15 + T]
                    plv = lambda a, b: PL.t[:, gi, a:b]
                    obv = OB.t[:, gi, 0:T]
                    cur = full
                    curoff = 0
                    kb.tt("dve", plv(0, T), full(15, 15 + T), full(14, 14 + T), ALU.add, [(PT, gi)], [(PL, gi)])
                    d = 2
                    while d < win:
                        d *= 2
                    if win > 2:
                        for dd in range(2, win):
                            kb.tt("dve", plv(0, T), plv(0, T), full(15 - dd, 15 - dd + T), ALU.add, [(PT, gi), (PL, gi)], [(PL, gi)])
                    if first_prompt:
                        kb.tt("dve", plv(0, 16), plv(0, 16), invcnt.t[:, gi, :], ALU.mult, [(PL, gi), (invcnt, gi)], [(PL, gi)])
                        kb.ts("dve", plv(16, T), plv(16, T), 1.0 / win, None, ALU.mult, None, [(PL, gi)], [(PL, gi)])
                        kb.tt("dve", plv(0, T), plv(0, T), newp, ALU.subtract, [(PL, gi), (PT, gi)], [(PL, gi)])
                    else:
                        kb.stt(plv(0, T), plv(0, T), 1.0 / win, newp, ALU.mult, ALU.subtract, [(PL, gi), (PT, gi)], [(PL, gi)])
                    if last_prompt:
                        p3 = kb.ps("sm")
                        kb.tr(p3.t[0:15, 0:128], PT.t[:, gi, T:T + 15], ident.t[:], [(PT, gi), ident], [p3])
                        kb.copy("act", STP[0].t[0:15, gi * 128:(gi + 1) * 128], p3.t[0:15, 0:128], [p3], [STP[0]])
                else:
                    pt3 = PT.t[:, gi, 0:NS * 23].rearrange("p (s w) -> p s w", w=23)
                    for r in range(2):
                        nr = 128 if r == 0 else NS * 15 - 128
                        p2 = kb.ps("sm")
                        kb.tr(p2.t[:, 0:nr], STP[r].t[0:nr, gi * 128:(gi + 1) * 128], ident.t[0:nr, 0:nr], [STP[r], ident], [p2])
                        kb.copy("dve", SQ.t[:, r * 128:r * 128 + nr], p2.t[:, 0:nr], [p2], [SQ])
                    kb.copy("dve", pt3[:, :, 0:15], SQ.t[:, 0:NS * 15].rearrange("p (s w) -> p s w", w=15), [SQ], [(PT, gi)])
                    kb.copy(evac_eng(), pt3[:, :, 15:23], p.t[:, 0:T].rearrange("p (s w) -> p s w", w=8), [p], [(PT, gi)])
                    pl3 = PL.t[:, gi, 0:T].rearrange("p (s w) -> p s w", w=8)
                    newp = pt3[:, :, 15:23]
                    kb.tt("dve", pl3, pt3[:, :, 15:23], pt3[:, :, 14:22], ALU.add, [(PT, gi)], [(PL, gi)])
                    for dd in range(2, win):
                        kb.tt("dve", pl3, pl3, pt3[:, :, 15 - dd:23 - dd], ALU.add, [(PT, gi), (PL, gi)], [(PL, gi)])
                    kb.stt(pl3, pl3, 1.0 / win, newp, ALU.mult, ALU.subtract, [(PL, gi), (PT, gi)], [(PL, gi)])
                    kb.copy("dve", SQ.t[:, 0:NS * 15].rearrange("p (s w) -> p s w", w=15), pt3[:, :, 8:23], [(PT, gi)], [SQ])
                    for r in range(2):
                        nr = 128 if r == 0 else NS * 15 - 128
                        p3 = kb.ps("sm")
                        kb.tr(p3.t[0:nr, 0:128], SQ.t[:, r * 128:r * 128 + nr], ident.t[:], [SQ, ident], [p3])
                        kb.copy("act", STP[r].t[0:nr, gi * 128:(gi + 1) * 128], p3.t[0:nr, 0:128], [p3], [STP[r]])
                p5 = kb.ps("big")
                kb.mm(p5.t[:, 0:T], wmix.t[:, gi, :], PL.t[:, gi, 0:T], True, True, [wmix, (PL, gi)], [p5])
                kb.ts("dve", R32(OB.t[:, gi, 0:T]), p5.t[:, 0:T], pscale_c.t[:, gi:gi + 1], None, ALU.mult, None, [p5, pscale_c], [(OB, gi)])
            if last_prompt:
                kb.dma("sp", poolp[:, :], STP[0].t[0:15, :], [STP[0]], [], STP[0].name)
            if kind == "s":
                for r in range(2):
                    nr = 128 if r == 0 else NS * 15 - 128
                    kb.dma("sp", pools[r * 128:r * 128 + nr, :], STP[r].t[0:nr, :], [STP[r]], [], STP[r].name)

            checkpoint("pool", bi)
            z_chunk(3)

            rsb = [RS, SQ]

            def l2_square(idx):
                j, grp = deferred[idx]
                kb.act(PT.t[:, idx % 4, 0:T], CV.t[:, j, 0:T], AF.Square, [(CV, j)], [(PT, idx % 4)])

            if deferred:
                l2_square(0)
            for idx, (j, grp) in enumerate(deferred):
                if idx + 1 < len(deferred):
                    l2_square(idx + 1)
                p4 = kb.ps("big")
                cm = c128 if grp == 0 else ones
                kb.mm(p4.t[:, 0:T], cm.t[:, :], PT.t[:, idx % 4, 0:T], True, True, [(PT, idx % 4), cm], [p4])
                ec = eps128col if grp == 0 else epscol
                rb = rsb[idx % 2]
                kb.act(rb.t[:, 0:T], p4.t[:, 0:T], AF.Ln, [p4, ec], [rb], bias=ec.t[:, :])
                kb.act(rb.t[:, 0:T], rb.t[:, 0:T], AF.Exp, [rb], [rb], scale=-0.5)
                kb.tt("dve", CV.t[:, j, 0:T], CV.t[:, j, 0:T], rb.t[:, 0:T], ALU.mult, [(CV, j), rb], [(CV, j)])
            del deferred[:]
            for h in range(4):
                for (sel, dstt) in ((selg, GBC), (sell, LBC)):
                    p = kb.ps("big")
                    kb.mm(p.t[:, 0:T], sel.t[:, h, :], RC.t[:, 0:T], True, True, [sel, RC], [p])
                    kb.copy(evac_eng(), dstt.t[:, h, 0:T], p.t[:, 0:T], [p], [(dstt, h)])

            checkpoint("z", bi)
            sta, va = stage_k4(w_a, 0)
            stb, vb_ = stage_k4(w_b, 0)
            nsteps = int(math.log2(c if kind == "p" else DS)) - 1
            if kind == "s":
                kb.memset("pool", SQ.t[:, 0:128], 1.0, [SQ])
                onesrc = SQ.t[:, 0:128]
                kb.op("pool", lambda e: e.affine_select(UTT[0].t[:], onesrc[0:NS, :], [[1, 128]], ALU.is_ge, 0.0, base=0, channel_multiplier=-DS), [SQ], [UTT[0]])
                kb.op("pool", lambda e: e.affine_select(UTT[1].t[:], UTT[0].t[:], [[-1, 128]], ALU.is_ge, 0.0, base=DS - 1, channel_multiplier=DS), [UTT[0]], [UTT[1]])
                kb.op("pool", lambda e: e.affine_select(UT[0].t[:], onesrc[:, 0:NS], [[-DS, NS]], ALU.is_ge, 0.0, base=0, channel_multiplier=1), [SQ], [UT[0]])
                kb.op("pool", lambda e: e.affine_select(UT[1].t[:], UT[0].t[:], [[DS, NS]], ALU.is_ge, 0.0, base=DS - 1, channel_multiplier=-1), [UT[0]], [UT[1]])
                kb.op("pool", lambda e: e.affine_select(LSEL.t[:], onesrc, [[-DS, NS], [0, DS]], ALU.is_equal, 0.0, base=-(DS - 1), channel_multiplier=1), [SQ], [LSEL])
                pbd = kb.ps("sm")
                kb.mm(pbd.t[:, 0:128], UTT[1].t[:, :], UTT[1].t[:, :], True, True, [UTT[1]], [pbd])
                kb.ts("dve", MADD.t[:], pbd.t[:, 0:128], -NEG, NEG, ALU.mult, ALU.add, [pbd], [MADD])
                kb.tt("dve", negup.t[:], negup.t[:], MADD.t[:], ALU.add, [negup, MADD], [negup])
                kb.tt("dve", negcz.t[:], negcz.t[:], MADD.t[:], ALU.add, [negcz, MADD], [negcz])
                kb.tt("dve", poslo.t[:], poslo.t[:], MADD.t[:], ALU.subtract, [poslo, MADD], [poslo])
            for ci in range(nch):
                cs = slice(ci * c, (ci + 1) * c)
                last = (ci + 1) * c - 1
                Sc = S
                pc = kb.ps("sm")
                kb.tr(pc.t[0:c, 0:8], RC.t[0:8, cs], ident.t[0:8, 0:8], [RC, ident], [pc])
                kb.copy("dve", COL.t[0:c, 0:8], pc.t[0:c, 0:8], [pc], [COL])
                kb.tt("dve", COL.t[0:c, 8:12], COL.t[0:c, 0:4], COL.t[0:c, 4:8], ALU.add, [COL], [COL])
                if kind == "p":
                    kb.tt("dve", COL.t[0:c, 12:16], GBC.t[0:c, :, last], COL.t[0:c, 0:4], ALU.subtract, [COL, GBC], [COL])
                else:
                    pg = kb.ps("sm")
                    kb.mm(pg.t[:, 0:4], LSEL.t[:, :], COL.t[:, 0:4], True, True, [LSEL, COL], [pg])
                    kb.tt("dve", COL.t[0:c, 12:16], pg.t[:, 0:4], COL.t[0:c, 0:4], ALU.subtract, [COL, pg], [COL])
                kb.act(COL.t[0:c, 16:24], COL.t[0:c, 8:16], AF.Exp, [COL], [COL])
                kb.act(COL.t[0:c, 4:8], COL.t[0:c, 4:8], AF.Exp, [COL], [COL])
                pk = kb.ps("sm")
                pv_ = kb.ps("sm")
                for h in range(4):
                    kb.tr(pk.t[0:c, h * 128:(h + 1) * 128], CV.t[:, 4 + h, cs], ident.t[:], [(CV, 4 + h), ident], [pk], inc=(h == 3))
                for h in range(4):
                    kb.tr(pv_.t[0:c, h * 128:(h + 1) * 128], CV.t[:, 8 + h, cs], ident.t[:], [(CV, 8 + h), ident], [pv_], inc=(h == 3))
                pk3 = pk.t[0:c, :].rearrange("p (h d) -> p h d", h=4)
                pv3 = pv_.t[0:c, :].rearrange("p (h d) -> p h d", h=4)
                pkk = kb.ps("sm")
                pqk = kb.ps("sm")
                pkk3 = pkk.t[0:c, 0:4 * c].rearrange("p (h n) -> p h n", h=4)
                pqk3 = pqk.t[0:c, 0:4 * c].rearrange("p (h n) -> p h n", h=4)
                for h in range(4):
                    kb.mm(pkk3[:, h, :], CV.t[:, 4 + h, cs], CV.t[:, 4 + h, cs], True, True, [(CV, 4 + h)], [pkk])
                for h in range(4):
                    kb.mm(pqk3[:, h, :], CV.t[:, 4 + h, cs], CV.t[:, h, cs], True, True, [(CV, 4 + h), (CV, h)], [pqk])
                e1 = E1.t[0:c, :, 0:c]
                e2 = E2.t[0:c, :, 0:c]
                e3 = E3.t[0:c, :, 0:c]
                for h in range(4):
                    kb.stt(e1[:, h, :], LBC.t[0:c, h, cs], COL.t[0:c, h:h + 1], negup.t[0:c, 0:c], ALU.subtract, ALU.add, [(LBC, h), COL, negup], [(E1, h)])
                kb.act(e1, e1, AF.Exp, [E1], [E1])
                for h in range(4):
                    kb.stt(e2[:, h, :], GBC.t[0:c, h, cs], COL.t[0:c, 8 + h:9 + h], poslo.t[0:c, 0:c], ALU.subtract, ALU.add, [(GBC, h), COL, poslo], [(E2, h)])
                kb.act(e2, e2, AF.Exp, [E2], [E2], scale=-1.0)
                for h in range(4):
                    kb.stt(e3[:, h, :], GBC.t[0:c, h, cs], COL.t[0:c, h:h + 1], negcz.t[0:c, 0:c], ALU.subtract, ALU.add, [(GBC, h), COL, negcz], [(E3, h)])
                kb.act(e3, e3, AF.Exp, [E3], [E3])
                bcol = lambda a: COL.t[0:c, a:a + 4].unsqueeze(2).to_broadcast([c, 4, 128])
                kb.tt("dve", R32(KBE.t[0:c, :, :]), pk3, bcol(16), ALU.mult, [pk, COL], [KBE])
                kb.tt("dve", R32(KD.t[0:c, :, :]), pk3, bcol(20), ALU.mult, [pk, COL], [KD])
                kb.tt("dve", R32(VB.t[0:c, :, :]), pv3, bcol(4), ALU.mult, [pv_, COL], [VB])
                kb.act(EG.t[:, :, 0:c], GBC.t[:, :, cs], AF.Exp, [GBC], [EG])
                kb.tt("dve", R32(QD.t[:, :, 0:c]), CV.t[:, 0:4, cs], EG.t[:, :, 0:c], ALU.mult, [(CV, 0), (CV, 1), (CV, 2), (CV, 3), EG], [QD])
                T0, T1 = PTA
                pa = lambda tl: tl.t[0:c, :, 0:c]
                PRv = PRt.t[0:c, :, 0:2 * c]
                Pv = PRt.t[0:c, :, 0:c]
                rr = PRt.t[0:c, :, c:2 * c]
                KP, KR = (PRt, "P"), (PRt, "R")
                kb.stt(R32(Pv), pkk3, -1.0, e1, ALU.mult, ALU.mult, [pkk, E1], [KP])
                kb.stt(R32(pa(T0)), pkk3, -1.0, e2, ALU.mult, ALU.mult, [pkk, E2], [T0])
                kb.tt("dve", R32(QKD.t[0:c, :, 0:c]), pqk3, e3, ALU.mult, [pqk, E3], [QKD])
                for h in range(4):
                    kb.copy("pool", R32(rr[:, h, :]), ident.t[0:c, 0:c], [ident], [KR])
                Tc, Tn = T0, T1
                for step in range(1, nsteps + 1):
                    lastst = step == nsteps
                    pas = [kb.ps("sm"), kb.ps("sm")]
                    pa3 = [p_.t[0:c, 0:4 * c].rearrange("p (h n) -> p h n", h=2) for p_ in pas]
                    for h in range(4):
                        kb.mm(pa3[h // 2][:, h % 2, :], R32(pa(Tc)[:, h, :]), R32(PRv[:, h, :]), True, True, [Tc, KP, KR], [pas[h // 2]])
                    pq = kb.ps("sm")
                    pq3 = pq.t[0:c, 0:4 * c].rearrange("p (h n) -> p h n", h=4)
                    for h in range(4):
                        kb.mm(pq3[:, h, :], R32(Pv[:, h, :]), R32(pa(Tc)[:, h, :]), True, True, [Tc, KP], [pq])
                    for hf in range(2):
                        kb.tt("dve", R32(rr[:, 2 * hf:2 * hf + 2, :]), rr[:, 2 * hf:2 * hf + 2, :], pa3[hf][:, :, c:2 * c], ALU.add, [KR, pas[hf]], [KR])
                        if not lastst:
                            kb.copy("act", R32(Pv[:, 2 * hf:2 * hf + 2, :]), pa3[hf][:, :, 0:c], [pas[hf]], [KP])
                    kb.copy("act" if lastst else "dve", R32(pa(Tn)), pq3, [pq], [Tn])
                    Tc, Tn = Tn, Tc
                pr = kb.ps("sm")
                pr3 = pr.t[0:c, 0:4 * c].rearrange("p (h n) -> p h n", h=4)
                for h in range(4):
                    kb.mm(pr3[:, h, :], R32(pa(Tc)[:, h, :]), R32(rr[:, h, :]), True, True, [Tc, KR], [pr])
                kb.tt("dve", R32(rr), rr, pr3, ALU.add, [KR, pr], [KR])
                RR = KR
                pw = kb.ps("sm")
                pw3 = pw.t[:, 0:4 * c].rearrange("p (h n) -> p h n", h=4)
                for h in range(4):
                    kb.mm(pw3[:, h, :], R32(KBE.t[0:c, h, :]), R32(rr[:, h, :]), True, True, [KBE, RR], [pw])
                kb.ts("dve", R32(WTN.t[:, :, 0:c]), pw3, -1.0, None, ALU.mult, None, [pw], [WTN])
                if kind == "p":
                    pn = kb.ps("sm")
                    pn3 = pn.t[0:c, :].rearrange("p (h d) -> p h d", h=4)
                    for h in range(4):
                        kb.mm(pn3[:, h, :], R32(rr[:, h, :]), R32(VB.t[0:c, h, :]), True, False, [RR, VB], [pn])
                        kb.mm(pn3[:, h, :], R32(WTN.t[:, h, 0:c]), R32(Sc.t[:, h, :]), False, True, [WTN, Sc], [pn])
                    kb.copy("act", R32(VNEW.t[0:c, :, :]), pn3, [pn], [VNEW])
                    po = kb.ps("sm")
                    po3 = po.t[:, 0:4 * c].rearrange("p (h n) -> p h n", h=4)
                    for h in range(4):
                        kb.mm(po3[:, h, :], R32(Sc.t[:, h, :]), R32(QD.t[:, h, 0:c]), True, False, [Sc, QD], [po])
                        kb.mm(po3[:, h, :], R32(VNEW.t[0:c, h, :]), R32(QKD.t[0:c, h, 0:c]), False, True, [VNEW, QKD], [po])
                    pS = kb.ps("sm")
                    pS3 = pS.t[:, :].rearrange("p (h d) -> p h d", h=4)
                    for h in range(4):
                        kb.mm(pS3[:, h, :], R32(KD.t[0:c, h, :]), R32(VNEW.t[0:c, h, :]), True, True, [KD, VNEW], [pS])
                    for h in range(4):
                        kb.stt(R32(Sc.t[:, h, :]), Sc.t[:, h, :], EG.t[:, h, c - 1:c], pS3[:, h, :], ALU.mult, ALU.add, [Sc, EG, pS], [Sc])
                    chunk_epilogue(po, po3, cs, c)
                    checkpoint("gdn%d" % ci, bi)
                else:
                    def ldS(s_):
                        Sb = SS[s_ % 2]
                        kb.dma("pool", R32(Sb.t[:, :, :]), sssm[s_].rearrange("h k v -> k h v"), [], [Sb], Sb.name)
                        return Sb
                    pvt = kb.ps("sm")
                    pvt3 = pvt.t[:, :].rearrange("p (h n) -> p h n", h=4)
                    for h in range(4):
                        kb.mm(pvt3[:, h, :], R32(VB.t[0:c, h, :]), R32(rr[:, h, :]), h == 0, False, [VB, RR], [pvt])
                    for s_ in range(NS):
                        Sb = ldS(s_)
                        for h in range(4):
                            fin = (s_ == NS - 1 and h == 3)
                            kb.op("pe", lambda e, h=h, s_=s_, Sb=Sb, fin=fin: e.matmul(pvt3[:, h, s_ * DS:(s_ + 1) * DS], R32(Sb.t[:, h, :]), R32(WTN.t[:, h, s_ * DS:(s_ + 1) * DS]), start=False, stop=fin),
                                  [Sb, WTN], [pvt], inc=(h == 3))
                    kb.copy("act", E1.t[:, :, :], pvt3, [pvt], [E1])
                    pvn = kb.ps("sm")
                    for h in range(4):
                        kb.tr(pvn.t[:, h * 128:(h + 1) * 128], E1.t[:, h, :], ident.t[:], [E1, ident], [pvn], inc=(h == 3))
                    kb.copy("act", R32(VNEW.t[:, :, :]), pvn.t[:, :].rearrange("p (h d) -> p h d", h=4), [pvn], [VNEW])
                    po = kb.ps("sm")
                    po3 = po.t[:, :].rearrange("p (h n) -> p h n", h=4)
                    for h in range(4):
                        kb.mm(po3[:, h, :], R32(VNEW.t[0:c, h, :]), R32(QKD.t[0:c, h, 0:c]), h == 0, False, [VNEW, QKD], [po])
                    VMs = [KBE, VB]
                    for s_ in range(NS):
                        Sb = ldS(s_)
                        for h in range(4):
                            fin = (s_ == NS - 1 and h == 3)
                            kb.op("pe", lambda e, h=h, s_=s_, Sb=Sb, fin=fin: e.matmul(po3[:, h, s_ * DS:(s_ + 1) * DS], R32(Sb.t[:, h, :]), R32(QD.t[:, h, s_ * DS:(s_ + 1) * DS]), start=False, stop=fin),
                                  [Sb, QD], [po], inc=(h == 3))
                        VM = VMs[s_ % 2]
                        kb.ts("dve", R32(VM.t[:, :, :]), VNEW.t[:, :, :], UT[1].t[:, s_:s_ + 1], None, ALU.mult, None, [VNEW, UT[1]], [VM])
                        pS = kb.ps("big")
                        pS3 = pS.t[:, :].rearrange("p (h d) -> p h d", h=4)
                        for h in range(4):
                            kb.mm(pS3[:, h, :], R32(KD.t[0:c, h, :]), R32(VM.t[:, h, :]), True, True, [KD, VM], [pS])
                        for h in range(4):
                            lc = s_ * DS + DS - 1
                            kb.stt(R32(Sb.t[:, h, :]), Sb.t[:, h, :], EG.t[:, h, lc:lc + 1], pS3[:, h, :], ALU.mult, ALU.add, [Sb, EG, pS], [Sb])
                        kb.dma("sp", ssms[s_].rearrange("h k v -> k h v"), Sb.t[:, :, :], [Sb], [], Sb.name)
                    chunk_epilogue(po, po3, cs, c)
            if last_prompt:
                kb.dma("sp", ssmp.rearrange("h k v -> k h v"), S.t[:, :, :], [S], [], "S")

            checkpoint("gdn", bi)
            checkpoint("epi", bi)
            MT = CV
            MR = lambda m: (OA, m) if m < 4 else (OB, m - 4)
            MRv = lambda m: (OA.t[:, m] if m < 4 else OB.t[:, m - 4])
            for m in range(8):
                pA = kb.ps("big")
                for k in range(4):
                    kb.mm(pA.t[:, 0:T], R32(va[:, k, m * 128:(m + 1) * 128]), R32(OA.t[:, k, 0:T]), k == 0, k == 3, [(OA, k), sta], [pA])
                kb.copy("act", MT.t[:, m, 0:T], pA.t[:, 0:T], [pA], [(MT, m)])
                pB = kb.ps("big")
                for k in range(4):
                    kb.mm(pB.t[:, 0:T], R32(vb_[:, k, m * 128:(m + 1) * 128]), R32(OB.t[:, k, 0:T]), k == 0, k == 3, [(OB, k), stb], [pB])
                bdst = (GBC if m < 4 else LBC)
                kb.copy("dve", bdst.t[:, m % 4, 0:T], pB.t[:, 0:T], [pB], [(bdst, m % 4)])
            for gsel in range(2):
                for half in range(2):
                    st, vw = stage_k8(w_in, 2568 + gsel * 1024 + half * 512, 512)
                    for mm_ in range(4):
                        m = half * 4 + mm_
                        p = proj_chunk(st, vw, mm_)
                        kb.act(SQ.t[:, 0:T], p.t[:, 0:T], AF.Sigmoid, [p], [SQ])
                        if gsel == 0:
                            kb.tt("dve", MT.t[:, m, 0:T], MT.t[:, m, 0:T], SQ.t[:, 0:T], ALU.mult, [(MT, m), SQ], [(MT, m)])
                        else:
                            bsrc = (GBC if m < 4 else LBC)
                            kb.tt("dve", SQ.t[:, 0:T], SQ.t[:, 0:T], bsrc.t[:, m % 4, 0:T], ALU.mult, [SQ, (bsrc, m % 4)], [SQ])
                            kb.tt("dve", R32(MRv(m)[:, 0:T]), MT.t[:, m, 0:T], SQ.t[:, 0:T], ALU.add, [(MT, m), SQ], [MR(m)])

            checkpoint("merge", bi)
            for half in range(2):
                st, vw = stage_k8(w_o, half * 512, 512)
                for i in range(nt):
                    p = kb.ps("big")
                    for k in range(8):
                        kb.mm(p.t[:, :], R32(MRv(k)[:, i * 128:(i + 1) * 128]), R32(vw[:, k, :]), k == 0, k == 7, [MR(k), st], [p])
                    kb.tt("dve", X.t[:, i, half * 512:(half + 1) * 512], X.t[:, i, half * 512:(half + 1) * 512], p.t[:, :], ALU.add, [(X, i), p], [(X, i)])

            checkpoint("wo", bi)
            norm_to_hT(gmlp_c)
            def mlp_up(fg, stu, vu):
                for f in range(4):
                    cf = (fg % 2) * 4 + f
                    p = proj_chunk(stu, vu, f)
                    kb.act(SQ.t[:, 0:T], p.t[:, 0:T], AF.Relu, [p], [SQ])
                    kb.tt("dve", R32(MRv(cf)[:, 0:T]), SQ.t[:, 0:T], SQ.t[:, 0:T], ALU.mult, [SQ], [MR(cf)])

            def mlp_down(fg, std, vd):
                for i in range(nt):
                    for half in range(2):
                        p = kb.ps("big")
                        for f in range(4):
                            cf = (fg % 2) * 4 + f
                            kb.mm(p.t[:, :], R32(MRv(cf)[:, i * 128:(i + 1) * 128]), R32(vd[:, f, half * 512:(half + 1) * 512]), f == 0, f == 3, [MR(cf), std], [p])
                        kb.tt("dve", X.t[:, i, half * 512:(half + 1) * 512], X.t[:, i, half * 512:(half + 1) * 512], p.t[:, :], ALU.add, [(X, i), p], [(X, i)])

            stu, vu = stage_k8(w_up, 0, 512)
            mlp_up(0, stu, vu)
            for fg in range(8):
                if fg + 1 < 8:
                    stu, vu = stage_k8(w_up, (fg + 1) * 512, 512)
                    mlp_up(fg + 1, stu, vu)
                std, vd = stage_k4(w_down, fg * 512)
                mlp_down(fg, std, vd)

            checkpoint("mlp", bi)
            for i in range(nt):
                kb.act(XNv, X.t[:, i, :], AF.Square, [(X, i)], XNk + [(ssq, i)], accum_out=ssq.t[:, i:i + 1])
                kb.act(lntmp.t[:, i:i + 1], ssq.t[:, i:i + 1], AF.Ln, [(ssq, i), epscol], [(lntmp, i)], bias=epscol.t[:, :], scale=1.0 / D)
                kb.act(rstd.t[:, i:i + 1], lntmp.t[:, i:i + 1], AF.Exp, [(lntmp, i)], [(rstd, i)], scale=-0.5)
                yv = CV.t[:, 2 * i:2 * i + 2, :].rearrange("p a b -> p (a b)")
                yk = [(CV, 2 * i), (CV, 2 * i + 1)]
                kb.stt(yv, X.t[:, i, :], rstd.t[:, i:i + 1], gfin.t[:, :], ALU.mult, ALU.mult, [(X, i), (rstd, i), gfin], yk)
                kb.dma("sp", ydst[i * 128:(i + 1) * 128, :], yv, yk, [], "Y%d" % i)

        try:
            if not stopped:
                run_all()
        except StopBuild:
            pass
        kb.finish()
        kb.emit()
    return nc


_NC = None


def kernel(x_prompt, x_sample, state_conv, state_pool, state_ssm, w_in, w_conv, a_log, dt_bias, w_onorm,
           w_pool_mix, pool_scale, w_a_out, w_b_out, w_o, g_attn, g_mlp, w_up, w_down, g_final):
    global _NC
    f = lambda a: np.ascontiguousarray(np.asarray(a, dtype=np.float32))
    if _NC is None:
        _NC = build_program()
    nc = _NC
    ncores = 8
    shared = {
        "w_in": f(w_in[0]), "w_conv": f(w_conv[0]), "a_log": f(a_log[0]).reshape(NH, 1), "dt_bias": f(dt_bias[0]).reshape(NH, 1),
        "w_onorm": f(w_onorm[0]).reshape(128, 1), "w_mix": f(w_pool_mix[0]), "pool_scale": f(pool_scale[0]).reshape(512, 1),
        "w_a": f(w_a_out[0]), "w_b": f(w_b_out[0]), "w_o": f(w_o[0]), "g_attn": f(g_attn[0]).reshape(D, 1),
        "g_mlp": f(g_mlp[0]).reshape(D, 1), "w_up": f(w_up[0]), "w_down": f(w_down[0]), "g_final": f(g_final).reshape(1, D),
    }
    in_maps = []
    for ci in range(ncores):
        sl = slice(ci * NS, (ci + 1) * NS)
        m = dict(shared)
        m["xp"] = f(x_prompt[ci])
        m["xs"] = f(x_sample[sl]).reshape(NS * DS, D)
        m["sconv"] = f(state_conv[0, sl]).reshape(NS * 3, QKVW)
        m["spool"] = f(state_pool[0, sl]).reshape(NS * 15, 512)
        m["sssm"] = f(state_ssm[0, sl])
        in_maps.append(m)
    res = run_bass_kernel_spmd(nc, in_maps, core_ids=list(range(ncores)))
    r = res.results
    y_prompt = np.stack([r[i]["yp"] for i in range(ncores)], 0)
    y_sample = np.concatenate([r[i]["ys"].reshape(NS, DS, D) for i in range(ncores)], 0)
    conv_p = np.stack([r[i]["convp"] for i in range(ncores)], 0)[None]
    pool_p = np.stack([r[i]["poolp"] for i in range(ncores)], 0)[None]
    ssm_p = np.stack([r[i]["ssmp"] for i in range(ncores)], 0)[None]
    conv_s = np.concatenate([r[i]["convs"].reshape(NS, 3, QKVW) for i in range(ncores)], 0)[None]
    pool_s = np.concatenate([r[i]["pools"].reshape(NS, 15, 512) for i in range(ncores)], 0)[None]
    ssm_s = np.concatenate([r[i]["ssms"] for i in range(ncores)], 0)[None]
    return (y_prompt.astype(np.float32), y_sample.astype(np.float32), conv_p.astype(np.float32), pool_p.astype(np.float32),
            ssm_p.astype(np.float32), conv_s.astype(np.float32), pool_s.astype(np.float32), ssm_s.astype(np.float32))
```

```python
import math
from contextlib import ExitStack

import numpy as np
import concourse.bass as bass
import concourse.mybir as mybir
from concourse.bass_utils import run_bass_kernel_spmd

F32 = mybir.dt.float32
F32R = mybir.dt.float32r
AF = mybir.ActivationFunctionType
ALU = mybir.AluOpType

D = 1024
NH = 4
QKVW = 1536
INW = 4616
DFF = 4096
EPS = 1e-6
SEQ = 2048
NS = 16
DS = 8
TB = 512
CP = 128
NEG = -1.0e30
NSTAGE = 3
FORCE_DVE_EVAC = True


class Trk:
    __slots__ = ("w", "r")

    def __init__(self):
        self.w = None
        self.r = []


class Tl:
    def __init__(self, name, t):
        self.name = name
        self.t = t
        self.tr = {None: Trk()}

    def trackers(self, key):
        if key is None:
            return list(self.tr.values())
        if key not in self.tr:
            self.tr[key] = Trk()
        return [self.tr[key], self.tr[None]]

    def tracker(self, key):
        if key not in self.tr:
            self.tr[key] = Trk()
        return self.tr[key]


def _norm(acc):
    out = []
    for a in acc:
        if isinstance(a, Tl):
            out.append((a, None))
        else:
            out.append(a)
    return out


class KB:
    ENG = ("pe", "act", "dve", "pool", "sp")

    def __init__(self, nc, stack):
        self.nc = nc
        self.stack = stack
        self.ops = {e: [] for e in self.ENG}
        self.sem = {e: stack.enter_context(nc.semaphore("s_" + e)) for e in self.ENG}
        self.cnt = {e: 0 for e in self.ENG}
        self.waited = {e: {} for e in self.ENG}
        self.dsem = {}
        self.semobj = {}
        for e in self.ENG:
            self.semobj[self.sem[e].name] = self.sem[e]
        self.ps_pools = {}
        self.ps_idx = {}
        self.n_sb = 0
        self.needed = set()

    def sb(self, name, shape, dtype=F32):
        t = self.stack.enter_context(self.nc.sbuf_tensor(name, list(shape), dtype))
        return Tl(name, t)

    def psum(self, name, shape):
        t = self.stack.enter_context(self.nc.psum_tensor(name, list(shape), F32))
        return Tl(name, t)

    def mkpool(self, pname, n):
        self.ps_pools[pname] = [self.psum("%s%d" % (pname, i), [128, 512]) for i in range(n)]
        self.ps_idx[pname] = 0

    def ps(self, pname):
        i = self.ps_idx[pname]
        self.ps_idx[pname] = (i + 1) % len(self.ps_pools[pname])
        return self.ps_pools[pname][i]

    def _deps(self, reads, writes):
        evs = []
        for tl, key in reads:
            for trk in tl.trackers(key):
                if trk.w is not None:
                    evs.append(trk.w)
        for tl, key in writes:
            for trk in tl.trackers(key):
                if trk.w is not None:
                    evs.append(trk.w)
                evs.extend(trk.r)
        return evs

    def _commit(self, ev, reads, writes):
        for tl, key in reads:
            tl.tracker(key).r.append(ev)
        for tl, key in writes:
            if key is None:
                for trk in tl.tr.values():
                    trk.w = ev
                    trk.r = []
            else:
                trk = tl.tracker(key)
                trk.w = ev
                trk.r = []

    def _waits(self, eng, evs):
        need = {}
        for semname, val in evs:
            if eng == "pe" and semname == self.sem["pe"].name:
                continue
            if self.waited[eng].get(semname, 0) >= val:
                continue
            if need.get(semname, 0) < val:
                need[semname] = val
        for k, v in need.items():
            self.waited[eng][k] = v
            self.needed.add((k, v))
        return [(self.semobj[k], v) for k, v in need.items()]

    def op(self, eng, fn, reads=(), writes=(), inc=True):
        reads = _norm(reads)
        writes = _norm(writes)
        waits = self._waits(eng, self._deps(reads, writes))
        ev = (self.sem[eng].name, self.cnt[eng] + 1)
        if inc:
            self.cnt[eng] += 1
        self.ops[eng].append((waits, fn, self.sem[eng] if inc else None, 1))
        self._commit(ev, reads, writes)

    def dma(self, q, out_ap, in_ap, reads, writes, semkey, slow=False):
        reads = _norm(reads)
        writes = _norm(writes)
        waits = self._waits(q, self._deps(reads, writes))
        if semkey not in self.dsem:
            s = self.stack.enter_context(self.nc.semaphore("d%d" % len(self.dsem)))
            self.dsem[semkey] = [s, 0]
            self.semobj[s.name] = s
        ent = self.dsem[semkey]
        ent[1] += 16
        ev = (ent[0].name, ent[1])
        if slow:
            self.ops[q].append((waits, lambda e: e.dma_start(out=out_ap, in_=in_ap, allow_slow_non_contiguous=True), ent[0], 16))
        else:
            self.ops[q].append((waits, lambda e: e.dma_start(out=out_ap, in_=in_ap), ent[0], 16))
        self._commit(ev, reads, writes)

    def finish(self):
        waits = []
        for semkey, (s, v) in self.dsem.items():
            if self.waited["sp"].get(s.name, 0) < v:
                waits.append((s, v))
        self.ops["sp"].append((waits, None, None, 0))

    def emit(self):
        nc = self.nc
        kb = self

        eng_sems = {kb.sem[e_].name: e_ for e_ in kb.ENG}
        remap = {}
        for semname, e_ in eng_sems.items():
            vals = sorted(v for (k, v) in kb.needed if k == semname)
            remap[semname] = {v: i + 1 for i, v in enumerate(vals)}

        def run(e, name):
            cnt = 0
            for waits, fn, sem, incv in kb.ops[name]:
                for s, v in waits:
                    if s.name in remap:
                        e.wait_ge(s, remap[s.name][v])
                    else:
                        e.wait_ge(s, v)
                if fn is None:
                    continue
                ins = fn(e)
                if sem is not None:
                    if sem.name in remap:
                        cnt += 1
                        if cnt in remap[sem.name]:
                            ins.then_inc(sem, 1)
                    else:
                        ins.then_inc(sem, incv)

        with nc.Block() as block:
            @block.sync
            def _(e):
                run(e, "sp")

            @block.tensor
            def _(e):
                run(e, "pe")

            @block.scalar
            def _(e):
                run(e, "act")

            @block.vector
            def _(e):
                run(e, "dve")

            @block.gpsimd
            def _(e):
                run(e, "pool")

    def mm(self, out, lhsT, rhs, start, stop, reads, writes):
        self.op("pe", lambda e: e.matmul(out, lhsT, rhs, start=start, stop=stop), reads, writes, inc=stop)

    def tr(self, out, in_, ident, reads, writes, inc=True):
        self.op("pe", lambda e: e.transpose(out, in_, ident), reads, writes, inc=inc)

    def act(self, out, in_, func, reads, writes, bias=None, scale=None, accum_out=None):
        kw = {}
        if bias is not None:
            kw["bias"] = bias
        if scale is not None:
            kw["scale"] = scale
        if accum_out is not None:
            kw["accum_out"] = accum_out
        self.op("act", lambda e: e.activation(out=out, in_=in_, func=func, **kw), reads, writes)

    def ts(self, eng, out, in0, s1, s2, op0, op1, reads, writes):
        if op1 is None:
            self.op(eng, lambda e: e.tensor_scalar(out, in0, s1, None, op0), reads, writes)
        else:
            self.op(eng, lambda e: e.tensor_scalar(out, in0, s1, s2, op0, op1), reads, writes)

    def tt(self, eng, out, in0, in1, op, reads, writes):
        self.op(eng, lambda e: e.tensor_tensor(out, in0, in1, op), reads, writes)

    def stt(self, out, in0, scalar, in1, op0, op1, reads, writes):
        self.op("dve", lambda e: e.scalar_tensor_tensor(out, in0, scalar, in1, op0, op1), reads, writes)

    def copy(self, eng, out, in_, reads, writes):
        if eng == "act":
            self.op("act", lambda e: e.copy(out, in_), reads, writes)
        else:
            self.op(eng, lambda e: e.tensor_copy(out, in_), reads, writes)

    def memset(self, eng, ap, val, writes):
        self.op(eng, lambda e: e.memset(ap, val), (), writes)


class StopBuild(Exception):
    pass


def build_program(stop=None, stop_block=0, dumps=()):
    nc = bass.Bass("TRN2", target_bir_lowering=False)

    def din(name, shape):
        return nc.dram_tensor(name, list(shape), F32, kind="ExternalInput").ap()

    def dout(name, shape):
        return nc.dram_tensor(name, list(shape), F32, kind="ExternalOutput").ap()

    xp = din("xp", [SEQ, D])
    xs = din("xs", [NS * DS, D])
    sconv = din("sconv", [NS * 3, QKVW])
    spool = din("spool", [NS * 15, 512])
    sssm = din("sssm", [NS, NH, 128, 128])
    w_in = din("w_in", [D, INW])
    w_conv = din("w_conv", [4, QKVW])
    a_log = din("a_log", [NH, 1])
    dt_bias = din("dt_bias", [NH, 1])
    w_onorm = din("w_onorm", [128, 1])
    w_mix = din("w_mix", [4, 128, 128])
    pool_scale = din("pool_scale", [512, 1])
    w_a = din("w_a", [512, D])
    w_b = din("w_b", [512, D])
    w_o = din("w_o", [D, D])
    g_attn = din("g_attn", [D, 1])
    g_mlp = din("g_mlp", [D, 1])
    w_up = din("w_up", [D, DFF])
    w_down = din("w_down", [DFF, D])
    g_final = din("g_final", [1, D])

    yp = dout("yp", [SEQ, D])
    ys = dout("ys", [NS * DS, D])
    convp = dout("convp", [3, QKVW])
    poolp = dout("poolp", [15, 512])
    ssmp = dout("ssmp", [NH, 128, 128])
    convs = dout("convs", [NS * 3, QKVW])
    pools = dout("pools", [NS * 15, 512])
    ssms = dout("ssms", [NS, NH, 128, 128])

    with ExitStack() as stack:
        kb = KB(nc, stack)
        TILES = {}
        def sb(name, shape, dtype=F32):
            tl = kb.sb(name, shape, dtype)
            TILES[name] = tl
            return tl
        OUT = Tl("OUT", None)

        X = sb("X", [128, 4, D])
        hT = sb("hT", [128, 8, TB])
        STG = [sb("stg%d" % i, [128, 4096]) for i in range(2)]
        PRE = [sb("pre%d" % i, [128, TB + 3]) for i in range(2)]
        HIST = sb("hist", [128, 12, 3])
        CV = sb("cv", [128, 12, TB])
        PT = sb("pT", [128, 4, TB + 15])
        PL = sb("pooled", [128, 4, TB])
        OB = sb("obT", [128, 4, TB])
        OA = sb("oaT", [128, 4, TB])
        SQ = sb("sq", [128, TB])
        SQR = sb("sqr", [128, TB])
        RS = sb("rsb", [128, TB])
        GBC = sb("gbc", [128, 4, TB])
        LBC = sb("lbc", [128, 4, TB])
        S = sb("S", [128, 4, 128])
        SS = [sb("Ss%d" % i, [128, 4, 128]) for i in range(2)]
        ssq = sb("ssq", [128, 4])
        rstd = sb("rstd", [128, 4])
        ident = sb("ident", [128, 128])
        ones = sb("ones", [128, 128])
        c128 = sb("c128", [128, 128])
        negup = sb("negup", [CP, CP])
        negcz = sb("negcz", [CP, CP])
        poslo = sb("poslo", [CP, CP])
        selg = sb("selg", [8, 4, 128])
        sell = sb("sell", [8, 4, 128])
        mask_p = sb("mask_p", [8, TB])
        mask_s = sb("mask_s", [8, NS * DS])
        gattn_c = sb("gattn_c", [128, 8])
        gmlp_c = sb("gmlp_c", [128, 8])
        gfin = sb("gfin", [128, D])
        wconv_c = sb("wconv_c", [128, 4, 12])
        pscale_c = sb("pscale_c", [128, 4])
        onorm_c = sb("onorm_c", [128, 1])
        wmix = sb("wmix", [128, 4, 128])
        biascol = sb("biascol", [8, 1])
        scalecol = sb("scalecol", [8, 1])
        mulcol = sb("mulcol", [8, 1])
        alog_t = sb("alog_t", [8, 1])
        WBA = sb("wba", [128, 8, 8])
        invcnt = sb("invcnt", [128, 4, 16])
        epscol = sb("epscol", [128, 1])
        eps128col = sb("eps128col", [128, 1])
        onecol = sb("onecol", [128, 1])
        lntmp = sb("lntmp", [128, 4])
        RC = sb("rc", [8, TB])
        COL = sb("col", [CP, 24])
        KBE = sb("kbe", [CP, 4, 128])
        KD = sb("kd", [CP, 4, 128])
        VB = sb("vb", [CP, 4, 128])
        E1 = sb("e1", [CP, 4, CP])
        E2 = sb("e2", [CP, 4, CP])
        E3 = sb("e3", [CP, 4, CP])
        QKD = sb("qkd", [CP, 4, CP])
        PRt = sb("prt", [CP, 4, 2 * CP])
        PTA = [sb("pta%d" % i, [CP, 4, CP]) for i in range(2)]
        WTN = sb("wtn", [128, 4, CP])
        VNEW = sb("vnew", [CP, 4, 128])
        EG = sb("eg", [128, 4, CP])
        QD = sb("qd", [128, 4, CP])
        STP = [sb("stp%d" % i, [128, 512]) for i in range(2)]
        UTT = [sb("utt%d" % i, [NS, 128]) for i in range(2)]
        UT = [sb("ut%d" % i, [128, NS]) for i in range(2)]
        LSEL = sb("lsel", [128, 128])
        MADD = sb("madd", [128, 128])

        class _V:
            pass
        STC = _V()
        STC.t = PL.t[0:48, :, :].rearrange("p g t -> p (g t)")[:, 0:QKVW]
        XNv = CV.t[:, 10:12, :].rearrange("p a b -> p (a b)")
        XNk = [(CV, 10), (CV, 11)]
        ROW1 = _V(); ROW1.t = SQ.t[0:8, :]
        ROW2 = _V(); ROW2.t = RS.t[0:8, :]
        dbg_aps = {}

        def checkpoint(label, bi_):
            if stop is None or label != stop or bi_ != stop_block:
                return
            for nm in dumps:
                tl = TILES[nm]
                shp = list(tl.t.shape)
                d = nc.dram_tensor("dbg_" + nm, shp, F32, kind="ExternalOutput").ap()
                kb.dma("sp", d, tl.t[:], [tl], [], "dbg_" + nm)
            raise StopBuild()

        kb.mkpool("big", 4)
        kb.mkpool("sm", 4)

        R32 = lambda ap: ap.bitcast(F32R)

        kb.memset("pool", CV.t[:, 0, :], 1.0, [CV])
        kb.copy("dve", R32(ones.t[:]), CV.t[:, 0, 0:128], [CV], [ones])
        kb.ts("dve", R32(c128.t[:]), CV.t[:, 0, 0:128], 128.0, None, ALU.mult, None, [CV], [c128])
        kb.op("pool", lambda e: e.affine_select(ident.t[:], CV.t[:, 0, 0:128], [[1, 128]], ALU.is_equal, 0.0, base=0, channel_multiplier=-1), [CV], [ident])
        kb.memset("pool", CV.t[:, 1, 0:CP], 0.0, [CV])
        zsrc = CV.t[0:CP, 1, 0:CP]
        kb.op("pool", lambda e: e.affine_select(negup.t[:], zsrc, [[1, CP]], ALU.is_gt, NEG, base=0, channel_multiplier=-1), [CV], [negup])
        kb.op("pool", lambda e: e.affine_select(negcz.t[:], zsrc, [[1, CP]], ALU.is_ge, NEG, base=0, channel_multiplier=-1), [CV], [negcz])
        kb.op("pool", lambda e: e.affine_select(poslo.t[:], zsrc, [[-1, CP]], ALU.is_gt, -NEG, base=0, channel_multiplier=1), [CV], [poslo])
        o8v = CV.t[0:8, 0, :].rearrange("p (h m) -> p h m", h=4)
        kb.op("pool", lambda e: e.affine_select(selg.t[:], o8v, [[-1, 4], [0, 128]], ALU.is_equal, 0.0, base=0, channel_multiplier=1), [CV], [selg])
        kb.op("pool", lambda e: e.affine_select(sell.t[:], selg.t[:], [[-1, 4], [0, 128]], ALU.not_equal, 1.0, base=-4, channel_multiplier=1), [selg], [sell])
        for mk, T, c, m1, m1t in ((mask_p, TB, CP, ROW1, SQ), (mask_s, NS * DS, DS, ROW2, RS)):
            kb.op("pool", lambda e, mk=mk, T=T, c=c, m1=m1: e.affine_select(m1.t[:, 0:T], CV.t[0:8, 0, 0:T], [[0, T // c], [1, c]], ALU.not_equal, 0.0, base=0, channel_multiplier=0), [CV], [m1t])
            kb.op("pool", lambda e, mk=mk, T=T, m1=m1: e.affine_select(mk.t[:], m1.t[:, 0:T], [[0, T]], ALU.is_ge, 0.0, base=3, channel_multiplier=-1), [m1t], [mk])
        for gi, win in enumerate((2, 4, 8, 16)):
            kb.memset("pool", invcnt.t[:, gi, :], 1.0 / win, [(invcnt, gi)])
            for t in range(win - 1):
                kb.memset("pool", invcnt.t[:, gi, t:t + 1], 1.0 / (t + 1), [(invcnt, gi)])
        def early(label):
            try:
                checkpoint(label, 0)
            except StopBuild:
                return True
            return False

        stopped = early("setup0")

        def small_load(tl, out_ap, in_ap, semkey):
            kb.dma("sp", out_ap, in_ap, [], [tl], semkey, slow=True)

        small_load(gattn_c, gattn_c.t[:], g_attn.rearrange("(c p) o -> p (c o)", p=128), "gattn")
        small_load(gmlp_c, gmlp_c.t[:], g_mlp.rearrange("(c p) o -> p (c o)", p=128), "gmlp")
        small_load(pscale_c, pscale_c.t[:], pool_scale.rearrange("(c p) o -> p (c o)", p=128), "pscale")
        small_load(onorm_c, onorm_c.t[:], w_onorm[:, :], "onorm")
        for j in range(4):
            small_load(wconv_c, wconv_c.t[:, j, :], w_conv[j:j + 1, :].rearrange("o (c p) -> p (o c)", p=128), "wconv")
        kb.dma("sp", gfin.t[:], g_final[0:1, :].to_broadcast([128, D]), [], [gfin], "gfin")
        kb.dma("sp", wmix.t[:], w_mix.rearrange("g c d -> c g d"), [], [wmix], "wmix")
        kb.memset("dve", biascol.t[:], 0.0, [biascol])
        small_load(biascol, biascol.t[0:4, :], dt_bias[:, :], "biascol")
        kb.memset("dve", scalecol.t[:], -1.0, [scalecol])
        kb.memset("dve", scalecol.t[0:4, :], 1.0, [scalecol])
        kb.memset("dve", alog_t.t[:], 0.0, [alog_t])
        small_load(alog_t, alog_t.t[0:4, :], a_log[:, :], "alog")
        kb.memset("dve", mulcol.t[:], -1.0, [mulcol])
        kb.act(mulcol.t[0:4, :], alog_t.t[0:4, :], AF.Exp, [alog_t], [mulcol])
        kb.ts("dve", mulcol.t[0:4, :], mulcol.t[0:4, :], -1.0, None, ALU.mult, None, [mulcol], [mulcol])
        kb.memset("dve", HIST.t[:], 0.0, [HIST])
        kb.memset("dve", epscol.t[:], EPS, [epscol])
        kb.memset("dve", eps128col.t[:], 128.0 * EPS, [eps128col])
        kb.memset("dve", onecol.t[:], 1.0, [onecol])
        kb.ts("dve", R32(S.t[:].rearrange("p h d -> p (h d)")), gfin.t[:, 0:512], 0.0, None, ALU.mult, None, [gfin], [S])

        for (c0, d0) in ((2052, 0), (2048, 4)):
            kb.dma("pool", R32(WBA.t[:, :, d0:d0 + 4]), w_in[:, c0:c0 + 4].rearrange("(k p) n -> p k n", p=128), [], [WBA], "wba", slow=True)
        stopped = stopped or early("setup1")
        stg_i = [0]

        class Stage:
            def __init__(self, tiles):
                self.tiles = tiles
                self.keys = list(tiles)
                self.kind = None

            def ap(self, k, c0, n):
                if len(self.tiles) == 1:
                    t = self.tiles[0].t
                    if self.kind == "k8":
                        return t[:, 0:4096].rearrange("p (k n) -> p k n", k=8)[:, k, c0:c0 + n]
                    return t[:, :].rearrange("p (k n) -> p k n", k=4)[:, k, c0:c0 + n]
                if self.kind == "k8":
                    return self.flat(k)[:, c0:c0 + n]
                return self.flat(2 * k + c0 // 512)[:, c0 % 512:c0 % 512 + n]

            def flat(self, i):
                return self.tiles[i].t[:, :, :].rearrange("p h d -> p (h d)")

        STAGES = [Stage([STG[0]]), Stage([STG[1]]), Stage([KBE, KD, VB, VNEW, QKD, WTN, QD, PTA[0]])]

        def next_stage():
            st = STAGES[stg_i[0] % NSTAGE]
            stg_i[0] += 1
            return st

        def stage_k8(w_ap, col0, ncol, extra=None):
            st = next_stage()
            st.kind = "k8"
            if len(st.tiles) == 1:
                view = st.tiles[0].t[:, 0:8 * ncol].rearrange("p (k n) -> p k n", k=8)
                kb.dma("pool", R32(view), w_ap[:, col0:col0 + ncol].rearrange("(k p) n -> p k n", p=128), [], [st.tiles[0]], st.tiles[0].name)
            else:
                for k in range(8):
                    kb.dma("pool", R32(st.flat(k)), w_ap[k * 128:(k + 1) * 128, col0:col0 + ncol], [], [st.tiles[k]], "v_" + st.tiles[k].name)
            return st, st

        def stage_k4(w_ap, row0):
            st = next_stage()
            st.kind = "k4"
            if len(st.tiles) == 1:
                view = st.tiles[0].t[:, :].rearrange("p (k n) -> p k n", k=4)
                kb.dma("pool", R32(view), w_ap[row0:row0 + 512, :].rearrange("(k p) n -> p k n", p=128), [], [st.tiles[0]], st.tiles[0].name)
            else:
                for f in range(4):
                    for hh in range(2):
                        i = 2 * f + hh
                        kb.dma("pool", R32(st.flat(i)), w_ap[row0 + f * 128:row0 + (f + 1) * 128, hh * 512:(hh + 1) * 512], [], [st.tiles[i]], "v_" + st.tiles[i].name)
            return st, st

        def chunk_epilogue(po, po3, cs, cw):
            n4 = 4 * cw
            sq3 = SQR.t[:, 0:n4].rearrange("p (h n) -> p h n", h=4)
            rs3 = RS.t[:, 0:n4].rearrange("p (h n) -> p h n", h=4)
            kb.act(R32(sq3), po3, AF.Square, [po], [SQR])
            p4 = kb.ps("big")
            kb.mm(p4.t[:, 0:n4], R32(ones.t[:, :]), R32(SQR.t[:, 0:n4]), True, True, [SQR, ones], [p4])
            kb.act(RS.t[:, 0:n4], p4.t[:, 0:n4], AF.Ln, [p4, epscol], [RS], bias=epscol.t[:, :], scale=1.0 / 128.0)
            kb.act(RS.t[:, 0:n4], RS.t[:, 0:n4], AF.Exp, [RS], [RS], scale=-0.5)
            kb.stt(rs3, rs3, onorm_c.t[:, 0:1], PL.t[:, :, cs], ALU.mult, ALU.mult, [RS, onorm_c, PL], [RS])
            kb.tt("dve", R32(OA.t[:, :, cs]), po3, rs3, ALU.mult, [po, RS], [(OA, 0), (OA, 1), (OA, 2), (OA, 3)])

        evac_flip = [0]

        def evac_eng():
            evac_flip[0] ^= 1
            return "act" if evac_flip[0] else "dve"

        blocks = [("p", i * TB) for i in range(SEQ // TB)] + [("s", 0)]
        def run_all():
          for bi, (kind, tok0) in enumerate(blocks):
            T = TB if kind == "p" else NS * DS
            nt = T // 128
            c = CP
            nch = T // c
            xsrc = xp[tok0:tok0 + T, :] if kind == "p" else xs[:, :]
            ydst = yp[tok0:tok0 + T, :] if kind == "p" else ys[:, :]
            first_prompt = kind == "p" and tok0 == 0
            last_prompt = kind == "p" and tok0 == SEQ - TB

            for i in range(nt):
                kb.dma("pool", X.t[:, i, :], xsrc[i * 128:(i + 1) * 128, :], [], [(X, i)], "X%d" % i)

            def norm_to_hT(gcol):
                for i in range(nt):
                    kb.act(XNv, X.t[:, i, :], AF.Square, [(X, i)], XNk + [(ssq, i)], accum_out=ssq.t[:, i:i + 1])
                    kb.act(lntmp.t[:, i:i + 1], ssq.t[:, i:i + 1], AF.Ln, [(ssq, i), epscol], [(lntmp, i)], bias=epscol.t[:, :], scale=1.0 / D)
                    kb.act(rstd.t[:, i:i + 1], lntmp.t[:, i:i + 1], AF.Exp, [(lntmp, i)], [(rstd, i)], scale=-0.5)
                    checkpoint("S0a", bi)
                    kb.ts("dve", XNv, X.t[:, i, :], rstd.t[:, i:i + 1], None, ALU.mult, None, [(X, i), (rstd, i)], XNk)
                    checkpoint("S0b", bi)
                    for half in range(2):
                        p = kb.ps("big")
                        for q in range(4):
                            cch = half * 4 + q
                            kb.tr(p.t[:, q * 128:(q + 1) * 128], XNv[:, cch * 128:(cch + 1) * 128], ident.t[:], XNk + [ident], [p], inc=(q == 3))
                        checkpoint("S0c", bi)
                        for q in range(4):
                            cch = half * 4 + q
                            eng = "dve" if FORCE_DVE_EVAC else evac_eng()
                            dst = R32(hT.t[:, cch, i * 128:(i + 1) * 128])
                            src = p.t[:, q * 128:(q + 1) * 128]
                            if eng == "act":
                                kb.act(dst, src, AF.Copy, [p, gcol], [(hT, cch)], scale=gcol.t[:, cch:cch + 1])
                            else:
                                kb.ts("dve", dst, src, gcol.t[:, cch:cch + 1], None, ALU.mult, None, [p, gcol], [(hT, cch)])

            norm_to_hT(gattn_c)
            checkpoint("S0", bi)

            def proj_chunk(stt_, view, m, ncols_m=128):
                p = kb.ps("big")
                for k in range(8):
                    kb.mm(p.t[:, 0:T], R32(view.ap(k, m * 128, ncols_m)), R32(hT.t[:, k, 0:T]), k == 0, k == 7,
                          [(hT, k)] + stt_.keys, [p])
                return p


            pba = kb.ps("big")
            for k in range(8):
                kb.mm(pba.t[0:8, 0:T], R32(WBA.t[:, k, :]), R32(hT.t[:, k, 0:T]), k == 0, k == 7, [(hT, k), WBA], [pba])
            kb.act(ROW1.t[:, 0:T], pba.t[0:8, 0:T], AF.Exp, [pba, biascol, scalecol], [SQ], bias=biascol.t[:, :], scale=scalecol.t[:, :])
            kb.act(ROW2.t[:, 0:T], ROW1.t[:, 0:T], AF.Ln, [SQ, onecol], [RS], bias=onecol.t[0:8, :])
            kb.ts("dve", ROW1.t[:, 0:T], ROW2.t[:, 0:T], mulcol.t[:, :], None, ALU.mult, None, [RS, mulcol], [SQ])
            mk = mask_p if kind == "p" else mask_s
            kb.op("dve", lambda e, mk=mk, T=T: e.tensor_tensor_scan(RC.t[:, 0:T], mk.t[:, 0:T], ROW1.t[:, 0:T], 0.0, ALU.mult, ALU.add), [mk, SQ], [RC])
            checkpoint("S1a", bi)
            if kind == "s":
                kb.dma("sp", STC.t[:, :], sconv[:, :], [], [PL], "STC")
            deferred = []

            if kind == "p":
                stages_ = {}

                def conv_A(j):
                    grp = j // 4
                    if grp not in stages_:
                        stages_[grp] = stage_k8(w_in, grp * 512, 512)
                    st, vw = stages_[grp]
                    p = proj_chunk(st, vw, j % 4)
                    pre = PRE[j % 2]
                    kb.copy("dve", pre.t[:, 0:3], HIST.t[:, j, :], [(HIST, j)], [pre])
                    kb.copy("act", pre.t[:, 3:3 + T], p.t[:, 0:T], [p], [pre])
                    kb.act(CV.t[:, j, 0:T], p.t[:, 0:T], AF.Identity, [p, wconv_c], [(CV, j)], scale=wconv_c.t[:, 3, j:j + 1])

                def conv_B(j):
                    pre = PRE[j % 2]
                    cvv = CV.t[:, j, 0:T]
                    for tap in (2, 1, 0):
                        kb.stt(cvv, pre.t[:, tap:tap + T], wconv_c.t[:, tap, j:j + 1], cvv, ALU.mult, ALU.add, [pre, wconv_c, (CV, j)], [(CV, j)])
                    kb.act(cvv, cvv, AF.Silu, [(CV, j)], [(CV, j)])
                    kb.copy("dve", HIST.t[:, j, :], pre.t[:, T:T + 3], [pre], [(HIST, j)])
                    if last_prompt:
                        p3 = kb.ps("sm")
                        kb.tr(p3.t[0:3, 0:128], pre.t[:, T:T + 3], ident.t[:], [pre, ident], [p3])
                        kb.copy("act", STC.t[0:3, j * 128:(j + 1) * 128], p3.t[0:3, 0:128], [p3], [PL])
                    if j < 8:
                        deferred.append((j, j // 4))

                conv_A(0)
                for j in range(12):
                    if j + 1 < 12:
                        conv_A(j + 1)
                    conv_B(j)
            else:
                stages_ = {}

                def sviews(j):
                    pre = PRE[j % 2]
                    pre3 = pre.t[:, 0:NS * 11].rearrange("p (s w) -> p s w", w=11)
                    cvv = CV.t[:, j, 0:T].rearrange("p (s w) -> p s w", w=8)
                    return pre, pre3, cvv

                def sconv_A(j):
                    grp = j // 4
                    if grp not in stages_:
                        stages_[grp] = stage_k8(w_in, grp * 512, 512)
                    st, vw = stages_[grp]
                    p = proj_chunk(st, vw, j % 4)
                    pre, pre3, cvv = sviews(j)
                    p2 = kb.ps("sm")
                    kb.tr(p2.t[:, 0:48], STC.t[:, j * 128:(j + 1) * 128], ident.t[0:48, 0:48], [(PL, ("s", j)), ident], [p2])
                    kb.copy("dve", pre3[:, :, 0:3], p2.t[:, 0:48].rearrange("p (s w) -> p s w", w=3), [p2], [pre])
                    kb.copy("act", pre3[:, :, 3:11], p.t[:, 0:T].rearrange("p (s w) -> p s w", w=8), [p], [pre])
                    kb.act(cvv, p.t[:, 0:T].rearrange("p (s w) -> p s w", w=8), AF.Identity, [p, wconv_c], [(CV, j)], scale=wconv_c.t[:, 3, j:j + 1])

                def sconv_B(j):
                    pre, pre3, cvv = sviews(j)
                    for tap in (2, 1, 0):
                        kb.stt(cvv, pre3[:, :, tap:tap + 8], wconv_c.t[:, tap, j:j + 1], cvv, ALU.mult, ALU.add, [pre, wconv_c, (CV, j)], [(CV, j)])
                    kb.act(cvv, cvv, AF.Silu, [(CV, j)], [(CV, j)])
                    p3 = kb.ps("sm")
                    tmpc = SQ.t[:, 0:48].rearrange("p (s w) -> p s w", w=3)
                    kb.copy("dve", tmpc, pre3[:, :, 8:11], [pre], [SQ])
                    kb.tr(p3.t[0:48, 0:128], SQ.t[:, 0:48], ident.t[:], [SQ, ident], [p3])
                    kb.copy("act", STC.t[:, j * 128:(j + 1) * 128], p3.t[0:48, 0:128], [p3], [(PL, ("s", j))])
                    if j < 8:
                        deferred.append((j, j // 4))

                sconv_A(0)
                for j in range(12):
                    if j + 1 < 12:
                        sconv_A(j + 1)
                    sconv_B(j)
            if last_prompt:
                kb.dma("sp", convp[:, :], STC.t[0:3, :], [PL], [], "STC")
            if kind == "s":
                kb.dma("sp", convs[:, :], STC.t[:, :], [PL], [], "STC")

            checkpoint("conv", bi)
            st, vw = stage_k8(w_in, 2056, 512)
            stz, vwz = stage_k8(w_in, 1536, 512)

            def z_chunk(m):
                pz = proj_chunk(stz, vwz, m)
                kb.act(PL.t[:, m, 0:T], pz.t[:, 0:T], AF.Silu, [pz], [(PL, m)])

            if kind == "s":
                for r in range(2):
                    nr = 128 if r == 0 else NS * 15 - 128
                    kb.dma("sp", STP[r].t[0:nr, :], spool[r * 128:r * 128 + nr, :], [], [STP[r]], STP[r].name)
            for gi, win in enumerate((2, 4, 8, 16)):
                p = proj_chunk(st, vw, gi)
                if gi >= 1:
                    z_chunk(gi - 1)
                if kind == "p":
                    if first_prompt:
                        kb.memset("dve", PT.t[:, gi, 0:15], 0.0, [(PT, gi)])
                    else:
                        kb.copy("dve", PT.t[:, gi, 0:15], PT.t[:, gi, TB:TB + 15], [(PT, gi)], [(PT, gi)])
                    kb.copy(evac_eng(), PT.t[:, gi, 15:15 + T], p.t[:, 0:T], [p], [(PT, gi)])
                    W = T + 15
                    full = lambda a, b: PT.t[:, gi, a:b]
                    newp = PT.t[:, gi, 15:15 + T]
                    plv = lambda a, b: PL.t[:, gi, a:b]
                    obv = OB.t[:, gi, 0:T]
                    cur = full
                    curoff = 0
                    if win <= 4:
                        kb.tt("dve", plv(0, T), full(15, 15 + T), full(14, 14 + T), ALU.add, [(PT, gi)], [(PL, gi)])
                        for dd in range(2, win):
                            kb.tt("dve", plv(0, T), plv(0, T), full(15 - dd, 15 - dd + T), ALU.add, [(PT, gi), (PL, gi)], [(PL, gi)])
                    else:
                        og = 3 if gi == 2 else 0
                        S2 = PT.t[:, og, 0:512]
                        HW = T // 2
                        for hf in range(2):
                            t0 = hf * HW
                            base = 15 + t0 - (win - 1)
                            L = HW + win - 1
                            kb.tt("dve", SQ.t[:, 1:L], full(base + 1, base + L), full(base, base + L - 1), ALU.add, [(PT, gi)], [SQ])
                            bufs = [(SQ.t[:, 0:512], [SQ]), (S2, [(PT, og)])]
                            si, lo, sh = 0, 1, 2
                            while sh < win:
                                sv, sk = bufs[si]
                                dv, dk = bufs[1 - si]
                                if sh * 2 >= win:
                                    kb.tt("dve", plv(t0, t0 + HW), sv[:, win - 1:L], sv[:, win - 1 - sh:L - sh], ALU.add, sk, [(PL, gi)])
                                else:
                                    lo2 = lo + sh
                                    kb.tt("dve", dv[:, lo2:L], sv[:, lo2:L], sv[:, lo2 - sh:L - sh], ALU.add, sk, dk)
                                    si, lo = 1 - si, lo2
                                sh *= 2
                    if first_prompt:
                        kb.tt("dve", plv(0, 16), plv(0, 16), invcnt.t[:, gi, :], ALU.mult, [(PL, gi), (invcnt, gi)], [(PL, gi)])
                        kb.ts("dve", plv(16, T), plv(16, T), 1.0 / win, None, ALU.mult, None, [(PL, gi)], [(PL, gi)])
                        kb.tt("dve", plv(0, T), plv(0, T), newp, ALU.subtract, [(PL, gi), (PT, gi)], [(PL, gi)])
                    else:
                        kb.stt(plv(0, T), plv(0, T), 1.0 / win, newp, ALU.mult, ALU.subtract, [(PL, gi), (PT, gi)], [(PL, gi)])
                    if last_prompt:
                        p3 = kb.ps("sm")
                        kb.tr(p3.t[0:15, 0:128], PT.t[:, gi, T:T + 15], ident.t[:], [(PT, gi), ident], [p3])
                        kb.copy("act", STP[0].t[0:15, gi * 128:(gi + 1) * 128], p3.t[0:15, 0:128], [p3], [STP[0]])
                else:
                    pt3 = PT.t[:, gi, 0:NS * 23].rearrange("p (s w) -> p s w", w=23)
                    for r in range(2):
                        nr = 128 if r == 0 else NS * 15 - 128
                        p2 = kb.ps("sm")
                        kb.tr(p2.t[:, 0:nr], STP[r].t[0:nr, gi * 128:(gi + 1) * 128], ident.t[0:nr, 0:nr], [STP[r], ident], [p2])
                        kb.copy("dve", SQ.t[:, r * 128:r * 128 + nr], p2.t[:, 0:nr], [p2], [SQ])
                    kb.copy("dve", pt3[:, :, 0:15], SQ.t[:, 0:NS * 15].rearrange("p (s w) -> p s w", w=15), [SQ], [(PT, gi)])
                    kb.copy(evac_eng(), pt3[:, :, 15:23], p.t[:, 0:T].rearrange("p (s w) -> p s w", w=8), [p], [(PT, gi)])
                    pl3 = PL.t[:, gi, 0:T].rearrange("p (s w) -> p s w", w=8)
                    newp = pt3[:, :, 15:23]
                    kb.tt("dve", pl3, pt3[:, :, 15:23], pt3[:, :, 14:22], ALU.add, [(PT, gi)], [(PL, gi)])
                    for dd in range(2, win):
                        kb.tt("dve", pl3, pl3, pt3[:, :, 15 - dd:23 - dd], ALU.add, [(PT, gi), (PL, gi)], [(PL, gi)])
                    kb.stt(pl3, pl3, 1.0 / win, newp, ALU.mult, ALU.subtract, [(PL, gi), (PT, gi)], [(PL, gi)])
                    kb.copy("dve", SQ.t[:, 0:NS * 15].rearrange("p (s w) -> p s w", w=15), pt3[:, :, 8:23], [(PT, gi)], [SQ])
                    for r in range(2):
                        nr = 128 if r == 0 else NS * 15 - 128
                        p3 = kb.ps("sm")
                        kb.tr(p3.t[0:nr, 0:128], SQ.t[:, r * 128:r * 128 + nr], ident.t[:], [SQ, ident], [p3])
                        kb.copy("act", STP[r].t[0:nr, gi * 128:(gi + 1) * 128], p3.t[0:nr, 0:128], [p3], [STP[r]])
                p5 = kb.ps("big")
                kb.mm(p5.t[:, 0:T], wmix.t[:, gi, :], PL.t[:, gi, 0:T], True, True, [wmix, (PL, gi)], [p5])
                kb.ts("dve", R32(OB.t[:, gi, 0:T]), p5.t[:, 0:T], pscale_c.t[:, gi:gi + 1], None, ALU.mult, None, [p5, pscale_c], [(OB, gi)])
            if last_prompt:
                kb.dma("sp", poolp[:, :], STP[0].t[0:15, :], [STP[0]], [], STP[0].name)
            if kind == "s":
                for r in range(2):
                    nr = 128 if r == 0 else NS * 15 - 128
                    kb.dma("sp", pools[r * 128:r * 128 + nr, :], STP[r].t[0:nr, :], [STP[r]], [], STP[r].name)

            checkpoint("pool", bi)
            z_chunk(3)

            rsb = [RS, SQ]

            def l2_square(idx):
                j, grp = deferred[idx]
                kb.act(PT.t[:, idx % 4, 0:T], CV.t[:, j, 0:T], AF.Square, [(CV, j)], [(PT, idx % 4)])

            if deferred:
                l2_square(0)
            for idx, (j, grp) in enumerate(deferred):
                if idx + 1 < len(deferred):
                    l2_square(idx + 1)
                p4 = kb.ps("big")
                cm = c128 if grp == 0 else ones
                kb.mm(p4.t[:, 0:T], cm.t[:, :], PT.t[:, idx % 4, 0:T], True, True, [(PT, idx % 4), cm], [p4])
                ec = eps128col if grp == 0 else epscol
                rb = rsb[idx % 2]
                kb.act(rb.t[:, 0:T], p4.t[:, 0:T], AF.Ln, [p4, ec], [rb], bias=ec.t[:, :])
                kb.act(rb.t[:, 0:T], rb.t[:, 0:T], AF.Exp, [rb], [rb], scale=-0.5)
                kb.tt("dve", CV.t[:, j, 0:T], CV.t[:, j, 0:T], rb.t[:, 0:T], ALU.mult, [(CV, j), rb], [(CV, j)])
            del deferred[:]
            for h in range(4):
                for (sel, dstt) in ((selg, GBC), (sell, LBC)):
                    p = kb.ps("big")
                    kb.mm(p.t[:, 0:T], sel.t[:, h, :], RC.t[:, 0:T], True, True, [sel, RC], [p])
                    kb.copy(evac_eng(), dstt.t[:, h, 0:T], p.t[:, 0:T], [p], [(dstt, h)])

            checkpoint("z", bi)
            nsteps = int(math.log2(c if kind == "p" else DS)) - 1
            if kind == "s":
                kb.memset("pool", SQ.t[:, 0:128], 1.0, [SQ])
                onesrc = SQ.t[:, 0:128]
                kb.op("pool", lambda e: e.affine_select(UTT[0].t[:], onesrc[0:NS, :], [[1, 128]], ALU.is_ge, 0.0, base=0, channel_multiplier=-DS), [SQ], [UTT[0]])
                kb.op("pool", lambda e: e.affine_select(UTT[1].t[:], UTT[0].t[:], [[-1, 128]], ALU.is_ge, 0.0, base=DS - 1, channel_multiplier=DS), [UTT[0]], [UTT[1]])
                kb.op("pool", lambda e: e.affine_select(UT[0].t[:], onesrc[:, 0:NS], [[-DS, NS]], ALU.is_ge, 0.0, base=0, channel_multiplier=1), [SQ], [UT[0]])
                kb.op("pool", lambda e: e.affine_select(UT[1].t[:], UT[0].t[:], [[DS, NS]], ALU.is_ge, 0.0, base=DS - 1, channel_multiplier=-1), [UT[0]], [UT[1]])
                kb.op("pool", lambda e: e.affine_select(LSEL.t[:], onesrc, [[-DS, NS], [0, DS]], ALU.is_equal, 0.0, base=-(DS - 1), channel_multiplier=1), [SQ], [LSEL])
                pbd = kb.ps("sm")
                kb.mm(pbd.t[:, 0:128], UTT[1].t[:, :], UTT[1].t[:, :], True, True, [UTT[1]], [pbd])
                kb.ts("dve", MADD.t[:], pbd.t[:, 0:128], -NEG, NEG, ALU.mult, ALU.add, [pbd], [MADD])
                kb.tt("dve", negup.t[:], negup.t[:], MADD.t[:], ALU.add, [negup, MADD], [negup])
                kb.tt("dve", negcz.t[:], negcz.t[:], MADD.t[:], ALU.add, [negcz, MADD], [negcz])
                kb.tt("dve", poslo.t[:], poslo.t[:], MADD.t[:], ALU.subtract, [poslo, MADD], [poslo])
            for ci in range(nch):
                cs = slice(ci * c, (ci + 1) * c)
                last = (ci + 1) * c - 1
                Sc = S
                pc = kb.ps("sm")
                kb.tr(pc.t[0:c, 0:8], RC.t[0:8, cs], ident.t[0:8, 0:8], [RC, ident], [pc])
                kb.copy("dve", COL.t[0:c, 0:8], pc.t[0:c, 0:8], [pc], [COL])
                kb.tt("dve", COL.t[0:c, 8:12], COL.t[0:c, 0:4], COL.t[0:c, 4:8], ALU.add, [COL], [COL])
                if kind == "p":
                    kb.tt("dve", COL.t[0:c, 12:16], GBC.t[0:c, :, last], COL.t[0:c, 0:4], ALU.subtract, [COL, GBC], [COL])
                else:
                    pg = kb.ps("sm")
                    kb.mm(pg.t[:, 0:4], LSEL.t[:, :], COL.t[:, 0:4], True, True, [LSEL, COL], [pg])
                    kb.tt("dve", COL.t[0:c, 12:16], pg.t[:, 0:4], COL.t[0:c, 0:4], ALU.subtract, [COL, pg], [COL])
                kb.act(COL.t[0:c, 16:24], COL.t[0:c, 8:16], AF.Exp, [COL], [COL])
                kb.act(COL.t[0:c, 4:8], COL.t[0:c, 4:8], AF.Exp, [COL], [COL])
                pk = kb.ps("sm")
                pv_ = kb.ps("sm")
                for h in range(4):
                    kb.tr(pk.t[0:c, h * 128:(h + 1) * 128], CV.t[:, 4 + h, cs], ident.t[:], [(CV, 4 + h), ident], [pk], inc=(h == 3))
                for h in range(4):
                    kb.tr(pv_.t[0:c, h * 128:(h + 1) * 128], CV.t[:, 8 + h, cs], ident.t[:], [(CV, 8 + h), ident], [pv_], inc=(h == 3))
                pk3 = pk.t[0:c, :].rearrange("p (h d) -> p h d", h=4)
                pv3 = pv_.t[0:c, :].rearrange("p (h d) -> p h d", h=4)
                pkk = kb.ps("sm")
                pqk = kb.ps("sm")
                pkk3 = pkk.t[0:c, 0:4 * c].rearrange("p (h n) -> p h n", h=4)
                pqk3 = pqk.t[0:c, 0:4 * c].rearrange("p (h n) -> p h n", h=4)
                for h in range(4):
                    kb.mm(pkk3[:, h, :], CV.t[:, 4 + h, cs], CV.t[:, 4 + h, cs], True, True, [(CV, 4 + h)], [pkk])
                for h in range(4):
                    kb.mm(pqk3[:, h, :], CV.t[:, 4 + h, cs], CV.t[:, h, cs], True, True, [(CV, 4 + h), (CV, h)], [pqk])
                e1 = E1.t[0:c, :, 0:c]
                e2 = E2.t[0:c, :, 0:c]
                e3 = E3.t[0:c, :, 0:c]
                for h in range(4):
                    kb.stt(e1[:, h, :], LBC.t[0:c, h, cs], COL.t[0:c, h:h + 1], negup.t[0:c, 0:c], ALU.subtract, ALU.add, [(LBC, h), COL, negup], [(E1, h)])
                kb.act(e1, e1, AF.Exp, [E1], [E1])
                for h in range(4):
                    kb.stt(e2[:, h, :], GBC.t[0:c, h, cs], COL.t[0:c, 8 + h:9 + h], poslo.t[0:c, 0:c], ALU.subtract, ALU.add, [(GBC, h), COL, poslo], [(E2, h)])
                kb.act(e2, e2, AF.Exp, [E2], [E2], scale=-1.0)
                for h in range(4):
                    kb.stt(e3[:, h, :], GBC.t[0:c, h, cs], COL.t[0:c, h:h + 1], negcz.t[0:c, 0:c], ALU.subtract, ALU.add, [(GBC, h), COL, negcz], [(E3, h)])
                kb.act(e3, e3, AF.Exp, [E3], [E3])
                bcol = lambda a: COL.t[0:c, a:a + 4].unsqueeze(2).to_broadcast([c, 4, 128])
                kb.tt("dve", R32(KBE.t[0:c, :, :]), pk3, bcol(16), ALU.mult, [pk, COL], [KBE])
                kb.tt("dve", R32(KD.t[0:c, :, :]), pk3, bcol(20), ALU.mult, [pk, COL], [KD])
                kb.tt("dve", R32(VB.t[0:c, :, :]), pv3, bcol(4), ALU.mult, [pv_, COL], [VB])
                kb.act(EG.t[:, :, 0:c], GBC.t[:, :, cs], AF.Exp, [GBC], [EG])
                kb.tt("dve", R32(QD.t[:, :, 0:c]), CV.t[:, 0:4, cs], EG.t[:, :, 0:c], ALU.mult, [(CV, 0), (CV, 1), (CV, 2), (CV, 3), EG], [QD])
                T0, T1 = PTA
                pa = lambda tl: tl.t[0:c, :, 0:c]
                PRv = PRt.t[0:c, :, 0:2 * c]
                Pv = PRt.t[0:c, :, 0:c]
                rr = PRt.t[0:c, :, c:2 * c]
                KP, KR = (PRt, "P"), (PRt, "R")
                kb.stt(R32(Pv), pkk3, -1.0, e1, ALU.mult, ALU.mult, [pkk, E1], [KP])
                kb.stt(R32(pa(T0)), pkk3, -1.0, e2, ALU.mult, ALU.mult, [pkk, E2], [T0])
                kb.tt("dve", R32(QKD.t[0:c, :, 0:c]), pqk3, e3, ALU.mult, [pqk, E3], [QKD])
                for h in range(4):
                    kb.copy("pool", R32(rr[:, h, :]), ident.t[0:c, 0:c], [ident], [KR])
                Tc, Tn = T0, T1
                for step in range(1, nsteps + 1):
                    lastst = step == nsteps
                    pas = [kb.ps("sm"), kb.ps("sm")]
                    pa3 = [p_.t[0:c, 0:4 * c].rearrange("p (h n) -> p h n", h=2) for p_ in pas]
                    for h in range(4):
                        kb.mm(pa3[h // 2][:, h % 2, :], R32(pa(Tc)[:, h, :]), R32(PRv[:, h, :]), True, True, [Tc, KP, KR], [pas[h // 2]])
                    pq = kb.ps("sm")
                    pq3 = pq.t[0:c, 0:4 * c].rearrange("p (h n) -> p h n", h=4)
                    for h in range(4):
                        kb.mm(pq3[:, h, :], R32(Pv[:, h, :]), R32(pa(Tc)[:, h, :]), True, True, [Tc, KP], [pq])
                    for hf in range(2):
                        kb.tt("dve", R32(rr[:, 2 * hf:2 * hf + 2, :]), rr[:, 2 * hf:2 * hf + 2, :], pa3[hf][:, :, c:2 * c], ALU.add, [KR, pas[hf]], [KR])
                        if not lastst:
                            kb.copy("act", R32(Pv[:, 2 * hf:2 * hf + 2, :]), pa3[hf][:, :, 0:c], [pas[hf]], [KP])
                    kb.copy("act" if lastst else "dve", R32(pa(Tn)), pq3, [pq], [Tn])
                    Tc, Tn = Tn, Tc
                pr = kb.ps("sm")
                pr3 = pr.t[0:c, 0:4 * c].rearrange("p (h n) -> p h n", h=4)
                for h in range(4):
                    kb.mm(pr3[:, h, :], R32(pa(Tc)[:, h, :]), R32(rr[:, h, :]), True, True, [Tc, KR], [pr])
                kb.tt("dve", R32(rr), rr, pr3, ALU.add, [KR, pr], [KR])
                RR = KR
                pw = kb.ps("sm")
                pw3 = pw.t[:, 0:4 * c].rearrange("p (h n) -> p h n", h=4)
                for h in range(4):
                    kb.mm(pw3[:, h, :], R32(KBE.t[0:c, h, :]), R32(rr[:, h, :]), True, True, [KBE, RR], [pw])
                kb.ts("dve", R32(WTN.t[:, :, 0:c]), pw3, -1.0, None, ALU.mult, None, [pw], [WTN])
                if kind == "p":
                    pn = kb.ps("sm")
                    pn3 = pn.t[0:c, :].rearrange("p (h d) -> p h d", h=4)
                    for h in range(4):
                        kb.mm(pn3[:, h, :], R32(rr[:, h, :]), R32(VB.t[0:c, h, :]), True, False, [RR, VB], [pn])
                        kb.mm(pn3[:, h, :], R32(WTN.t[:, h, 0:c]), R32(Sc.t[:, h, :]), False, True, [WTN, Sc], [pn])
                    kb.copy("act", R32(VNEW.t[0:c, :, :]), pn3, [pn], [VNEW])
                    po = kb.ps("sm")
                    po3 = po.t[:, 0:4 * c].rearrange("p (h n) -> p h n", h=4)
                    for h in range(4):
                        kb.mm(po3[:, h, :], R32(Sc.t[:, h, :]), R32(QD.t[:, h, 0:c]), True, False, [Sc, QD], [po])
                        kb.mm(po3[:, h, :], R32(VNEW.t[0:c, h, :]), R32(QKD.t[0:c, h, 0:c]), False, True, [VNEW, QKD], [po])
                    pS = kb.ps("sm")
                    pS3 = pS.t[:, :].rearrange("p (h d) -> p h d", h=4)
                    for h in range(4):
                        kb.mm(pS3[:, h, :], R32(KD.t[0:c, h, :]), R32(VNEW.t[0:c, h, :]), True, True, [KD, VNEW], [pS])
                    for h in range(4):
                        kb.stt(R32(Sc.t[:, h, :]), Sc.t[:, h, :], EG.t[:, h, c - 1:c], pS3[:, h, :], ALU.mult, ALU.add, [Sc, EG, pS], [Sc])
                    chunk_epilogue(po, po3, cs, c)
                    checkpoint("gdn%d" % ci, bi)
                else:
                    def ldS(s_):
                        Sb = SS[s_ % 2]
                        kb.dma("pool", R32(Sb.t[:, :, :]), sssm[s_].rearrange("h k v -> k h v"), [], [Sb], Sb.name)
                        return Sb
                    pvt = kb.ps("sm")
                    pvt3 = pvt.t[:, :].rearrange("p (h n) -> p h n", h=4)
                    for h in range(4):
                        kb.mm(pvt3[:, h, :], R32(VB.t[0:c, h, :]), R32(rr[:, h, :]), h == 0, False, [VB, RR], [pvt])
                    for s_ in range(NS):
                        Sb = ldS(s_)
                        for h in range(4):
                            fin = (s_ == NS - 1 and h == 3)
                            kb.op("pe", lambda e, h=h, s_=s_, Sb=Sb, fin=fin: e.matmul(pvt3[:, h, s_ * DS:(s_ + 1) * DS], R32(Sb.t[:, h, :]), R32(WTN.t[:, h, s_ * DS:(s_ + 1) * DS]), start=False, stop=fin),
                                  [Sb, WTN], [pvt], inc=(h == 3))
                    kb.copy("act", E1.t[:, :, :], pvt3, [pvt], [E1])
                    pvn = kb.ps("sm")
                    for h in range(4):
                        kb.tr(pvn.t[:, h * 128:(h + 1) * 128], E1.t[:, h, :], ident.t[:], [E1, ident], [pvn], inc=(h == 3))
                    kb.copy("act", R32(VNEW.t[:, :, :]), pvn.t[:, :].rearrange("p (h d) -> p h d", h=4), [pvn], [VNEW])
                    po = kb.ps("sm")
                    po3 = po.t[:, :].rearrange("p (h n) -> p h n", h=4)
                    for h in range(4):
                        kb.mm(po3[:, h, :], R32(VNEW.t[0:c, h, :]), R32(QKD.t[0:c, h, 0:c]), h == 0, False, [VNEW, QKD], [po])
                    VMs = [KBE, VB]
                    for s_ in range(NS):
                        Sb = ldS(s_)
                        for h in range(4):
                            fin = (s_ == NS - 1 and h == 3)
                            kb.op("pe", lambda e, h=h, s_=s_, Sb=Sb, fin=fin: e.matmul(po3[:, h, s_ * DS:(s_ + 1) * DS], R32(Sb.t[:, h, :]), R32(QD.t[:, h, s_ * DS:(s_ + 1) * DS]), start=False, stop=fin),
                                  [Sb, QD], [po], inc=(h == 3))
                        VM = VMs[s_ % 2]
                        kb.ts("dve", R32(VM.t[:, :, :]), VNEW.t[:, :, :], UT[1].t[:, s_:s_ + 1], None, ALU.mult, None, [VNEW, UT[1]], [VM])
                        pS = kb.ps("big")
                        pS3 = pS.t[:, :].rearrange("p (h d) -> p h d", h=4)
                        for h in range(4):
                            kb.mm(pS3[:, h, :], R32(KD.t[0:c, h, :]), R32(VM.t[:, h, :]), True, True, [KD, VM], [pS])
                        for h in range(4):
                            lc = s_ * DS + DS - 1
                            kb.stt(R32(Sb.t[:, h, :]), Sb.t[:, h, :], EG.t[:, h, lc:lc + 1], pS3[:, h, :], ALU.mult, ALU.add, [Sb, EG, pS], [Sb])
                        kb.dma("sp", ssms[s_].rearrange("h k v -> k h v"), Sb.t[:, :, :], [Sb], [], Sb.name)
                    chunk_epilogue(po, po3, cs, c)
            if last_prompt:
                kb.dma("sp", ssmp.rearrange("h k v -> k h v"), S.t[:, :, :], [S], [], "S")

            checkpoint("gdn", bi)
            checkpoint("epi", bi)
            sta, va = stage_k4(w_a, 0)
            stb, vb_ = stage_k4(w_b, 0)
            MT = CV
            MR = lambda m: (OA, m) if m < 4 else (OB, m - 4)
            MRv = lambda m: (OA.t[:, m] if m < 4 else OB.t[:, m - 4])
            for m in range(8):
                pA = kb.ps("big")
                for k in range(4):
                    kb.mm(pA.t[:, 0:T], R32(va.ap(k, m * 128, 128)), R32(OA.t[:, k, 0:T]), k == 0, k == 3, [(OA, k)] + sta.keys, [pA])
                kb.copy("act", MT.t[:, m, 0:T], pA.t[:, 0:T], [pA], [(MT, m)])
                pB = kb.ps("big")
                for k in range(4):
                    kb.mm(pB.t[:, 0:T], R32(vb_.ap(k, m * 128, 128)), R32(OB.t[:, k, 0:T]), k == 0, k == 3, [(OB, k)] + stb.keys, [pB])
                bdst = (GBC if m < 4 else LBC)
                kb.copy("dve", bdst.t[:, m % 4, 0:T], pB.t[:, 0:T], [pB], [(bdst, m % 4)])
            for gsel in range(2):
                for half in range(2):
                    st, vw = stage_k8(w_in, 2568 + gsel * 1024 + half * 512, 512)
                    for mm_ in range(4):
                        m = half * 4 + mm_
                        p = proj_chunk(st, vw, mm_)
                        kb.act(SQ.t[:, 0:T], p.t[:, 0:T], AF.Sigmoid, [p], [SQ])
                        if gsel == 0:
                            kb.tt("dve", MT.t[:, m, 0:T], MT.t[:, m, 0:T], SQ.t[:, 0:T], ALU.mult, [(MT, m), SQ], [(MT, m)])
                        else:
                            bsrc = (GBC if m < 4 else LBC)
                            kb.tt("dve", SQ.t[:, 0:T], SQ.t[:, 0:T], bsrc.t[:, m % 4, 0:T], ALU.mult, [SQ, (bsrc, m % 4)], [SQ])
                            kb.tt("dve", R32(MRv(m)[:, 0:T]), MT.t[:, m, 0:T], SQ.t[:, 0:T], ALU.add, [(MT, m), SQ], [MR(m)])

            checkpoint("merge", bi)
            for half in range(2):
                st, vw = stage_k8(w_o, half * 512, 512)
                for i in range(nt):
                    p = kb.ps("big")
                    for k in range(8):
                        kb.mm(p.t[:, :], R32(MRv(k)[:, i * 128:(i + 1) * 128]), R32(vw.ap(k, 0, 512)), k == 0, k == 7, [MR(k)] + st.keys, [p])
                    kb.tt("dve", X.t[:, i, half * 512:(half + 1) * 512], X.t[:, i, half * 512:(half + 1) * 512], p.t[:, :], ALU.add, [(X, i), p], [(X, i)])

            checkpoint("wo", bi)
            norm_to_hT(gmlp_c)
            def mlp_up(fg, stu, vu):
                for f in range(4):
                    cf = (fg % 2) * 4 + f
                    p = proj_chunk(stu, vu, f)
                    kb.act(SQ.t[:, 0:T], p.t[:, 0:T], AF.Relu, [p], [SQ])
                    kb.tt("dve", R32(MRv(cf)[:, 0:T]), SQ.t[:, 0:T], SQ.t[:, 0:T], ALU.mult, [SQ], [MR(cf)])

            def mlp_down(fg, std, vd):
                for i in range(nt):
                    for half in range(2):
                        p = kb.ps("big")
                        for f in range(4):
                            cf = (fg % 2) * 4 + f
                            kb.mm(p.t[:, :], R32(MRv(cf)[:, i * 128:(i + 1) * 128]), R32(vd.ap(f, half * 512, 512)), f == 0, f == 3, [MR(cf)] + std.keys, [p])
                        kb.tt("dve", X.t[:, i, half * 512:(half + 1) * 512], X.t[:, i, half * 512:(half + 1) * 512], p.t[:, :], ALU.add, [(X, i), p], [(X, i)])

            stu, vu = stage_k8(w_up, 0, 512)
            mlp_up(0, stu, vu)
            for fg in range(8):
                if fg + 1 < 8:
                    stu, vu = stage_k8(w_up, (fg + 1) * 512, 512)
                    mlp_up(fg + 1, stu, vu)
                std, vd = stage_k4(w_down, fg * 512)
                mlp_down(fg, std, vd)

            checkpoint("mlp", bi)
            for i in range(nt):
                kb.act(XNv, X.t[:, i, :], AF.Square, [(X, i)], XNk + [(ssq, i)], accum_out=ssq.t[:, i:i + 1])
                kb.act(lntmp.t[:, i:i + 1], ssq.t[:, i:i + 1], AF.Ln, [(ssq, i), epscol], [(lntmp, i)], bias=epscol.t[:, :], scale=1.0 / D)
                kb.act(rstd.t[:, i:i + 1], lntmp.t[:, i:i + 1], AF.Exp, [(lntmp, i)], [(rstd, i)], scale=-0.5)
                yv = CV.t[:, 2 * i:2 * i + 2, :].rearrange("p a b -> p (a b)")
                yk = [(CV, 2 * i), (CV, 2 * i + 1)]
                kb.stt(yv, X.t[:, i, :], rstd.t[:, i:i + 1], gfin.t[:, :], ALU.mult, ALU.mult, [(X, i), (rstd, i), gfin], yk)
                kb.dma("sp", ydst[i * 128:(i + 1) * 128, :], yv, yk, [], "Y%d" % i)

        try:
            if not stopped:
                run_all()
        except StopBuild:
            pass
        kb.finish()
        kb.emit()
    return nc


_NC = None


def kernel(x_prompt, x_sample, state_conv, state_pool, state_ssm, w_in, w_conv, a_log, dt_bias, w_onorm,
           w_pool_mix, pool_scale, w_a_out, w_b_out, w_o, g_attn, g_mlp, w_up, w_down, g_final):
    global _NC
    f = lambda a: np.ascontiguousarray(np.asarray(a, dtype=np.float32))
    if _NC is None:
        _NC = build_program()
    nc = _NC
    ncores = 8
    shared = {
        "w_in": f(w_in[0]), "w_conv": f(w_conv[0]), "a_log": f(a_log[0]).reshape(NH, 1), "dt_bias": f(dt_bias[0]).reshape(NH, 1),
        "w_onorm": f(w_onorm[0]).reshape(128, 1), "w_mix": f(w_pool_mix[0]), "pool_scale": f(pool_scale[0]).reshape(512, 1),
        "w_a": f(w_a_out[0]), "w_b": f(w_b_out[0]), "w_o": f(w_o[0]), "g_attn": f(g_attn[0]).reshape(D, 1),
        "g_mlp": f(g_mlp[0]).reshape(D, 1), "w_up": f(w_up[0]), "w_down": f(w_down[0]), "g_final": f(g_final).reshape(1, D),
    }
    in_maps = []
    for ci in range(ncores):
        sl = slice(ci * NS, (ci + 1) * NS)
        m = dict(shared)
        m["xp"] = f(x_prompt[ci])
        m["xs"] = f(x_sample[sl]).reshape(NS * DS, D)
        m["sconv"] = f(state_conv[0, sl]).reshape(NS * 3, QKVW)
        m["spool"] = f(state_pool[0, sl]).reshape(NS * 15, 512)
        m["sssm"] = f(state_ssm[0, sl])
        in_maps.append(m)
    res = run_bass_kernel_spmd(nc, in_maps, core_ids=list(range(ncores)))
    r = res.results
    y_prompt = np.stack([r[i]["yp"] for i in range(ncores)], 0)
    y_sample = np.concatenate([r[i]["ys"].reshape(NS, DS, D) for i in range(ncores)], 0)
    conv_p = np.stack([r[i]["convp"] for i in range(ncores)], 0)[None]
    pool_p = np.stack([r[i]["poolp"] for i in range(ncores)], 0)[None]
    ssm_p = np.stack([r[i]["ssmp"] for i in range(ncores)], 0)[None]
    conv_s = np.concatenate([r[i]["convs"].reshape(NS, 3, QKVW) for i in range(ncores)], 0)[None]
    pool_s = np.concatenate([r[i]["pools"].reshape(NS, 15, 512) for i in range(ncores)], 0)[None]
    ssm_s = np.concatenate([r[i]["ssms"] for i in range(ncores)], 0)[None]
    return (y_prompt.astype(np.float32), y_sample.astype(np.float32), conv_p.astype(np.float32), pool_p.astype(np.float32),
            ssm_p.astype(np.float32), conv_s.astype(np.float32), pool_s.astype(np.float32), ssm_s.astype(np.float32))
```

```python
import math
from contextlib import ExitStack

import numpy as np
import concourse.bass as bass
import concourse.mybir as mybir
from concourse.bass_utils import run_bass_kernel_spmd

F32 = mybir.dt.float32
F32R = mybir.dt.float32r
AF = mybir.ActivationFunctionType
ALU = mybir.AluOpType

D = 1024
NH = 4
QKVW = 1536
INW = 4616
DFF = 4096
EPS = 1e-6
SEQ = 2048
NS = 16
DS = 8
TB = 512
CP = 128
NEG = -1.0e30
NSTAGE = 3
FORCE_DVE_EVAC = True


class Trk:
    __slots__ = ("w", "r")

    def __init__(self):
        self.w = None
        self.r = []


class Tl:
    def __init__(self, name, t):
        self.name = name
        self.t = t
        self.tr = {None: Trk()}

    def trackers(self, key):
        if key is None:
            return list(self.tr.values())
        if key not in self.tr:
            self.tr[key] = Trk()
        return [self.tr[key], self.tr[None]]

    def tracker(self, key):
        if key not in self.tr:
            self.tr[key] = Trk()
        return self.tr[key]


def _norm(acc):
    out = []
    for a in acc:
        if isinstance(a, Tl):
            out.append((a, None))
        else:
            out.append(a)
    return out


class KB:
    ENG = ("pe", "act", "dve", "pool", "sp")

    def __init__(self, nc, stack):
        self.nc = nc
        self.stack = stack
        self.ops = {e: [] for e in self.ENG}
        self.sem = {e: stack.enter_context(nc.semaphore("s_" + e)) for e in self.ENG}
        self.cnt = {e: 0 for e in self.ENG}
        self.waited = {e: {} for e in self.ENG}
        self.dsem = {}
        self.semobj = {}
        for e in self.ENG:
            self.semobj[self.sem[e].name] = self.sem[e]
        self.ps_pools = {}
        self.ps_idx = {}
        self.n_sb = 0
        self.needed = set()

    def sb(self, name, shape, dtype=F32):
        t = self.stack.enter_context(self.nc.sbuf_tensor(name, list(shape), dtype))
        return Tl(name, t)

    def psum(self, name, shape):
        t = self.stack.enter_context(self.nc.psum_tensor(name, list(shape), F32))
        return Tl(name, t)

    def mkpool(self, pname, n):
        self.ps_pools[pname] = [self.psum("%s%d" % (pname, i), [128, 512]) for i in range(n)]
        self.ps_idx[pname] = 0

    def ps(self, pname):
        i = self.ps_idx[pname]
        self.ps_idx[pname] = (i + 1) % len(self.ps_pools[pname])
        return self.ps_pools[pname][i]

    def _deps(self, reads, writes):
        evs = []
        for tl, key in reads:
            for trk in tl.trackers(key):
                if trk.w is not None:
                    evs.append(trk.w)
        for tl, key in writes:
            for trk in tl.trackers(key):
                if trk.w is not None:
                    evs.append(trk.w)
                evs.extend(trk.r)
        return evs

    def _commit(self, ev, reads, writes):
        for tl, key in reads:
            tl.tracker(key).r.append(ev)
        for tl, key in writes:
            if key is None:
                for trk in tl.tr.values():
                    trk.w = ev
                    trk.r = []
            else:
                trk = tl.tracker(key)
                trk.w = ev
                trk.r = []

    def _waits(self, eng, evs):
        need = {}
        for semname, val in evs:
            if eng == "pe" and semname == self.sem["pe"].name:
                continue
            if self.waited[eng].get(semname, 0) >= val:
                continue
            if need.get(semname, 0) < val:
                need[semname] = val
        for k, v in need.items():
            self.waited[eng][k] = v
            self.needed.add((k, v))
        return [(self.semobj[k], v) for k, v in need.items()]

    def op(self, eng, fn, reads=(), writes=(), inc=True):
        reads = _norm(reads)
        writes = _norm(writes)
        waits = self._waits(eng, self._deps(reads, writes))
        ev = (self.sem[eng].name, self.cnt[eng] + 1)
        if inc:
            self.cnt[eng] += 1
        self.ops[eng].append((waits, fn, self.sem[eng] if inc else None, 1))
        self._commit(ev, reads, writes)

    def dma(self, q, out_ap, in_ap, reads, writes, semkey, slow=False):
        reads = _norm(reads)
        writes = _norm(writes)
        waits = self._waits(q, self._deps(reads, writes))
        if semkey not in self.dsem:
            s = self.stack.enter_context(self.nc.semaphore("d%d" % len(self.dsem)))
            self.dsem[semkey] = [s, 0]
            self.semobj[s.name] = s
        ent = self.dsem[semkey]
        ent[1] += 16
        ev = (ent[0].name, ent[1])
        if slow:
            self.ops[q].append((waits, lambda e: e.dma_start(out=out_ap, in_=in_ap, allow_slow_non_contiguous=True), ent[0], 16))
        else:
            self.ops[q].append((waits, lambda e: e.dma_start(out=out_ap, in_=in_ap), ent[0], 16))
        self._commit(ev, reads, writes)

    def finish(self):
        waits = []
        for semkey, (s, v) in self.dsem.items():
            if self.waited["sp"].get(s.name, 0) < v:
                waits.append((s, v))
        self.ops["sp"].append((waits, None, None, 0))

    def emit(self):
        nc = self.nc
        kb = self

        eng_sems = {kb.sem[e_].name: e_ for e_ in kb.ENG}
        remap = {}
        for semname, e_ in eng_sems.items():
            vals = sorted(v for (k, v) in kb.needed if k == semname)
            remap[semname] = {v: i + 1 for i, v in enumerate(vals)}

        def run(e, name):
            cnt = 0
            for waits, fn, sem, incv in kb.ops[name]:
                for s, v in waits:
                    if s.name in remap:
                        e.wait_ge(s, remap[s.name][v])
                    else:
                        e.wait_ge(s, v)
                if fn is None:
                    continue
                ins = fn(e)
                if sem is not None:
                    if sem.name in remap:
                        cnt += 1
                        if cnt in remap[sem.name]:
                            ins.then_inc(sem, 1)
                    else:
                        ins.then_inc(sem, incv)

        with nc.Block() as block:
            @block.sync
            def _(e):
                run(e, "sp")

            @block.tensor
            def _(e):
                run(e, "pe")

            @block.scalar
            def _(e):
                run(e, "act")

            @block.vector
            def _(e):
                run(e, "dve")

            @block.gpsimd
            def _(e):
                run(e, "pool")

    def mm(self, out, lhsT, rhs, start, stop, reads, writes):
        self.op("pe", lambda e: e.matmul(out, lhsT, rhs, start=start, stop=stop), reads, writes, inc=stop)

    def tr(self, out, in_, ident, reads, writes, inc=True):
        self.op("pe", lambda e: e.transpose(out, in_, ident), reads, writes, inc=inc)

    def act(self, out, in_, func, reads, writes, bias=None, scale=None, accum_out=None):
        kw = {}
        if bias is not None:
            kw["bias"] = bias
        if scale is not None:
            kw["scale"] = scale
        if accum_out is not None:
            kw["accum_out"] = accum_out
        self.op("act", lambda e: e.activation(out=out, in_=in_, func=func, **kw), reads, writes)

    def ts(self, eng, out, in0, s1, s2, op0, op1, reads, writes):
        if op1 is None:
            self.op(eng, lambda e: e.tensor_scalar(out, in0, s1, None, op0), reads, writes)
        else:
            self.op(eng, lambda e: e.tensor_scalar(out, in0, s1, s2, op0, op1), reads, writes)

    def tt(self, eng, out, in0, in1, op, reads, writes):
        self.op(eng, lambda e: e.tensor_tensor(out, in0, in1, op), reads, writes)

    def stt(self, out, in0, scalar, in1, op0, op1, reads, writes):
        self.op("dve", lambda e: e.scalar_tensor_tensor(out, in0, scalar, in1, op0, op1), reads, writes)

    def copy(self, eng, out, in_, reads, writes):
        if eng == "act":
            self.op("act", lambda e: e.copy(out, in_), reads, writes)
        else:
            self.op(eng, lambda e: e.tensor_copy(out, in_), reads, writes)

    def memset(self, eng, ap, val, writes):
        self.op(eng, lambda e: e.memset(ap, val), (), writes)


class StopBuild(Exception):
    pass


def build_program(stop=None, stop_block=0, dumps=()):
    nc = bass.Bass("TRN2", target_bir_lowering=False)

    def din(name, shape):
        return nc.dram_tensor(name, list(shape), F32, kind="ExternalInput").ap()

    def dout(name, shape):
        return nc.dram_tensor(name, list(shape), F32, kind="ExternalOutput").ap()

    xp = din("xp", [SEQ, D])
    xs = din("xs", [NS * DS, D])
    sconv = din("sconv", [NS * 3, QKVW])
    spool = din("spool", [NS * 15, 512])
    sssm = din("sssm", [NS, NH, 128, 128])
    w_in = din("w_in", [D, INW])
    w_conv = din("w_conv", [4, QKVW])
    a_log = din("a_log", [NH, 1])
    dt_bias = din("dt_bias", [NH, 1])
    w_onorm = din("w_onorm", [128, 1])
    w_mix = din("w_mix", [4, 128, 128])
    pool_scale = din("pool_scale", [512, 1])
    w_a = din("w_a", [512, D])
    w_b = din("w_b", [512, D])
    w_o = din("w_o", [D, D])
    g_attn = din("g_attn", [D, 1])
    g_mlp = din("g_mlp", [D, 1])
    w_up = din("w_up", [D, DFF])
    w_down = din("w_down", [DFF, D])
    g_final = din("g_final", [1, D])

    yp = dout("yp", [SEQ, D])
    ys = dout("ys", [NS * DS, D])
    convp = dout("convp", [3, QKVW])
    poolp = dout("poolp", [15, 512])
    ssmp = dout("ssmp", [NH, 128, 128])
    convs = dout("convs", [NS * 3, QKVW])
    pools = dout("pools", [NS * 15, 512])
    ssms = dout("ssms", [NS, NH, 128, 128])

    with ExitStack() as stack:
        kb = KB(nc, stack)
        TILES = {}
        def sb(name, shape, dtype=F32):
            tl = kb.sb(name, shape, dtype)
            TILES[name] = tl
            return tl
        OUT = Tl("OUT", None)

        X = sb("X", [128, 4, D])
        hT = sb("hT", [128, 8, TB])
        STG = [sb("stg%d" % i, [128, 4096]) for i in range(2)]
        PRE = [sb("pre%d" % i, [128, TB + 3]) for i in range(2)]
        HIST = sb("hist", [128, 12, 3])
        CV = sb("cv", [128, 12, TB])
        PT = sb("pT", [128, 4, TB + 15])
        PL = sb("pooled", [128, 4, TB])
        OB = sb("obT", [128, 4, TB])
        OA = sb("oaT", [128, 4, TB])
        SQ = sb("sq", [128, TB])
        SQR = sb("sqr", [128, TB])
        RS = sb("rsb", [128, TB])
        GBC = sb("gbc", [128, 4, TB])
        LBC = sb("lbc", [128, 4, TB])
        S = sb("S", [128, 4, 128])
        SS = [sb("Ss%d" % i, [128, 4, 128]) for i in range(2)]
        ssq = sb("ssq", [128, 4])
        rstd = sb("rstd", [128, 4])
        ident = sb("ident", [128, 128])
        ones = sb("ones", [128, 128])
        c128 = sb("c128", [128, 128])
        negup = sb("negup", [CP, CP])
        negcz = sb("negcz", [CP, CP])
        poslo = sb("poslo", [CP, CP])
        selg = sb("selg", [8, 4, 128])
        sell = sb("sell", [8, 4, 128])
        mask_p = sb("mask_p", [8, TB])
        mask_s = sb("mask_s", [8, NS * DS])
        gattn_c = sb("gattn_c", [128, 8])
        gmlp_c = sb("gmlp_c", [128, 8])
        gfin = sb("gfin", [128, D])
        wconv_c = sb("wconv_c", [128, 4, 12])
        pscale_c = sb("pscale_c", [128, 4])
        onorm_c = sb("onorm_c", [128, 1])
        wmix = sb("wmix", [128, 4, 128])
        biascol = sb("biascol", [8, 1])
        scalecol = sb("scalecol", [8, 1])
        mulcol = sb("mulcol", [8, 1])
        alog_t = sb("alog_t", [8, 1])
        WBA = sb("wba", [128, 8, 8])
        invcnt = sb("invcnt", [128, 4, 16])
        epscol = sb("epscol", [128, 1])
        eps128col = sb("eps128col", [128, 1])
        onecol = sb("onecol", [128, 1])
        lntmp = sb("lntmp", [128, 4])
        RC = sb("rc", [8, TB])
        COL = sb("col", [CP, 24])
        KBE = sb("kbe", [CP, 4, 128])
        KD = sb("kd", [CP, 4, 128])
        VB = sb("vb", [CP, 4, 128])
        E1 = sb("e1", [CP, 4, CP])
        E2 = sb("e2", [CP, 4, CP])
        E3 = sb("e3", [CP, 4, CP])
        QKD = sb("qkd", [CP, 4, CP])
        PRt = sb("prt", [CP, 4, 2 * CP])
        PTA = [sb("pta%d" % i, [CP, 4, CP]) for i in range(2)]
        WTN = sb("wtn", [128, 4, CP])
        VNEW = sb("vnew", [CP, 4, 128])
        EG = sb("eg", [128, 4, CP])
        QD = sb("qd", [128, 4, CP])
        STP = [sb("stp%d" % i, [128, 512]) for i in range(2)]
        UTT = [sb("utt%d" % i, [NS, 128]) for i in range(2)]
        UT = [sb("ut%d" % i, [128, NS]) for i in range(2)]
        LSEL = sb("lsel", [128, 128])
        MADD = sb("madd", [128, 128])

        class _V:
            pass
        STC = _V()
        STC.t = PL.t[0:48, :, :].rearrange("p g t -> p (g t)")[:, 0:QKVW]
        XNv = CV.t[:, 10:12, :].rearrange("p a b -> p (a b)")
        XNk = [(CV, 10), (CV, 11)]
        ROW1 = _V(); ROW1.t = SQ.t[0:8, :]
        ROW2 = _V(); ROW2.t = RS.t[0:8, :]
        dbg_aps = {}

        def checkpoint(label, bi_):
            if stop is None or label != stop or bi_ != stop_block:
                return
            for nm in dumps:
                tl = TILES[nm]
                shp = list(tl.t.shape)
                d = nc.dram_tensor("dbg_" + nm, shp, F32, kind="ExternalOutput").ap()
                kb.dma("sp", d, tl.t[:], [tl], [], "dbg_" + nm)
            raise StopBuild()

        kb.mkpool("big", 4)
        kb.mkpool("sm", 4)

        R32 = lambda ap: ap.bitcast(F32R)

        kb.memset("pool", CV.t[:, 0, :], 1.0, [CV])
        kb.copy("dve", R32(ones.t[:]), CV.t[:, 0, 0:128], [CV], [ones])
        kb.ts("dve", R32(c128.t[:]), CV.t[:, 0, 0:128], 128.0, None, ALU.mult, None, [CV], [c128])
        kb.op("pool", lambda e: e.affine_select(ident.t[:], CV.t[:, 0, 0:128], [[1, 128]], ALU.is_equal, 0.0, base=0, channel_multiplier=-1), [CV], [ident])
        kb.memset("pool", CV.t[:, 1, 0:CP], 0.0, [CV])
        zsrc = CV.t[0:CP, 1, 0:CP]
        kb.op("pool", lambda e: e.affine_select(negup.t[:], zsrc, [[1, CP]], ALU.is_gt, NEG, base=0, channel_multiplier=-1), [CV], [negup])
        kb.op("pool", lambda e: e.affine_select(negcz.t[:], zsrc, [[1, CP]], ALU.is_ge, NEG, base=0, channel_multiplier=-1), [CV], [negcz])
        kb.op("pool", lambda e: e.affine_select(poslo.t[:], zsrc, [[-1, CP]], ALU.is_gt, -NEG, base=0, channel_multiplier=1), [CV], [poslo])
        o8v = CV.t[0:8, 0, :].rearrange("p (h m) -> p h m", h=4)
        kb.op("pool", lambda e: e.affine_select(selg.t[:], o8v, [[-1, 4], [0, 128]], ALU.is_equal, 0.0, base=0, channel_multiplier=1), [CV], [selg])
        kb.op("pool", lambda e: e.affine_select(sell.t[:], selg.t[:], [[-1, 4], [0, 128]], ALU.not_equal, 1.0, base=-4, channel_multiplier=1), [selg], [sell])
        for mk, T, c, m1, m1t in ((mask_p, TB, CP, ROW1, SQ), (mask_s, NS * DS, DS, ROW2, RS)):
            kb.op("pool", lambda e, mk=mk, T=T, c=c, m1=m1: e.affine_select(m1.t[:, 0:T], CV.t[0:8, 0, 0:T], [[0, T // c], [1, c]], ALU.not_equal, 0.0, base=0, channel_multiplier=0), [CV], [m1t])
            kb.op("pool", lambda e, mk=mk, T=T, m1=m1: e.affine_select(mk.t[:], m1.t[:, 0:T], [[0, T]], ALU.is_ge, 0.0, base=3, channel_multiplier=-1), [m1t], [mk])
        for gi, win in enumerate((2, 4, 8, 16)):
            kb.memset("pool", invcnt.t[:, gi, :], 1.0 / win, [(invcnt, gi)])
            for t in range(win - 1):
                kb.memset("pool", invcnt.t[:, gi, t:t + 1], 1.0 / (t + 1), [(invcnt, gi)])
        def early(label):
            try:
                checkpoint(label, 0)
            except StopBuild:
                return True
            return False

        stopped = early("setup0")

        def small_load(tl, out_ap, in_ap, semkey):
            kb.dma("sp", out_ap, in_ap, [], [tl], semkey, slow=True)

        small_load(gattn_c, gattn_c.t[:], g_attn.rearrange("(c p) o -> p (c o)", p=128), "gattn")
        small_load(gmlp_c, gmlp_c.t[:], g_mlp.rearrange("(c p) o -> p (c o)", p=128), "gmlp")
        small_load(pscale_c, pscale_c.t[:], pool_scale.rearrange("(c p) o -> p (c o)", p=128), "pscale")
        small_load(onorm_c, onorm_c.t[:], w_onorm[:, :], "onorm")
        for j in range(4):
            small_load(wconv_c, wconv_c.t[:, j, :], w_conv[j:j + 1, :].rearrange("o (c p) -> p (o c)", p=128), "wconv")
        kb.dma("sp", gfin.t[:], g_final[0:1, :].to_broadcast([128, D]), [], [gfin], "gfin")
        kb.dma("sp", wmix.t[:], w_mix.rearrange("g c d -> c g d"), [], [wmix], "wmix")
        kb.memset("dve", biascol.t[:], 0.0, [biascol])
        small_load(biascol, biascol.t[0:4, :], dt_bias[:, :], "biascol")
        kb.memset("dve", scalecol.t[:], -1.0, [scalecol])
        kb.memset("dve", scalecol.t[0:4, :], 1.0, [scalecol])
        kb.memset("dve", alog_t.t[:], 0.0, [alog_t])
        small_load(alog_t, alog_t.t[0:4, :], a_log[:, :], "alog")
        kb.memset("dve", mulcol.t[:], -1.0, [mulcol])
        kb.act(mulcol.t[0:4, :], alog_t.t[0:4, :], AF.Exp, [alog_t], [mulcol])
        kb.ts("dve", mulcol.t[0:4, :], mulcol.t[0:4, :], -1.0, None, ALU.mult, None, [mulcol], [mulcol])
        kb.memset("dve", HIST.t[:], 0.0, [HIST])
        kb.memset("dve", epscol.t[:], EPS, [epscol])
        kb.memset("dve", eps128col.t[:], 128.0 * EPS, [eps128col])
        kb.memset("dve", onecol.t[:], 1.0, [onecol])
        kb.ts("dve", R32(S.t[:].rearrange("p h d -> p (h d)")), gfin.t[:, 0:512], 0.0, None, ALU.mult, None, [gfin], [S])

        for (c0, d0) in ((2052, 0), (2048, 4)):
            kb.dma("pool", R32(WBA.t[:, :, d0:d0 + 4]), w_in[:, c0:c0 + 4].rearrange("(k p) n -> p k n", p=128), [], [WBA], "wba", slow=True)
        stopped = stopped or early("setup1")
        stg_i = [0]

        class Stage:
            def __init__(self, tiles):
                self.tiles = tiles
                self.keys = list(tiles)
                self.kind = None

            def ap(self, k, c0, n):
                if len(self.tiles) == 1:
                    t = self.tiles[0].t
                    if self.kind == "k8":
                        return t[:, 0:4096].rearrange("p (k n) -> p k n", k=8)[:, k, c0:c0 + n]
                    return t[:, :].rearrange("p (k n) -> p k n", k=4)[:, k, c0:c0 + n]
                if self.kind == "k8":
                    return self.flat(k)[:, c0:c0 + n]
                return self.flat(2 * k + c0 // 512)[:, c0 % 512:c0 % 512 + n]

            def flat(self, i):
                return self.tiles[i].t[:, :, :].rearrange("p h d -> p (h d)")

        STAGES = [Stage([STG[0]]), Stage([STG[1]]), Stage([KBE, KD, VB, VNEW, QKD, WTN, QD, PTA[0]])]

        def next_stage():
            st = STAGES[stg_i[0] % NSTAGE]
            stg_i[0] += 1
            return st

        def stage_k8(w_ap, col0, ncol, extra=None):
            st = next_stage()
            st.kind = "k8"
            if len(st.tiles) == 1:
                view = st.tiles[0].t[:, 0:8 * ncol].rearrange("p (k n) -> p k n", k=8)
                kb.dma("pool", R32(view), w_ap[:, col0:col0 + ncol].rearrange("(k p) n -> p k n", p=128), [], [st.tiles[0]], st.tiles[0].name)
            else:
                for k in range(8):
                    kb.dma("pool", R32(st.flat(k)), w_ap[k * 128:(k + 1) * 128, col0:col0 + ncol], [], [st.tiles[k]], "v_" + st.tiles[k].name)
            return st, st

        def stage_k4(w_ap, row0):
            st = next_stage()
            st.kind = "k4"
            if len(st.tiles) == 1:
                view = st.tiles[0].t[:, :].rearrange("p (k n) -> p k n", k=4)
                kb.dma("pool", R32(view), w_ap[row0:row0 + 512, :].rearrange("(k p) n -> p k n", p=128), [], [st.tiles[0]], st.tiles[0].name)
            else:
                for f in range(4):
                    for hh in range(2):
                        i = 2 * f + hh
                        kb.dma("pool", R32(st.flat(i)), w_ap[row0 + f * 128:row0 + (f + 1) * 128, hh * 512:(hh + 1) * 512], [], [st.tiles[i]], "v_" + st.tiles[i].name)
            return st, st

        def chunk_epilogue(po, po3, cs, cw):
            n4 = 4 * cw
            sq3 = SQR.t[:, 0:n4].rearrange("p (h n) -> p h n", h=4)
            rs3 = RS.t[:, 0:n4].rearrange("p (h n) -> p h n", h=4)
            kb.act(R32(sq3), po3, AF.Square, [po], [SQR])
            p4 = kb.ps("big")
            kb.mm(p4.t[:, 0:n4], R32(ones.t[:, :]), R32(SQR.t[:, 0:n4]), True, True, [SQR, ones], [p4])
            kb.act(RS.t[:, 0:n4], p4.t[:, 0:n4], AF.Ln, [p4, epscol], [RS], bias=epscol.t[:, :], scale=1.0 / 128.0)
            kb.act(RS.t[:, 0:n4], RS.t[:, 0:n4], AF.Exp, [RS], [RS], scale=-0.5)
            kb.stt(rs3, rs3, onorm_c.t[:, 0:1], PL.t[:, :, cs], ALU.mult, ALU.mult, [RS, onorm_c, PL], [RS])
            kb.tt("dve", R32(OA.t[:, :, cs]), po3, rs3, ALU.mult, [po, RS], [(OA, 0), (OA, 1), (OA, 2), (OA, 3)])

        evac_flip = [0]

        def evac_eng():
            evac_flip[0] ^= 1
            return "act" if evac_flip[0] else "dve"

        blocks = [("p", i * TB) for i in range(SEQ // TB)] + [("s", 0)]
        def run_all():
          for bi, (kind, tok0) in enumerate(blocks):
            T = TB if kind == "p" else NS * DS
            nt = T // 128
            c = CP
            nch = T // c
            xsrc = xp[tok0:tok0 + T, :] if kind == "p" else xs[:, :]
            ydst = yp[tok0:tok0 + T, :] if kind == "p" else ys[:, :]
            first_prompt = kind == "p" and tok0 == 0
            last_prompt = kind == "p" and tok0 == SEQ - TB

            for i in range(nt):
                kb.dma("pool", X.t[:, i, :], xsrc[i * 128:(i + 1) * 128, :], [], [(X, i)], "X%d" % i)

            def norm_to_hT(gcol):
                for i in range(nt):
                    kb.act(XNv, X.t[:, i, :], AF.Square, [(X, i)], XNk + [(ssq, i)], accum_out=ssq.t[:, i:i + 1])
                    kb.act(lntmp.t[:, i:i + 1], ssq.t[:, i:i + 1], AF.Ln, [(ssq, i), epscol], [(lntmp, i)], bias=epscol.t[:, :], scale=1.0 / D)
                    kb.act(rstd.t[:, i:i + 1], lntmp.t[:, i:i + 1], AF.Exp, [(lntmp, i)], [(rstd, i)], scale=-0.5)
                    checkpoint("S0a", bi)
                    kb.ts("dve", XNv, X.t[:, i, :], rstd.t[:, i:i + 1], None, ALU.mult, None, [(X, i), (rstd, i)], XNk)
                    checkpoint("S0b", bi)
                    for half in range(2):
                        p = kb.ps("big")
                        for q in range(4):
                            cch = half * 4 + q
                            kb.tr(p.t[:, q * 128:(q + 1) * 128], XNv[:, cch * 128:(cch + 1) * 128], ident.t[:], XNk + [ident], [p], inc=(q == 3))
                        checkpoint("S0c", bi)
                        for q in range(4):
                            cch = half * 4 + q
                            eng = "dve" if FORCE_DVE_EVAC else evac_eng()
                            dst = R32(hT.t[:, cch, i * 128:(i + 1) * 128])
                            src = p.t[:, q * 128:(q + 1) * 128]
                            if eng == "act":
                                kb.act(dst, src, AF.Copy, [p, gcol], [(hT, cch)], scale=gcol.t[:, cch:cch + 1])
                            else:
                                kb.ts("dve", dst, src, gcol.t[:, cch:cch + 1], None, ALU.mult, None, [p, gcol], [(hT, cch)])

            norm_to_hT(gattn_c)
            checkpoint("S0", bi)

            def proj_chunk(stt_, view, m, ncols_m=128):
                p = kb.ps("big")
                for k in range(8):
                    kb.mm(p.t[:, 0:T], R32(view.ap(k, m * 128, ncols_m)), R32(hT.t[:, k, 0:T]), k == 0, k == 7,
                          [(hT, k)] + stt_.keys, [p])
                return p


            pba = kb.ps("big")
            for k in range(8):
                kb.mm(pba.t[0:8, 0:T], R32(WBA.t[:, k, :]), R32(hT.t[:, k, 0:T]), k == 0, k == 7, [(hT, k), WBA], [pba])
            kb.act(ROW1.t[:, 0:T], pba.t[0:8, 0:T], AF.Exp, [pba, biascol, scalecol], [SQ], bias=biascol.t[:, :], scale=scalecol.t[:, :])
            kb.act(ROW2.t[:, 0:T], ROW1.t[:, 0:T], AF.Ln, [SQ, onecol], [RS], bias=onecol.t[0:8, :])
            kb.ts("dve", ROW1.t[:, 0:T], ROW2.t[:, 0:T], mulcol.t[:, :], None, ALU.mult, None, [RS, mulcol], [SQ])
            mk = mask_p if kind == "p" else mask_s
            kb.op("dve", lambda e, mk=mk, T=T: e.tensor_tensor_scan(RC.t[:, 0:T], mk.t[:, 0:T], ROW1.t[:, 0:T], 0.0, ALU.mult, ALU.add), [mk, SQ], [RC])
            checkpoint("S1a", bi)
            if kind == "s":
                kb.dma("sp", STC.t[:, :], sconv[:, :], [], [PL], "STC")
            deferred = []

            if kind == "p":
                stages_ = {}

                def conv_A(j):
                    grp = j // 4
                    if grp not in stages_:
                        stages_[grp] = stage_k8(w_in, grp * 512, 512)
                    st, vw = stages_[grp]
                    p = proj_chunk(st, vw, j % 4)
                    pre = PRE[j % 2]
                    kb.copy("dve", pre.t[:, 0:3], HIST.t[:, j, :], [(HIST, j)], [pre])
                    kb.copy("act", pre.t[:, 3:3 + T], p.t[:, 0:T], [p], [pre])
                    kb.act(CV.t[:, j, 0:T], p.t[:, 0:T], AF.Identity, [p, wconv_c], [(CV, j)], scale=wconv_c.t[:, 3, j:j + 1])

                def conv_B(j):
                    pre = PRE[j % 2]
                    cvv = CV.t[:, j, 0:T]
                    for tap in (2, 1, 0):
                        kb.stt(cvv, pre.t[:, tap:tap + T], wconv_c.t[:, tap, j:j + 1], cvv, ALU.mult, ALU.add, [pre, wconv_c, (CV, j)], [(CV, j)])
                    kb.act(cvv, cvv, AF.Silu, [(CV, j)], [(CV, j)])
                    kb.copy("dve", HIST.t[:, j, :], pre.t[:, T:T + 3], [pre], [(HIST, j)])
                    if last_prompt:
                        p3 = kb.ps("sm")
                        kb.tr(p3.t[0:3, 0:128], pre.t[:, T:T + 3], ident.t[:], [pre, ident], [p3])
                        kb.copy("act", STC.t[0:3, j * 128:(j + 1) * 128], p3.t[0:3, 0:128], [p3], [PL])
                    if j < 8:
                        deferred.append((j, j // 4))

                conv_A(0)
                for j in range(12):
                    if j + 1 < 12:
                        conv_A(j + 1)
                    conv_B(j)
            else:
                stages_ = {}

                def sviews(j):
                    pre = PRE[j % 2]
                    pre3 = pre.t[:, 0:NS * 11].rearrange("p (s w) -> p s w", w=11)
                    cvv = CV.t[:, j, 0:T].rearrange("p (s w) -> p s w", w=8)
                    return pre, pre3, cvv

                def sconv_A(j):
                    grp = j // 4
                    if grp not in stages_:
                        stages_[grp] = stage_k8(w_in, grp * 512, 512)
                    st, vw = stages_[grp]
                    p = proj_chunk(st, vw, j % 4)
                    pre, pre3, cvv = sviews(j)
                    p2 = kb.ps("sm")
                    kb.tr(p2.t[:, 0:48], STC.t[:, j * 128:(j + 1) * 128], ident.t[0:48, 0:48], [(PL, ("s", j)), ident], [p2])
                    kb.copy("dve", pre3[:, :, 0:3], p2.t[:, 0:48].rearrange("p (s w) -> p s w", w=3), [p2], [pre])
                    kb.copy("act", pre3[:, :, 3:11], p.t[:, 0:T].rearrange("p (s w) -> p s w", w=8), [p], [pre])
                    kb.act(cvv, p.t[:, 0:T].rearrange("p (s w) -> p s w", w=8), AF.Identity, [p, wconv_c], [(CV, j)], scale=wconv_c.t[:, 3, j:j + 1])

                def sconv_B(j):
                    pre, pre3, cvv = sviews(j)
                    for tap in (2, 1, 0):
                        kb.stt(cvv, pre3[:, :, tap:tap + 8], wconv_c.t[:, tap, j:j + 1], cvv, ALU.mult, ALU.add, [pre, wconv_c, (CV, j)], [(CV, j)])
                    kb.act(cvv, cvv, AF.Silu, [(CV, j)], [(CV, j)])
                    p3 = kb.ps("sm")
                    tmpc = SQ.t[:, 0:48].rearrange("p (s w) -> p s w", w=3)
                    kb.copy("dve", tmpc, pre3[:, :, 8:11], [pre], [SQ])
                    kb.tr(p3.t[0:48, 0:128], SQ.t[:, 0:48], ident.t[:], [SQ, ident], [p3])
                    kb.copy("act", STC.t[:, j * 128:(j + 1) * 128], p3.t[0:48, 0:128], [p3], [(PL, ("s", j))])
                    if j < 8:
                        deferred.append((j, j // 4))

                sconv_A(0)
                for j in range(12):
                    if j + 1 < 12:
                        sconv_A(j + 1)
                    sconv_B(j)
            if last_prompt:
                kb.dma("sp", convp[:, :], STC.t[0:3, :], [PL], [], "STC")
            if kind == "s":
                kb.dma("sp", convs[:, :], STC.t[:, :], [PL], [], "STC")

            checkpoint("conv", bi)
            st, vw = stage_k8(w_in, 2056, 512)
            stz, vwz = stage_k8(w_in, 1536, 512)

            def z_chunk(m):
                pz = proj_chunk(stz, vwz, m)
                kb.act(PL.t[:, m, 0:T], pz.t[:, 0:T], AF.Silu, [pz], [(PL, m)])

            if kind == "s":
                for r in range(2):
                    nr = 128 if r == 0 else NS * 15 - 128
                    kb.dma("sp", STP[r].t[0:nr, :], spool[r * 128:r * 128 + nr, :], [], [STP[r]], STP[r].name)
            for gi, win in enumerate((2, 4, 8, 16)):
                p = proj_chunk(st, vw, gi)
                if gi >= 1:
                    z_chunk(gi - 1)
                if kind == "p":
                    if first_prompt:
                        kb.memset("dve", PT.t[:, gi, 0:15], 0.0, [(PT, gi)])
                    else:
                        kb.copy("dve", PT.t[:, gi, 0:15], PT.t[:, gi, TB:TB + 15], [(PT, gi)], [(PT, gi)])
                    kb.copy(evac_eng(), PT.t[:, gi, 15:15 + T], p.t[:, 0:T], [p], [(PT, gi)])
                    W = T + 15
                    full = lambda a, b: PT.t[:, gi, a:b]
                    newp = PT.t[:, gi, 15:15 + T]
                    plv = lambda a, b: PL.t[:, gi, a:b]
                    obv = OB.t[:, gi, 0:T]
                    cur = full
                    curoff = 0
                    if win <= 4:
                        kb.tt("dve", plv(0, T), full(15, 15 + T), full(14, 14 + T), ALU.add, [(PT, gi)], [(PL, gi)])
                        for dd in range(2, win):
                            kb.tt("dve", plv(0, T), plv(0, T), full(15 - dd, 15 - dd + T), ALU.add, [(PT, gi), (PL, gi)], [(PL, gi)])
                    else:
                        og = 3 if gi == 2 else 0
                        S2 = PT.t[:, og, 0:512]
                        HW = T // 2
                        for hf in range(2):
                            t0 = hf * HW
                            base = 15 + t0 - (win - 1)
                            L = HW + win - 1
                            kb.tt("dve", SQ.t[:, 1:L], full(base + 1, base + L), full(base, base + L - 1), ALU.add, [(PT, gi)], [SQ])
                            bufs = [(SQ.t[:, 0:512], [SQ]), (S2, [(PT, og)])]
                            si, lo, sh = 0, 1, 2
                            while sh < win:
                                sv, sk = bufs[si]
                                dv, dk = bufs[1 - si]
                                if sh * 2 >= win:
                                    kb.tt("dve", plv(t0, t0 + HW), sv[:, win - 1:L], sv[:, win - 1 - sh:L - sh], ALU.add, sk, [(PL, gi)])
                                else:
                                    lo2 = lo + sh
                                    kb.tt("dve", dv[:, lo2:L], sv[:, lo2:L], sv[:, lo2 - sh:L - sh], ALU.add, sk, dk)
                                    si, lo = 1 - si, lo2
                                sh *= 2
                    if first_prompt:
                        kb.tt("dve", plv(0, 16), plv(0, 16), invcnt.t[:, gi, :], ALU.mult, [(PL, gi), (invcnt, gi)], [(PL, gi)])
                        kb.ts("dve", plv(16, T), plv(16, T), 1.0 / win, None, ALU.mult, None, [(PL, gi)], [(PL, gi)])
                        kb.tt("dve", plv(0, T), plv(0, T), newp, ALU.subtract, [(PL, gi), (PT, gi)], [(PL, gi)])
                    else:
                        kb.stt(plv(0, T), plv(0, T), 1.0 / win, newp, ALU.mult, ALU.subtract, [(PL, gi), (PT, gi)], [(PL, gi)])
                    if last_prompt:
                        p3 = kb.ps("sm")
                        kb.tr(p3.t[0:15, 0:128], PT.t[:, gi, T:T + 15], ident.t[:], [(PT, gi), ident], [p3])
                        kb.copy("act", STP[0].t[0:15, gi * 128:(gi + 1) * 128], p3.t[0:15, 0:128], [p3], [STP[0]])
                else:
                    pt3 = PT.t[:, gi, 0:NS * 23].rearrange("p (s w) -> p s w", w=23)
                    for r in range(2):
                        nr = 128 if r == 0 else NS * 15 - 128
                        p2 = kb.ps("sm")
                        kb.tr(p2.t[:, 0:nr], STP[r].t[0:nr, gi * 128:(gi + 1) * 128], ident.t[0:nr, 0:nr], [STP[r], ident], [p2])
                        kb.copy("dve", SQ.t[:, r * 128:r * 128 + nr], p2.t[:, 0:nr], [p2], [SQ])
                    kb.copy("dve", pt3[:, :, 0:15], SQ.t[:, 0:NS * 15].rearrange("p (s w) -> p s w", w=15), [SQ], [(PT, gi)])
                    kb.copy(evac_eng(), pt3[:, :, 15:23], p.t[:, 0:T].rearrange("p (s w) -> p s w", w=8), [p], [(PT, gi)])
                    pl3 = PL.t[:, gi, 0:T].rearrange("p (s w) -> p s w", w=8)
                    newp = pt3[:, :, 15:23]
                    kb.tt("dve", pl3, pt3[:, :, 15:23], pt3[:, :, 14:22], ALU.add, [(PT, gi)], [(PL, gi)])
                    for dd in range(2, win):
                        kb.tt("dve", pl3, pl3, pt3[:, :, 15 - dd:23 - dd], ALU.add, [(PT, gi), (PL, gi)], [(PL, gi)])
                    kb.stt(pl3, pl3, 1.0 / win, newp, ALU.mult, ALU.subtract, [(PL, gi), (PT, gi)], [(PL, gi)])
                    kb.copy("dve", SQ.t[:, 0:NS * 15].rearrange("p (s w) -> p s w", w=15), pt3[:, :, 8:23], [(PT, gi)], [SQ])
                    for r in range(2):
                        nr = 128 if r == 0 else NS * 15 - 128
                        p3 = kb.ps("sm")
                        kb.tr(p3.t[0:nr, 0:128], SQ.t[:, r * 128:r * 128 + nr], ident.t[:], [SQ, ident], [p3])
                        kb.copy("act", STP[r].t[0:nr, gi * 128:(gi + 1) * 128], p3.t[0:nr, 0:128], [p3], [STP[r]])
                p5 = kb.ps("big")
                kb.mm(p5.t[:, 0:T], wmix.t[:, gi, :], PL.t[:, gi, 0:T], True, True, [wmix, (PL, gi)], [p5])
                kb.ts("dve", R32(OB.t[:, gi, 0:T]), p5.t[:, 0:T], pscale_c.t[:, gi:gi + 1], None, ALU.mult, None, [p5, pscale_c], [(OB, gi)])
            if last_prompt:
                kb.dma("sp", poolp[:, :], STP[0].t[0:15, :], [STP[0]], [], STP[0].name)
            if kind == "s":
                for r in range(2):
                    nr = 128 if r == 0 else NS * 15 - 128
                    kb.dma("sp", pools[r * 128:r * 128 + nr, :], STP[r].t[0:nr, :], [STP[r]], [], STP[r].name)

            checkpoint("pool", bi)
            z_chunk(3)

            rsb = [RS, SQ]

            def l2_square(idx):
                j, grp = deferred[idx]
                kb.act(PT.t[:, idx % 4, 0:T], CV.t[:, j, 0:T], AF.Square, [(CV, j)], [(PT, idx % 4)])

            if deferred:
                l2_square(0)
            for idx, (j, grp) in enumerate(deferred):
                if idx + 1 < len(deferred):
                    l2_square(idx + 1)
                p4 = kb.ps("big")
                cm = c128 if grp == 0 else ones
                kb.mm(p4.t[:, 0:T], cm.t[:, :], PT.t[:, idx % 4, 0:T], True, True, [(PT, idx % 4), cm], [p4])
                ec = eps128col if grp == 0 else epscol
                rb = rsb[idx % 2]
                kb.act(rb.t[:, 0:T], p4.t[:, 0:T], AF.Ln, [p4, ec], [rb], bias=ec.t[:, :])
                kb.act(rb.t[:, 0:T], rb.t[:, 0:T], AF.Exp, [rb], [rb], scale=-0.5)
                kb.tt("dve", CV.t[:, j, 0:T], CV.t[:, j, 0:T], rb.t[:, 0:T], ALU.mult, [(CV, j), rb], [(CV, j)])
            del deferred[:]
            for h in range(4):
                for (sel, dstt) in ((selg, GBC), (sell, LBC)):
                    p = kb.ps("big")
                    kb.mm(p.t[:, 0:T], sel.t[:, h, :], RC.t[:, 0:T], True, True, [sel, RC], [p])
                    kb.copy(evac_eng(), dstt.t[:, h, 0:T], p.t[:, 0:T], [p], [(dstt, h)])

            checkpoint("z", bi)
            nsteps = int(math.log2(c if kind == "p" else DS)) - 1
            if kind == "s":
                kb.memset("pool", SQ.t[:, 0:128], 1.0, [SQ])
                onesrc = SQ.t[:, 0:128]
                kb.op("pool", lambda e: e.affine_select(UTT[0].t[:], onesrc[0:NS, :], [[1, 128]], ALU.is_ge, 0.0, base=0, channel_multiplier=-DS), [SQ], [UTT[0]])
                kb.op("pool", lambda e: e.affine_select(UTT[1].t[:], UTT[0].t[:], [[-1, 128]], ALU.is_ge, 0.0, base=DS - 1, channel_multiplier=DS), [UTT[0]], [UTT[1]])
                kb.op("pool", lambda e: e.affine_select(UT[0].t[:], onesrc[:, 0:NS], [[-DS, NS]], ALU.is_ge, 0.0, base=0, channel_multiplier=1), [SQ], [UT[0]])
                kb.op("pool", lambda e: e.affine_select(UT[1].t[:], UT[0].t[:], [[DS, NS]], ALU.is_ge, 0.0, base=DS - 1, channel_multiplier=-1), [UT[0]], [UT[1]])
                kb.op("pool", lambda e: e.affine_select(LSEL.t[:], onesrc, [[-DS, NS], [0, DS]], ALU.is_equal, 0.0, base=-(DS - 1), channel_multiplier=1), [SQ], [LSEL])
                pbd = kb.ps("sm")
                kb.mm(pbd.t[:, 0:128], UTT[1].t[:, :], UTT[1].t[:, :], True, True, [UTT[1]], [pbd])
                kb.ts("dve", MADD.t[:], pbd.t[:, 0:128], -NEG, NEG, ALU.mult, ALU.add, [pbd], [MADD])
                kb.tt("dve", negup.t[:], negup.t[:], MADD.t[:], ALU.add, [negup, MADD], [negup])
                kb.tt("dve", negcz.t[:], negcz.t[:], MADD.t[:], ALU.add, [negcz, MADD], [negcz])
                kb.tt("dve", poslo.t[:], poslo.t[:], MADD.t[:], ALU.subtract, [poslo, MADD], [poslo])
            def pre_cols(cx):
                cs = slice(cx * c, (cx + 1) * c)
                last = (cx + 1) * c - 1
                pc = kb.ps("sm")
                kb.tr(pc.t[0:c, 0:8], RC.t[0:8, cs], ident.t[0:8, 0:8], [RC, ident], [pc])
                kb.copy("dve", COL.t[0:c, 0:8], pc.t[0:c, 0:8], [pc], [COL])
                kb.tt("dve", COL.t[0:c, 8:12], COL.t[0:c, 0:4], COL.t[0:c, 4:8], ALU.add, [COL], [COL])
                if kind == "p":
                    kb.tt("dve", COL.t[0:c, 12:16], GBC.t[0:c, :, last], COL.t[0:c, 0:4], ALU.subtract, [COL, GBC], [COL])
                else:
                    pg = kb.ps("sm")
                    kb.mm(pg.t[:, 0:4], LSEL.t[:, :], COL.t[:, 0:4], True, True, [LSEL, COL], [pg])
                    kb.tt("dve", COL.t[0:c, 12:16], pg.t[:, 0:4], COL.t[0:c, 0:4], ALU.subtract, [COL, pg], [COL])
                kb.act(COL.t[0:c, 16:24], COL.t[0:c, 8:16], AF.Exp, [COL], [COL])
                kb.act(COL.t[0:c, 4:8], COL.t[0:c, 4:8], AF.Exp, [COL], [COL])

            def pre_E(cx, which):
                cs = slice(cx * c, (cx + 1) * c)
                e1 = E1.t[0:c, :, 0:c]
                e2 = E2.t[0:c, :, 0:c]
                e3 = E3.t[0:c, :, 0:c]
                if which == 0:
                    for h in range(4):
                        kb.stt(e1[:, h, :], LBC.t[0:c, h, cs], COL.t[0:c, h:h + 1], negup.t[0:c, 0:c], ALU.subtract, ALU.add, [(LBC, h), COL, negup], [(E1, h)])
                    kb.act(e1, e1, AF.Exp, [E1], [E1])
                elif which == 1:
                    for h in range(4):
                        kb.stt(e2[:, h, :], GBC.t[0:c, h, cs], COL.t[0:c, 8 + h:9 + h], poslo.t[0:c, 0:c], ALU.subtract, ALU.add, [(GBC, h), COL, poslo], [(E2, h)])
                    kb.act(e2, e2, AF.Exp, [E2], [E2], scale=-1.0)
                else:
                    for h in range(4):
                        kb.stt(e3[:, h, :], GBC.t[0:c, h, cs], COL.t[0:c, h:h + 1], negcz.t[0:c, 0:c], ALU.subtract, ALU.add, [(GBC, h), COL, negcz], [(E3, h)])
                    kb.act(e3, e3, AF.Exp, [E3], [E3])

            for ci in range(nch):
                cs = slice(ci * c, (ci + 1) * c)
                last = (ci + 1) * c - 1
                Sc = S
                if ci == 0:
                    pre_cols(ci)
                pk = kb.ps("sm")
                pv_ = kb.ps("sm")
                for h in range(4):
                    kb.tr(pk.t[0:c, h * 128:(h + 1) * 128], CV.t[:, 4 + h, cs], ident.t[:], [(CV, 4 + h), ident], [pk], inc=(h == 3))
                for h in range(4):
                    kb.tr(pv_.t[0:c, h * 128:(h + 1) * 128], CV.t[:, 8 + h, cs], ident.t[:], [(CV, 8 + h), ident], [pv_], inc=(h == 3))
                pk3 = pk.t[0:c, :].rearrange("p (h d) -> p h d", h=4)
                pv3 = pv_.t[0:c, :].rearrange("p (h d) -> p h d", h=4)
                pkk = kb.ps("sm")
                pqk = kb.ps("sm")
                pkk3 = pkk.t[0:c, 0:4 * c].rearrange("p (h n) -> p h n", h=4)
                pqk3 = pqk.t[0:c, 0:4 * c].rearrange("p (h n) -> p h n", h=4)
                for h in range(4):
                    kb.mm(pkk3[:, h, :], CV.t[:, 4 + h, cs], CV.t[:, 4 + h, cs], True, True, [(CV, 4 + h)], [pkk])
                for h in range(4):
                    kb.mm(pqk3[:, h, :], CV.t[:, 4 + h, cs], CV.t[:, h, cs], True, True, [(CV, 4 + h), (CV, h)], [pqk])
                e1 = E1.t[0:c, :, 0:c]
                e2 = E2.t[0:c, :, 0:c]
                e3 = E3.t[0:c, :, 0:c]
                if ci == 0:
                    for w_ in range(3):
                        pre_E(ci, w_)
                bcol = lambda a: COL.t[0:c, a:a + 4].unsqueeze(2).to_broadcast([c, 4, 128])
                kb.tt("dve", R32(KBE.t[0:c, :, :]), pk3, bcol(16), ALU.mult, [pk, COL], [KBE])
                kb.tt("dve", R32(KD.t[0:c, :, :]), pk3, bcol(20), ALU.mult, [pk, COL], [KD])
                kb.tt("dve", R32(VB.t[0:c, :, :]), pv3, bcol(4), ALU.mult, [pv_, COL], [VB])
                kb.act(EG.t[:, :, 0:c], GBC.t[:, :, cs], AF.Exp, [GBC], [EG])
                kb.tt("dve", R32(QD.t[:, :, 0:c]), CV.t[:, 0:4, cs], EG.t[:, :, 0:c], ALU.mult, [(CV, 0), (CV, 1), (CV, 2), (CV, 3), EG], [QD])
                T0, T1 = PTA
                pa = lambda tl: tl.t[0:c, :, 0:c]
                PRv = PRt.t[0:c, :, 0:2 * c]
                Pv = PRt.t[0:c, :, 0:c]
                rr = PRt.t[0:c, :, c:2 * c]
                KP, KR = (PRt, "P"), (PRt, "R")
                kb.stt(R32(Pv), pkk3, -1.0, e1, ALU.mult, ALU.mult, [pkk, E1], [KP])
                kb.stt(R32(pa(T0)), pkk3, -1.0, e2, ALU.mult, ALU.mult, [pkk, E2], [T0])
                kb.tt("dve", R32(QKD.t[0:c, :, 0:c]), pqk3, e3, ALU.mult, [pqk, E3], [QKD])
                for h in range(4):
                    kb.copy("pool", R32(rr[:, h, :]), ident.t[0:c, 0:c], [ident], [KR])
                Tc, Tn = T0, T1
                for step in range(1, nsteps + 1):
                    lastst = step == nsteps
                    pas = [kb.ps("sm"), kb.ps("sm")]
                    pa3 = [p_.t[0:c, 0:4 * c].rearrange("p (h n) -> p h n", h=2) for p_ in pas]
                    for h in range(4):
                        kb.mm(pa3[h // 2][:, h % 2, :], R32(pa(Tc)[:, h, :]), R32(PRv[:, h, :]), True, True, [Tc, KP, KR], [pas[h // 2]])
                    pq = kb.ps("sm")
                    pq3 = pq.t[0:c, 0:4 * c].rearrange("p (h n) -> p h n", h=4)
                    for h in range(4):
                        kb.mm(pq3[:, h, :], R32(Pv[:, h, :]), R32(pa(Tc)[:, h, :]), True, True, [Tc, KP], [pq])
                    for hf in range(2):
                        kb.tt("dve", R32(rr[:, 2 * hf:2 * hf + 2, :]), rr[:, 2 * hf:2 * hf + 2, :], pa3[hf][:, :, c:2 * c], ALU.add, [KR, pas[hf]], [KR])
                        if not lastst:
                            kb.copy("act", R32(Pv[:, 2 * hf:2 * hf + 2, :]), pa3[hf][:, :, 0:c], [pas[hf]], [KP])
                    kb.copy("act" if lastst else "dve", R32(pa(Tn)), pq3, [pq], [Tn])
                    Tc, Tn = Tn, Tc
                pr = kb.ps("sm")
                pr3 = pr.t[0:c, 0:4 * c].rearrange("p (h n) -> p h n", h=4)
                for h in range(4):
                    kb.mm(pr3[:, h, :], R32(pa(Tc)[:, h, :]), R32(rr[:, h, :]), True, True, [Tc, KR], [pr])
                kb.tt("dve", R32(rr), rr, pr3, ALU.add, [KR, pr], [KR])
                RR = KR
                pw = kb.ps("sm")
                pw3 = pw.t[:, 0:4 * c].rearrange("p (h n) -> p h n", h=4)
                for h in range(4):
                    kb.mm(pw3[:, h, :], R32(KBE.t[0:c, h, :]), R32(rr[:, h, :]), True, True, [KBE, RR], [pw])
                kb.ts("dve", R32(WTN.t[:, :, 0:c]), pw3, -1.0, None, ALU.mult, None, [pw], [WTN])
                nxt = kind == "p" and ci + 1 < nch
                if nxt:
                    pre_cols(ci + 1)
                if kind == "p":
                    pn = kb.ps("sm")
                    pn3 = pn.t[0:c, :].rearrange("p (h d) -> p h d", h=4)
                    for h in range(4):
                        kb.mm(pn3[:, h, :], R32(rr[:, h, :]), R32(VB.t[0:c, h, :]), True, False, [RR, VB], [pn])
                        kb.mm(pn3[:, h, :], R32(WTN.t[:, h, 0:c]), R32(Sc.t[:, h, :]), False, True, [WTN, Sc], [pn])
                    kb.copy("act", R32(VNEW.t[0:c, :, :]), pn3, [pn], [VNEW])
                    if nxt:
                        pre_E(ci + 1, 0)
                    po = kb.ps("sm")
                    po3 = po.t[:, 0:4 * c].rearrange("p (h n) -> p h n", h=4)
                    for h in range(4):
                        kb.mm(po3[:, h, :], R32(Sc.t[:, h, :]), R32(QD.t[:, h, 0:c]), True, False, [Sc, QD], [po])
                        kb.mm(po3[:, h, :], R32(VNEW.t[0:c, h, :]), R32(QKD.t[0:c, h, 0:c]), False, True, [VNEW, QKD], [po])
                    pS = kb.ps("sm")
                    pS3 = pS.t[:, :].rearrange("p (h d) -> p h d", h=4)
                    for h in range(4):
                        kb.mm(pS3[:, h, :], R32(KD.t[0:c, h, :]), R32(VNEW.t[0:c, h, :]), True, True, [KD, VNEW], [pS])
                    if nxt:
                        pre_E(ci + 1, 1)
                    for h in range(4):
                        kb.stt(R32(Sc.t[:, h, :]), Sc.t[:, h, :], EG.t[:, h, c - 1:c], pS3[:, h, :], ALU.mult, ALU.add, [Sc, EG, pS], [Sc])
                    if nxt:
                        pre_E(ci + 1, 2)
                    chunk_epilogue(po, po3, cs, c)
                    checkpoint("gdn%d" % ci, bi)
                else:
                    def ldS(s_):
                        Sb = SS[s_ % 2]
                        kb.dma("pool", R32(Sb.t[:, :, :]), sssm[s_].rearrange("h k v -> k h v"), [], [Sb], Sb.name)
                        return Sb
                    pvt = kb.ps("sm")
                    pvt3 = pvt.t[:, :].rearrange("p (h n) -> p h n", h=4)
                    for h in range(4):
                        kb.mm(pvt3[:, h, :], R32(VB.t[0:c, h, :]), R32(rr[:, h, :]), h == 0, False, [VB, RR], [pvt])
                    for s_ in range(NS):
                        Sb = ldS(s_)
                        for h in range(4):
                            fin = (s_ == NS - 1 and h == 3)
                            kb.op("pe", lambda e, h=h, s_=s_, Sb=Sb, fin=fin: e.matmul(pvt3[:, h, s_ * DS:(s_ + 1) * DS], R32(Sb.t[:, h, :]), R32(WTN.t[:, h, s_ * DS:(s_ + 1) * DS]), start=False, stop=fin),
                                  [Sb, WTN], [pvt], inc=(h == 3))
                    kb.copy("act", E1.t[:, :, :], pvt3, [pvt], [E1])
                    pvn = kb.ps("sm")
                    for h in range(4):
                        kb.tr(pvn.t[:, h * 128:(h + 1) * 128], E1.t[:, h, :], ident.t[:], [E1, ident], [pvn], inc=(h == 3))
                    kb.copy("act", R32(VNEW.t[:, :, :]), pvn.t[:, :].rearrange("p (h d) -> p h d", h=4), [pvn], [VNEW])
                    po = kb.ps("sm")
                    po3 = po.t[:, :].rearrange("p (h n) -> p h n", h=4)
                    for h in range(4):
                        kb.mm(po3[:, h, :], R32(VNEW.t[0:c, h, :]), R32(QKD.t[0:c, h, 0:c]), h == 0, False, [VNEW, QKD], [po])
                    VMs = [KBE, VB]
                    for s_ in range(NS):
                        Sb = ldS(s_)
                        for h in range(4):
                            fin = (s_ == NS - 1 and h == 3)
                            kb.op("pe", lambda e, h=h, s_=s_, Sb=Sb, fin=fin: e.matmul(po3[:, h, s_ * DS:(s_ + 1) * DS], R32(Sb.t[:, h, :]), R32(QD.t[:, h, s_ * DS:(s_ + 1) * DS]), start=False, stop=fin),
                                  [Sb, QD], [po], inc=(h == 3))
                        VM = VMs[s_ % 2]
                        kb.ts("dve", R32(VM.t[:, :, :]), VNEW.t[:, :, :], UT[1].t[:, s_:s_ + 1], None, ALU.mult, None, [VNEW, UT[1]], [VM])
                        pS = kb.ps("big")
                        pS3 = pS.t[:, :].rearrange("p (h d) -> p h d", h=4)
                        for h in range(4):
                            kb.mm(pS3[:, h, :], R32(KD.t[0:c, h, :]), R32(VM.t[:, h, :]), True, True, [KD, VM], [pS])
                        for h in range(4):
                            lc = s_ * DS + DS - 1
                            kb.stt(R32(Sb.t[:, h, :]), Sb.t[:, h, :], EG.t[:, h, lc:lc + 1], pS3[:, h, :], ALU.mult, ALU.add, [Sb, EG, pS], [Sb])
                        kb.dma("sp", ssms[s_].rearrange("h k v -> k h v"), Sb.t[:, :, :], [Sb], [], Sb.name)
                    chunk_epilogue(po, po3, cs, c)
            if last_prompt:
                kb.dma("sp", ssmp.rearrange("h k v -> k h v"), S.t[:, :, :], [S], [], "S")

            checkpoint("gdn", bi)
            checkpoint("epi", bi)
            sta, va = stage_k4(w_a, 0)
            stb, vb_ = stage_k4(w_b, 0)
            MT = CV
            MR = lambda m: (OA, m) if m < 4 else (OB, m - 4)
            MRv = lambda m: (OA.t[:, m] if m < 4 else OB.t[:, m - 4])
            for m in range(8):
                pA = kb.ps("big")
                for k in range(4):
                    kb.mm(pA.t[:, 0:T], R32(va.ap(k, m * 128, 128)), R32(OA.t[:, k, 0:T]), k == 0, k == 3, [(OA, k)] + sta.keys, [pA])
                kb.copy("act", MT.t[:, m, 0:T], pA.t[:, 0:T], [pA], [(MT, m)])
                pB = kb.ps("big")
                for k in range(4):
                    kb.mm(pB.t[:, 0:T], R32(vb_.ap(k, m * 128, 128)), R32(OB.t[:, k, 0:T]), k == 0, k == 3, [(OB, k)] + stb.keys, [pB])
                bdst = (GBC if m < 4 else LBC)
                kb.copy("dve", bdst.t[:, m % 4, 0:T], pB.t[:, 0:T], [pB], [(bdst, m % 4)])
            for gsel in range(2):
                for half in range(2):
                    st, vw = stage_k8(w_in, 2568 + gsel * 1024 + half * 512, 512)
                    for mm_ in range(4):
                        m = half * 4 + mm_
                        p = proj_chunk(st, vw, mm_)
                        kb.act(SQ.t[:, 0:T], p.t[:, 0:T], AF.Sigmoid, [p], [SQ])
                        if gsel == 0:
                            kb.tt("dve", MT.t[:, m, 0:T], MT.t[:, m, 0:T], SQ.t[:, 0:T], ALU.mult, [(MT, m), SQ], [(MT, m)])
                        else:
                            bsrc = (GBC if m < 4 else LBC)
                            kb.tt("dve", SQ.t[:, 0:T], SQ.t[:, 0:T], bsrc.t[:, m % 4, 0:T], ALU.mult, [SQ, (bsrc, m % 4)], [SQ])
                            kb.tt("dve", R32(MRv(m)[:, 0:T]), MT.t[:, m, 0:T], SQ.t[:, 0:T], ALU.add, [(MT, m), SQ], [MR(m)])

            checkpoint("merge", bi)
            for half in range(2):
                st, vw = stage_k8(w_o, half * 512, 512)
                for i in range(nt):
                    p = kb.ps("big")
                    for k in range(8):
                        kb.mm(p.t[:, :], R32(MRv(k)[:, i * 128:(i + 1) * 128]), R32(vw.ap(k, 0, 512)), k == 0, k == 7, [MR(k)] + st.keys, [p])
                    kb.tt("dve", X.t[:, i, half * 512:(half + 1) * 512], X.t[:, i, half * 512:(half + 1) * 512], p.t[:, :], ALU.add, [(X, i), p], [(X, i)])

            checkpoint("wo", bi)
            norm_to_hT(gmlp_c)
            def mlp_up(fg, stu, vu):
                for f in range(4):
                    cf = (fg % 2) * 4 + f
                    p = proj_chunk(stu, vu, f)
                    kb.act(SQ.t[:, 0:T], p.t[:, 0:T], AF.Relu, [p], [SQ])
                    kb.tt("dve", R32(MRv(cf)[:, 0:T]), SQ.t[:, 0:T], SQ.t[:, 0:T], ALU.mult, [SQ], [MR(cf)])

            def mlp_down(fg, std, vd):
                for i in range(nt):
                    for half in range(2):
                        p = kb.ps("big")
                        for f in range(4):
                            cf = (fg % 2) * 4 + f
                            kb.mm(p.t[:, :], R32(MRv(cf)[:, i * 128:(i + 1) * 128]), R32(vd.ap(f, half * 512, 512)), f == 0, f == 3, [MR(cf)] + std.keys, [p])
                        kb.tt("dve", X.t[:, i, half * 512:(half + 1) * 512], X.t[:, i, half * 512:(half + 1) * 512], p.t[:, :], ALU.add, [(X, i), p], [(X, i)])

            stu, vu = stage_k8(w_up, 0, 512)
            mlp_up(0, stu, vu)
            for fg in range(8):
                if fg + 1 < 8:
                    stu, vu = stage_k8(w_up, (fg + 1) * 512, 512)
                    mlp_up(fg + 1, stu, vu)
                std, vd = stage_k4(w_down, fg * 512)
                mlp_down(fg, std, vd)

            checkpoint("mlp", bi)
            for i in range(nt):
                kb.act(XNv, X.t[:, i, :], AF.Square, [(X, i)], XNk + [(ssq, i)], accum_out=ssq.t[:, i:i + 1])
                kb.act(lntmp.t[:, i:i + 1], ssq.t[:, i:i + 1], AF.Ln, [(ssq, i), epscol], [(lntmp, i)], bias=epscol.t[:, :], scale=1.0 / D)
                kb.act(rstd.t[:, i:i + 1], lntmp.t[:, i:i + 1], AF.Exp, [(lntmp, i)], [(rstd, i)], scale=-0.5)
                yv = CV.t[:, 2 * i:2 * i + 2, :].rearrange("p a b -> p (a b)")
                yk = [(CV, 2 * i), (CV, 2 * i + 1)]
                kb.stt(yv, X.t[:, i, :], rstd.t[:, i:i + 1], gfin.t[:, :], ALU.mult, ALU.mult, [(X, i), (rstd, i), gfin], yk)
                kb.dma("sp", ydst[i * 128:(i + 1) * 128, :], yv, yk, [], "Y%d" % i)

        try:
            if not stopped:
                run_all()
        except StopBuild:
            pass
        kb.finish()
        kb.emit()
    return nc


_NC = None


def kernel(x_prompt, x_sample, state_conv, state_pool, state_ssm, w_in, w_conv, a_log, dt_bias, w_onorm,
           w_pool_mix, pool_scale, w_a_out, w_b_out, w_o, g_attn, g_mlp, w_up, w_down, g_final):
    global _NC
    f = lambda a: np.ascontiguousarray(np.asarray(a, dtype=np.float32))
    if _NC is None:
        _NC = build_program()
    nc = _NC
    ncores = 8
    shared = {
        "w_in": f(w_in[0]), "w_conv": f(w_conv[0]), "a_log": f(a_log[0]).reshape(NH, 1), "dt_bias": f(dt_bias[0]).reshape(NH, 1),
        "w_onorm": f(w_onorm[0]).reshape(128, 1), "w_mix": f(w_pool_mix[0]), "pool_scale": f(pool_scale[0]).reshape(512, 1),
        "w_a": f(w_a_out[0]), "w_b": f(w_b_out[0]), "w_o": f(w_o[0]), "g_attn": f(g_attn[0]).reshape(D, 1),
        "g_mlp": f(g_mlp[0]).reshape(D, 1), "w_up": f(w_up[0]), "w_down": f(w_down[0]), "g_final": f(g_final).reshape(1, D),
    }
    in_maps = []
    for ci in range(ncores):
        sl = slice(ci * NS, (ci + 1) * NS)
        m = dict(shared)
        m["xp"] = f(x_prompt[ci])
        m["xs"] = f(x_sample[sl]).reshape(NS * DS, D)
        m["sconv"] = f(state_conv[0, sl]).reshape(NS * 3, QKVW)
        m["spool"] = f(state_pool[0, sl]).reshape(NS * 15, 512)
        m["sssm"] = f(state_ssm[0, sl])
        in_maps.append(m)
    res = run_bass_kernel_spmd(nc, in_maps, core_ids=list(range(ncores)))
    r = res.results
    y_prompt = np.stack([r[i]["yp"] for i in range(ncores)], 0)
    y_sample = np.concatenate([r[i]["ys"].reshape(NS, DS, D) for i in range(ncores)], 0)
    conv_p = np.stack([r[i]["convp"] for i in range(ncores)], 0)[None]
    pool_p = np.stack([r[i]["poolp"] for i in range(ncores)], 0)[None]
    ssm_p = np.stack([r[i]["ssmp"] for i in range(ncores)], 0)[None]
    conv_s = np.concatenate([r[i]["convs"].reshape(NS, 3, QKVW) for i in range(ncores)], 0)[None]
    pool_s = np.concatenate([r[i]["pools"].reshape(NS, 15, 512) for i in range(ncores)], 0)[None]
    ssm_s = np.concatenate([r[i]["ssms"] for i in range(ncores)], 0)[None]
    return (y_prompt.astype(np.float32), y_sample.astype(np.float32), conv_p.astype(np.float32), pool_p.astype(np.float32),
            ssm_p.astype(np.float32), conv_s.astype(np.float32), pool_s.astype(np.float32), ssm_s.astype(np.float32))
```
